# Optimizing a Trainium2 kernel written in Bass

```python
import math
import jax, jax.numpy as jnp
from jax import lax
import numpy as np

D_MODEL = 1024
BATCH = 8
SEQ = 4096
DEPTH = 4

HEAD_DIM = 64
N_HEADS = D_MODEL // HEAD_DIM
N_MIXERS = 3
DILATED_GROUPS = ((128, 1), (512, 4), (2048, 16))
N_DIL = len(DILATED_GROUPS)
GRID_W = 64
NA_ROWS = 8
NA_COLS = 16
SWA_RADIUS = 128
N_KV_HEADS = 4
GQA_GROUP = N_HEADS // N_KV_HEADS
D_FF = 256 * math.ceil(8 * D_MODEL / (3 * 256))
T5_BUCKETS = 32
T5_MAX_DISTANCE = 1024
N_A = (DEPTH + 2) // 3
N_B = (DEPTH + 1) // 3
N_C = DEPTH // 3
EPS = 1e-6
NEG_INF = -1e30

kernel_name = "hybrid_dilated_neighbourhood_swa_encoder"


def rmsnorm(x, g):
    xf = x.astype(jnp.float32)
    y = xf * lax.rsqrt(jnp.mean(xf * xf, axis=-1, keepdims=True) + EPS)
    return (y * g.astype(jnp.float32)).astype(x.dtype)


def t5_bucket(rel):
    half = T5_BUCKETS // 2
    max_exact = half // 2
    ret = jnp.where(rel > 0, half, 0)
    n = jnp.abs(rel)
    nf = jnp.maximum(n, 1).astype(jnp.float32)
    large = max_exact + (jnp.log(nf / max_exact) / math.log(T5_MAX_DISTANCE / max_exact)
                         * (half - max_exact)).astype(jnp.int32)
    large = jnp.minimum(large, half - 1)
    return ret + jnp.where(n < max_exact, n, large)


def t5_bias(table, rel):
    return jnp.moveaxis(table[t5_bucket(rel)], -1, 0)


def banded_attention(q, k, v, bias, sink):
    Q = bias.shape[-2]
    L = q.shape[0]
    nb = -(-L // Q)
    Lp = nb * Q
    pad = Lp - L
    scale = 1.0 / math.sqrt(q.shape[-1])
    qb = jnp.pad(q, ((0, pad),) + ((0, 0),) * (q.ndim - 1)).reshape(nb, Q, *q.shape[1:])

    def band(t):
        tp = jnp.pad(t, ((Q, pad + Q), (0, 0), (0, 0))).reshape(nb + 2, Q, *t.shape[1:])
        return jnp.concatenate([tp[:-2], tp[1:-1], tp[2:]], axis=1)

    kb, vb = band(k), band(v)
    kp = jnp.arange(-Q, Lp + Q).reshape(nb + 2, Q)
    key_pos = jnp.concatenate([kp[:-2], kp[1:-1], kp[2:]], axis=1)
    rel = jnp.arange(3 * Q)[None, :] - Q - jnp.arange(Q)[:, None]
    mask = (jnp.abs(rel) <= Q)[None] & ((key_pos >= 0) & (key_pos < L))[:, None, :]
    s = jnp.einsum('nqhgd,nkhd->nhgqk', qb, kb).astype(jnp.float32) * scale \
        + bias[None].astype(jnp.float32)
    s = jnp.where(mask[:, None, None], s, NEG_INF)
    lse = jax.nn.logsumexp(s, axis=-1)
    if sink is not None:
        lse = jnp.logaddexp(lse, sink.astype(jnp.float32)[None, :, :, None])
    p = jnp.exp(s - lse[..., None]).astype(v.dtype)
    o = jnp.einsum('nhgqk,nkhd->nqhgd', p, vb).reshape(Lp, *q.shape[1:])[:L]
    lse = lse.transpose(0, 3, 1, 2).reshape(Lp, q.shape[1], q.shape[2])[:L]
    return o, lse


def dilated_mixer(h, w_in, w_out, rel_bias):
    B_, S_, _ = h.shape
    qkv = (h @ w_in).reshape(B_, S_, N_DIL, 3, N_HEADS, HEAD_DIM)
    outs, lses = [], []
    for g, (window, d) in enumerate(DILATED_GROUPS):
        radius = window // (2 * d)
        L = S_ // d

        def by_residue(t):
            return t.reshape(B_, L, d, N_HEADS, HEAD_DIM).transpose(0, 2, 1, 3, 4) \
                    .reshape(B_ * d, L, N_HEADS, HEAD_DIM)

        q = by_residue(qkv[:, :, g, 0])[:, :, :, None]
        k = by_residue(qkv[:, :, g, 1])
        v = by_residue(qkv[:, :, g, 2])
        rel = (jnp.arange(3 * radius)[None, :] - radius - jnp.arange(radius)[:, None]) * d
        bias = t5_bias(rel_bias, rel)[:, None]
        o, lse = lax.map(lambda t: banded_attention(t[0], t[1], t[2], bias, None), (q, k, v))
        outs.append(o.reshape(B_, d, L, N_HEADS, HEAD_DIM).transpose(0, 2, 1, 3, 4)
                    .reshape(B_, S_, N_HEADS, HEAD_DIM))
        lses.append(lse.reshape(B_, d, L, N_HEADS).transpose(0, 2, 1, 3).reshape(B_, S_, N_HEADS))
    w = jax.nn.softmax(jnp.stack(lses), axis=0)
    o = jnp.einsum('gbsh,gbshd->bshd', w.astype(h.dtype), jnp.stack(outs))
    return o.reshape(B_, S_, D_MODEL) @ w_out


def neighbourhood_mixer(h, w_in, w_out, rpb):
    B_, S_, _ = h.shape
    rows = S_ // GRID_W
    kr = min(NA_ROWS, rows)
    kc = NA_COLS
    scale = 1.0 / math.sqrt(HEAD_DIM)
    qkv = (h @ w_in).reshape(B_, rows, GRID_W, 3, N_HEADS, HEAD_DIM)
    qg, kg, vg = qkv[:, :, :, 0], qkv[:, :, :, 1], qkv[:, :, :, 2]
    cols = jnp.arange(GRID_W)
    col_idx = jnp.clip(cols - kc // 2, 0, GRID_W - kc)[:, None] + jnp.arange(kc)[None, :]
    col_off = col_idx - cols[:, None] + (NA_COLS - 1)

    def row_fn(r):
        rs = jnp.clip(r - kr // 2, 0, rows - kr)
        ks = lax.dynamic_slice_in_dim(kg, rs, kr, axis=1)[:, :, col_idx]
        vs = lax.dynamic_slice_in_dim(vg, rs, kr, axis=1)[:, :, col_idx]
        qr = lax.dynamic_index_in_dim(qg, r, axis=1, keepdims=False)
        s = jnp.einsum('bqhd,brqkhd->bhqrk', qr, ks).astype(jnp.float32) * scale
        row_off = rs + jnp.arange(kr) - r + (NA_ROWS - 1)
        bias = rpb[:, row_off][:, :, col_off].transpose(0, 2, 1, 3)
        s = s + bias[None].astype(jnp.float32)
        p = jax.nn.softmax(s.reshape(B_, N_HEADS, GRID_W, kr * kc), axis=-1)
        p = p.reshape(B_, N_HEADS, GRID_W, kr, kc).astype(h.dtype)
        return jnp.einsum('bhqrk,brqkhd->bqhd', p, vs)

    o = lax.map(row_fn, jnp.arange(rows))
    o = o.transpose(1, 0, 2, 3, 4).reshape(B_, S_, D_MODEL)
    return o @ w_out


def window_gqa_mixer(h, w_in, w_out, sink, rel_bias):
    B_, S_, _ = h.shape
    qkv = h @ w_in
    nq, nk = N_HEADS * HEAD_DIM, N_KV_HEADS * HEAD_DIM
    q = qkv[..., :nq].reshape(B_, S_, N_KV_HEADS, GQA_GROUP, HEAD_DIM)
    k = qkv[..., nq:nq + nk].reshape(B_, S_, N_KV_HEADS, HEAD_DIM)
    v = qkv[..., nq + nk:].reshape(B_, S_, N_KV_HEADS, HEAD_DIM)
    R = SWA_RADIUS
    rel = jnp.arange(3 * R)[None, :] - R - jnp.arange(R)[:, None]
    bias = t5_bias(rel_bias, rel).reshape(N_KV_HEADS, GQA_GROUP, R, 3 * R)
    sk = sink.reshape(N_KV_HEADS, GQA_GROUP)
    o, _ = lax.map(lambda t: banded_attention(t[0], t[1], t[2], bias, sk), (q, k, v))
    return o.reshape(B_, S_, D_MODEL) @ w_out


def swiglu(h, w_in, w_out):
    gu = h @ w_in
    gate, up = gu[..., :D_FF], gu[..., D_FF:]
    return (jax.nn.silu(gate) * up) @ w_out


def setup_inputs(seed: int = 0) -> dict:
    key = jax.random.key(seed)
    ks = jax.random.split(key, 20)
    D = D_MODEL
    nrm = jax.random.normal
    f32 = jnp.float32
    return {
        "x": nrm(ks[0], (BATCH, SEQ, D), f32),
        "c": nrm(ks[1], (BATCH, D), f32),
        "rel_bias": 0.5 * nrm(ks[2], (T5_BUCKETS, N_HEADS), f32),
        "ada_w": 0.5 * D ** -0.5 * nrm(ks[3], (DEPTH, D, 6 * D), f32),
        "ada_b": 0.01 * nrm(ks[4], (DEPTH, 6 * D), f32),
        "norm_mix": 1.0 + 0.01 * nrm(ks[5], (DEPTH, D), f32),
        "norm_ffn": 1.0 + 0.01 * nrm(ks[6], (DEPTH, D), f32),
        "norm_final": 1.0 + 0.01 * nrm(ks[7], (D,), f32),
        "a_w_in": D ** -0.5 * nrm(ks[8], (N_A, D, N_DIL * 3 * N_HEADS * HEAD_DIM), f32),
        "a_w_out": D ** -0.5 * nrm(ks[9], (N_A, D, D), f32),
        "b_w_in": D ** -0.5 * nrm(ks[10], (N_B, D, 3 * D), f32),
        "b_w_out": D ** -0.5 * nrm(ks[11], (N_B, D, D), f32),
        "b_rpb": 0.5 * nrm(ks[12], (N_B, N_HEADS, 2 * NA_ROWS - 1, 2 * NA_COLS - 1), f32),
        "c_w_in": D ** -0.5 * nrm(ks[13], (N_C, D, (N_HEADS + 2 * N_KV_HEADS) * HEAD_DIM), f32),
        "c_w_out": D ** -0.5 * nrm(ks[14], (N_C, D, D), f32),
        "c_sink": 0.5 * nrm(ks[15], (N_C, N_HEADS), f32),
        "ffn_w_in": D ** -0.5 * nrm(ks[16], (DEPTH, D, 2 * D_FF), f32),
        "ffn_w_out": D_FF ** -0.5 * nrm(ks[17], (DEPTH, D_FF, D), f32),
    }


def reference(x, c, rel_bias, ada_w, ada_b, norm_mix, norm_ffn, norm_final,
              a_w_in, a_w_out, b_w_in, b_w_out, b_rpb,
              c_w_in, c_w_out, c_sink, ffn_w_in, ffn_w_out):
    cond = jax.nn.silu(c)
    for i in range(DEPTH):
        mod = (cond @ ada_w[i] + ada_b[i])[:, None, :]
        shift1, scale1, gate1, shift2, scale2, gate2 = jnp.split(mod, 6, axis=-1)
        h = rmsnorm(x, norm_mix[i]) * (1 + scale1) + shift1
        kind, j = i % N_MIXERS, i // N_MIXERS
        if kind == 0:
            m = dilated_mixer(h, a_w_in[j], a_w_out[j], rel_bias)
        elif kind == 1:
            m = neighbourhood_mixer(h, b_w_in[j], b_w_out[j], b_rpb[j])
        else:
            m = window_gqa_mixer(h, c_w_in[j], c_w_out[j], c_sink[j], rel_bias)
        x = x + gate1 * m
        h = rmsnorm(x, norm_ffn[i]) * (1 + scale2) + shift2
        x = x + gate2 * swiglu(h, ffn_w_in[i], ffn_w_out[i])
    return rmsnorm(x, norm_final)
```

```python
import math
import contextlib
import numpy as np
import concourse.bass as bass
import concourse.mybir as mybir
from concourse.bass_utils import run_bass_kernel_spmd

F32 = mybir.dt.float32
BF16 = mybir.dt.bfloat16
ALU = mybir.AluOpType
AF = mybir.ActivationFunctionType

D = 1024
S = 4096
DEPTH = 4
DFF = 2816
NFC = DFF // 128
EPS = 1e-6
NEG = -30000.0
ENGS = ("pe", "act", "dve", "pool", "sp")


class _Op:
    __slots__ = ("eng", "fn", "waits", "sig_key", "sig_val", "signaled", "idx", "epoch", "is_dma")


class Prog:
    def __init__(self, nc):
        self.nc = nc
        self.ops = {e: [] for e in ENGS}
        self.last_w = {}
        self.readers = {}
        self.epoch = 0
        self.dma_cnt = {}
        self.dma_last = {}
        self.waited = {}

    def new_epoch(self):
        self.epoch += 1

    def _stream(self, op):
        return ("dma", op.sig_key) if op.is_dma else ("eng", op.eng)

    def _pos(self, d):
        return d.sig_val if d.is_dma else (d.epoch, d.idx)

    def _finish(self, eng, op, deps):
        waits = {}
        for d, raw in deps:
            if (not d.is_dma) and d.eng == eng:
                if eng in ("pe", "sp") or not raw:
                    continue
            stt = self._stream(d)
            pos = self._pos(d)
            prev = waits.get(stt)
            if prev is None or pos > prev[0]:
                waits[stt] = (pos, d)
        final = []
        for stt, (pos, d) in waits.items():
            k = (eng, stt)
            have = self.waited.get(k)
            if have is not None and have >= pos:
                continue
            self.waited[k] = pos
            d.signaled = True
            final.append(d)
        op.waits = final
        self.ops[eng].append(op)

    def _add(self, eng, fn, reads, writes, is_dma=False, semkey=None, n_dma=1):
        op = _Op()
        op.eng = eng
        op.fn = fn
        op.is_dma = is_dma
        op.signaled = is_dma
        op.epoch = self.epoch
        op.idx = len(self.ops[eng])
        op.sig_key = None
        op.sig_val = None
        if is_dma:
            c = self.dma_cnt.get(semkey, 0) + 16 * n_dma
            self.dma_cnt[semkey] = c
            op.sig_key = semkey
            op.sig_val = c
            self.dma_last[semkey] = op
        deps = []
        for r in reads:
            w = self.last_w.get(r)
            if w is not None:
                deps.append((w, True))
        for w_ in writes:
            w = self.last_w.get(w_)
            if w is not None:
                deps.append((w, False))
            for rd in self.readers.get(w_, {}).values():
                deps.append((rd, False))
        self._finish(eng, op, deps)
        for r in reads:
            self.readers.setdefault(r, {})[self._stream(op)] = op
        for w_ in writes:
            self.last_w[w_] = op
            self.readers[w_] = {}
        return op

    def op(self, eng, fn, reads=(), writes=()):
        return self._add(eng, fn, tuple(reads), tuple(writes))

    def dma(self, q, fn, reads=(), writes=(), semkey=None, n=1):
        return self._add(q, fn, tuple(reads), tuple(writes), is_dma=True, semkey=semkey, n_dma=n)

    def barrier(self):
        lasts = [self.ops[e][-1] for e in ENGS if self.ops[e] and not self.ops[e][-1].is_dma]
        lasts = []
        for e in ENGS:
            for o in reversed(self.ops[e]):
                if not o.is_dma and o.fn is not None:
                    lasts.append(o)
                    break
        lasts += list(self.dma_last.values())
        for e in ENGS:
            op = _Op()
            op.eng = e
            op.fn = None
            op.is_dma = False
            op.signaled = False
            op.epoch = self.epoch
            op.idx = len(self.ops[e])
            op.sig_key = None
            op.sig_val = None
            self._finish(e, op, [(d, True) for d in lasts if d.is_dma or d.eng != e])

    def emit(self, final_waits=()):
        nc = self.nc
        with contextlib.ExitStack() as st:
            esem = {}
            for e in ENGS:
                used = sorted({o.epoch for o in self.ops[e] if o.signaled and not o.is_dma})
                for ep in used:
                    esem[(e, ep)] = st.enter_context(nc.semaphore(f"s_{e}_{ep}"))
            dsem = {}
            for k in self.dma_cnt:
                dsem[k] = st.enter_context(nc.semaphore(f"d_{len(dsem)}"))
            for e in ENGS:
                cnt = {}
                for o in self.ops[e]:
                    if o.is_dma or not o.signaled:
                        continue
                    c = cnt.get(o.epoch, 0) + 1
                    cnt[o.epoch] = c
                    o.sig_key = (e, o.epoch)
                    o.sig_val = c
            block = st.enter_context(nc.Block())

            def run(e, engine):
                for o in self.ops[e]:
                    for d in o.waits:
                        if d.is_dma:
                            engine.wait_ge(dsem[d.sig_key], d.sig_val)
                        else:
                            engine.wait_ge(esem[d.sig_key], d.sig_val)
                    if o.fn is None:
                        continue
                    if o.is_dma:
                        o.fn(engine, dsem[o.sig_key])
                    else:
                        ins = o.fn(engine)
                        if o.signaled:
                            ins.then_inc(esem[o.sig_key], 1)
                if e == "sp":
                    for d in final_waits:
                        engine.wait_ge(dsem[d.sig_key], d.sig_val)

            @block.tensor
            def _(eng):
                run("pe", eng)

            @block.scalar
            def _(eng):
                run("act", eng)

            @block.vector
            def _(eng):
                run("dve", eng)

            @block.gpsimd
            def _(eng):
                run("pool", eng)

            @block.sync
            def _(eng):
                run("sp", eng)


def sl(start, count, step=1):
    return slice(start, start + step * (count - 1) + 1, step)


def _t5_bucket(rel):
    half, max_exact = 16, 8
    ret = np.where(rel > 0, half, 0)
    n = np.abs(rel)
    nf = np.maximum(n, 1).astype(np.float32)
    large = max_exact + (np.log(nf / np.float32(max_exact)) / np.float32(math.log(1024 / max_exact))
                         * np.float32(half - max_exact)).astype(np.int32)
    large = np.minimum(large, half - 1)
    return ret + np.where(n < max_exact, n, large)


def _onehot_band(length, pad, center, radius, dil):
    oh = np.zeros((33, pad), np.float32)
    i = np.arange(pad)
    rel = center - i
    valid = (np.abs(rel) <= radius) & (i < length)
    b = _t5_bucket(rel * dil)
    for k in range(pad):
        if valid[k]:
            oh[b[k], k] = 1.0
        else:
            oh[32, k] = 1.0
    return oh


B_CLASSES = [
    ("int", 2, [4, 3, 2, 1, 0], 128),
    ("e0", 0, [3, 2, 1, 0], 0),
    ("e1", 1, [3, 2, 1, 0], 128),
    ("e30", 30, [31, 30, 29, 28], 256),
    ("e31", 31, [31, 30, 29, 28], 384),
]
B_OFF = {}
_o = 0
for _n, _j, _U, _x in B_CLASSES:
    B_OFF[_n] = (_o, len(_U), _x)
    _o += 128 * len(_U)
B_EBW = _o


def _b_class(j):
    return {0: "e0", 1: "e1", 30: "e30", 31: "e31"}.get(j, "int")


def _b_tiles(j):
    if j <= 1:
        return [3, 2, 1, 0]
    if j >= 30:
        return [31, 30, 29, 28]
    return [j + 2, j + 1, j, j - 1, j - 2]


def _valid_b():
    out = np.zeros((128, B_EBW), np.float32)
    kk = np.arange(128)[:, None]
    qq = np.arange(128)[None, :]
    for name, j, U, _x in B_CLASSES:
        off = B_OFF[name][0]
        for i, u in enumerate(U):
            kt = 128 * u + kk
            qt = 128 * j + qq
            kr, kc = kt // 64, kt % 64
            r, c = qt // 64, qt % 64
            rs = np.clip(r - 4, 0, 56)
            cs = np.clip(c - 8, 0, 48)
            v = (kr >= rs) & (kr < rs + 8) & (kc >= cs) & (kc < cs + 16)
            out[:, off + 128 * i: off + 128 * (i + 1)] = v.astype(np.float32)
    return out


def _static_tables():
    ohA = np.stack([_onehot_band(383, 384, 191, 64, d) for d in (1, 4, 16)])
    ohC = _onehot_band(511, 512, 255, 128, 1)
    return ohA, ohC, _valid_b()


MIX = ["A", "B", "C", "A"]
MIXJ = [0, 0, 0, 1]
A_GROUPS = [(1, 4096), (4, 1024), (16, 256)]


def build_program(nlayers=DEPTH):
    nc = bass.Bass("TRN2", target_bir_lowering=False)

    def din(name, shape, dt=F32):
        return nc.dram_tensor(name, list(shape), dt, kind="ExternalInput").ap()

    xT_in = din("xT", [D, S])
    cb_in = din("cb", [128, 8])
    relb = din("rel_bias", [32, 16])
    ada_w = din("ada_w", [DEPTH, D, 6 * D])
    ada_b = din("ada_bl", [DEPTH, 128, 48])
    gmix = din("gmix", [128, DEPTH * 8])
    gffn = din("gffn", [128, DEPTH * 8])
    gfin = din("gfin", [128, 8])
    a_w_in = din("a_w_in", [2, D, 9216])
    a_w_out = din("a_w_out", [2, D, D])
    b_w_in = din("b_w_in", [1, D, 3072])
    b_w_out = din("b_w_out", [1, D, D])
    rpbff = din("rpbff", [16, 15 * 31])
    c_w_in = din("c_w_in", [1, D, 1536])
    c_w_out = din("c_w_out", [1, D, D])
    c_sink = din("c_sink", [1, 16])
    ffn_w_in = din("ffn_w_in", [DEPTH, D, 2 * DFF])
    ffn_w_out = din("ffn_w_out", [DEPTH, DFF, D])
    ohA_d = din("ohA", [3, 33, 384])
    ohC_d = din("ohC", [33, 512])
    validB_d = din("validB", [128, B_EBW])
    ident_d = din("ident", [128, 128])
    yT = nc.dram_tensor("yT", [D, S], F32, kind="ExternalOutput").ap()

    xs = nc.dram_tensor("xs", [D, S], F32).ap()
    OTd = nc.dram_tensor("OTd", [D, S], BF16).ap()
    vecA = nc.dram_tensor("vecA", [3, 16, 384], F32).ap()
    vecC = nc.dram_tensor("vecC", [16, 512], F32).ap()
    vecB = nc.dram_tensor("vecB", [16, 1024], F32).ap()
    repA = nc.dram_tensor("repA", [3, 16, 128, 384], F32).ap()
    repC = nc.dram_tensor("repC", [16, 128, 512], F32).ap()
    repB = nc.dram_tensor("repB", [16, 128, 1024], F32).ap()

    P = Prog(nc)
    ES = contextlib.ExitStack()
    with ES:
        def sb(name, shape, dt):
            return ES.enter_context(nc.sbuf_tensor("t_" + name, list(shape), dt))

        psb = [ES.enter_context(nc.psum_tensor(f"psb{i}", [128, 512], F32)) for i in range(8)]

        def PSR(i):
            return ("ps", i)

        ones_bf = sb("ones_bf", [128, 128], BF16)
        cb = sb("cb", [128, 8], F32)
        condb = sb("condb", [128, 8], BF16)
        modsb = sb("modsb", [128, DEPTH * 48], F32)
        adab = sb("adab", [128, DEPTH * 48], F32)
        gmix_sb = sb("gmix_sb", [128, DEPTH * 8], F32)
        gffn_sb = sb("gffn_sb", [128, DEPTH * 8], F32)
        gfin_sb = sb("gfin_sb", [128, 8], F32)
        a1 = sb("a1", [128, DEPTH * 8], F32)
        a2 = sb("a2", [128, DEPTH * 8], F32)
        es_sink = sb("es_sink", [128, 16], F32)
        es_zero = sb("es_zero", [128, 16], F32)
        dummy = sb("dummy", [128, 8], F32)
        ident = sb("ident", [128, 128], BF16)

        P.op("dve", lambda e: e.memset(ones_bf[:], 1.0), writes=["ones"])
        P.op("dve", lambda e: e.memset(es_zero[:], 0.0), writes=["es_zero"])

        def simple_load(q, dst, src, res):
            P.dma(q, lambda e, s: e.dma_start(out=dst, in_=src).then_inc(s, 16), writes=[res], semkey=res)

        simple_load("sp", cb[:], cb_in, "cb")
        simple_load("pool", ident[:], ident_d, "ident")
        simple_load("sp", adab[:].rearrange("p (l j) -> p l j", l=DEPTH), ada_b.rearrange("l p j -> p l j"), "adab")
        simple_load("sp", gmix_sb[:], gmix, "gmix")
        simple_load("sp", gffn_sb[:], gffn, "gffn")
        simple_load("sp", gfin_sb[:], gfin, "gfin")
        P.dma("sp", lambda e, s: e.dma_start(out=es_sink[:], in_=bass.AP(c_sink.tensor, 0, [[0, 128], [1, 16]])).then_inc(s, 16),
              writes=["es_sink"], semkey="es_sink")
        P.op("act", lambda e: e.activation(out=condb[:], in_=cb[:], func=AF.Silu), reads=["cb"], writes=["condb"])
        P.op("act", lambda e: e.activation(out=es_sink[:], in_=es_sink[:], func=AF.Exp), reads=["es_sink"], writes=["es_sink"])

        with contextlib.ExitStack() as es1:
            adw = [es1.enter_context(nc.sbuf_tensor(f"adw{i}", [128, 8 * 1024], BF16)) for i in range(2)]
            it = 0
            for l in range(nlayers):
                for piece in range(6):
                    slot = it % 2
                    it += 1
                    buf = adw[slot]
                    src = ada_w[l, :, piece * 1024:(piece + 1) * 1024].rearrange("(kc p) n -> p kc n", p=128)
                    dst = buf[:].rearrange("p (kc n) -> p kc n", kc=8)
                    P.dma("pool", (lambda e, s, dst=dst, src=src: e.dma_start(out=dst, in_=src).then_inc(s, 16)),
                          writes=[("adw", slot)], semkey=("adw", slot))
                    bank = (l * 6 + piece) % 2

                    def mm_mod(e, buf=buf, piece=piece, bank=bank):
                        ins = None
                        for fc in range(8):
                            for kc in range(8):
                                ins = e.matmul(psb[bank][:, fc:fc + 1],
                                               buf[:, kc * 1024 + fc * 128: kc * 1024 + (fc + 1) * 128],
                                               condb[:, kc:kc + 1], start=(kc == 0), stop=(kc == 7))
                        return ins
                    P.op("pe", mm_mod, reads=[("adw", slot), "condb"], writes=[PSR(bank)])
                    col = l * 48 + piece * 8
                    P.op("dve", (lambda e, bank=bank, col=col: e.tensor_tensor(
                        out=modsb[:, col:col + 8], in0=psb[bank][:, 0:8], in1=adab[:, col:col + 8], op=ALU.add)),
                        reads=[PSR(bank), "adab"], writes=["modsb"])
            for l in range(nlayers):
                P.op("dve", (lambda e, l=l: e.scalar_tensor_tensor(
                    out=a1[:, l * 8:(l + 1) * 8], in0=modsb[:, l * 48 + 8: l * 48 + 16], scalar=1.0,
                    in1=gmix_sb[:, l * 8:(l + 1) * 8], op0=ALU.add, op1=ALU.mult)),
                    reads=["modsb", "gmix"], writes=["a1"])
                P.op("dve", (lambda e, l=l: e.scalar_tensor_tensor(
                    out=a2[:, l * 8:(l + 1) * 8], in0=modsb[:, l * 48 + 32: l * 48 + 40], scalar=1.0,
                    in1=gffn_sb[:, l * 8:(l + 1) * 8], op0=ALU.add, op1=ALU.mult)),
                    reads=["modsb", "gffn"], writes=["a2"])
            P.barrier()

        kinds_used = set(MIX[:nlayers])
        with contextlib.ExitStack() as es2:
            tab33 = es2.enter_context(nc.sbuf_tensor("tab33", [33, 16], F32))
            oh = es2.enter_context(nc.sbuf_tensor("oh", [33, 4 * 512], F32))
            vsb = es2.enter_context(nc.sbuf_tensor("vsb", [16, 4 * 512], F32))
            rpb_sb = es2.enter_context(nc.sbuf_tensor("rpb_sb", [16, 465], F32))
            zB = es2.enter_context(nc.sbuf_tensor("zB", [16, 1024], F32))
            P.op("dve", lambda e: e.memset(tab33[32:33, :], NEG), writes=["tab33b"])
            simple_load("sp", tab33[0:32, :], relb, "tab33a")
            for g in range(3):
                simple_load("sp", oh[:, g * 512: g * 512 + 384], ohA_d[g], ("oh", g))
            simple_load("sp", oh[:, 3 * 512: 4 * 512], ohC_d, ("oh", 3))
            for g in range(4):
                W = 384 if g < 3 else 512
                bank = g % 2
                P.op("pe", (lambda e, g=g, W=W, bank=bank: e.matmul(
                    psb[bank][0:16, 0:W], tab33[0:33, 0:16], oh[0:33, g * 512: g * 512 + W], start=True, stop=True)),
                    reads=["tab33a", "tab33b", ("oh", g)], writes=[PSR(bank)])
                P.op("act", (lambda e, g=g, W=W, bank=bank: e.activation(
                    out=vsb[:, g * 512: g * 512 + W], in_=psb[bank][0:16, 0:W], func=AF.Identity, scale=8.0)),
                    reads=[PSR(bank)], writes=[("vsb", g)])
                dstv = vecA[g] if g < 3 else vecC
                P.dma("sp", (lambda e, s, g=g, W=W, dstv=dstv: e.dma_start(out=dstv, in_=vsb[:, g * 512: g * 512 + W]).then_inc(s, 16)),
                      reads=[("vsb", g)], writes=[("vec", g)], semkey=("vec", g))
                if g < 3:
                    srcb = bass.AP(vecA.tensor, g * 16 * 384, [[384, 16], [0, 128], [1, 384]])
                    dstb = repA[g]
                else:
                    srcb = bass.AP(vecC.tensor, 0, [[512, 16], [0, 128], [1, 512]])
                    dstb = repC
                P.dma("sp", (lambda e, s, srcb=srcb, dstb=dstb: e.dma_start(out=dstb, in_=srcb).then_inc(s, 16)),
                      reads=[("vec", g)], writes=[("rep", g)], semkey=("rep", g))
            simple_load("sp", rpb_sb[:], rpbff, "rpb_sb")
            P.op("dve", lambda e: e.memset(zB[:], 8.0 * NEG), writes=["zB"])
            zview = bass.AP(zB[:].tensor, zB[:].offset + 48, [[zB[:].ap[0][0], 16], [64, 15], [1, 31]])
            P.op("act", lambda e: e.activation(out=zview, in_=rpb_sb[:].rearrange("p (a j) -> p a j", a=15), func=AF.Identity, scale=8.0),
                 reads=["rpb_sb", "zB"], writes=["zB"])
            P.dma("sp", lambda e, s: e.dma_start(out=vecB, in_=zB[:]).then_inc(s, 16), reads=["zB"], writes=["vecB"], semkey="vecB")
            srcb = bass.AP(vecB.tensor, 0, [[1024, 16], [0, 128], [1, 1024]])
            P.dma("sp", lambda e, s: e.dma_start(out=repB, in_=srcb).then_inc(s, 16), reads=["vecB"], writes=["repB"], semkey="repB")
            P.barrier()

        def norm_tile(xt, W, sq, bank, rs, tmp, a_t, b_t, col0, out_fn, tag):
            P.op("act", lambda e: e.activation(out=sq[:, 0:8 * W], in_=xt[:, 0:8 * W], func=AF.Square),
                 reads=[("xt", tag)], writes=["nsq"])

            def mm_ss(e):
                ins = None
                for c in range(8):
                    ins = e.matmul(psb[bank][:, 0:W], ones_bf[:], sq[:, c * W:(c + 1) * W], start=(c == 0), stop=(c == 7))
                return ins
            P.op("pe", mm_ss, reads=["nsq", "ones"], writes=[PSR(bank)])
            P.op("dve", lambda e: e.tensor_scalar(out=rs[:, 0:W], in0=psb[bank][:, 0:W], scalar1=1.0 / D, scalar2=EPS,
                                                  op0=ALU.mult, op1=ALU.add),
                 reads=[PSR(bank)], writes=["nrs"])
            P.op("act", lambda e: e.activation(out=rs[:, 0:W], in_=rs[:, 0:W], func=AF.Sqrt),
                 reads=["nrs"], writes=["nrs"])
            P.op("dve", lambda e: e.reciprocal(out=rs[:, 0:W], in_=rs[:, 0:W]),
                 reads=["nrs"], writes=["nrs"])
            for c in range(8):
                P.op("dve", (lambda e, c=c: e.scalar_tensor_tensor(
                    out=tmp[:, c * W:(c + 1) * W], in0=xt[:, c * W:(c + 1) * W], scalar=a_t[:, col0 + c: col0 + c + 1],
                    in1=rs[:, 0:W], op0=ALU.mult, op1=ALU.mult)),
                    reads=[("xt", tag), "nrs", "a1", "a2", "gfin"], writes=[("ntmp", c)])
                o_ap, o_res = out_fn(c)
                if b_t is None:
                    pass
                else:
                    bt, bcol = b_t
                    P.op("act", (lambda e, c=c, o_ap=o_ap, bt=bt, bcol=bcol: e.activation(
                        out=o_ap, in_=tmp[:, c * W:(c + 1) * W], func=AF.Identity, bias=bt[:, bcol + c: bcol + c + 1])),
                        reads=[("ntmp", c), "modsb"], writes=[o_res])

        def xview(ap, t0, W):
            return ap[:, t0:t0 + W].rearrange("(c p) t -> p c t", p=128)

        out_dma_ops = []
        def layer(l):
            kind = MIX[l]
            mj = MIXJ[l]
            x_src = xT_in if l == 0 else xs
            last_layer = (l == nlayers - 1)
            x_dst = yT if last_layer else xs
            P.new_epoch()
            with contextlib.ExitStack() as esL:
                hT = esL.enter_context(nc.sbuf_tensor(f"hT{l}", [128, 8 * S], BF16))

                with contextlib.ExitStack() as esn:
                    xts = [esn.enter_context(nc.sbuf_tensor(f"xt{l}_{i}", [128, 8 * 512], F32)) for i in range(2)]
                    sq = esn.enter_context(nc.sbuf_tensor(f"sq{l}", [128, 8 * 512], BF16))
                    rs = esn.enter_context(nc.sbuf_tensor(f"rs{l}", [128, 512], F32))
                    tmp = esn.enter_context(nc.sbuf_tensor(f"ntmp{l}", [128, 8 * 512], F32))
                    for tt in range(8):
                        slot = tt % 2
                        xt = xts[slot]
                        P.dma("sp", (lambda e, s, xt=xt, tt=tt: e.dma_start(
                            out=xt[:].rearrange("p (c t) -> p c t", c=8), in_=xview(x_src, tt * 512, 512)).then_inc(s, 16)),
                            reads=["xdram"], writes=[("xt", slot)], semkey=("xt", slot))

                        def out_fn(c, tt=tt):
                            return hT[:, c * S + tt * 512: c * S + (tt + 1) * 512], ("hT", tt)
                        norm_tile(xt, 512, sq, tt % 2, rs, tmp, a1, (modsb, l * 48 + 0), l * 8, out_fn, slot)
                    P.barrier()

                with contextlib.ExitStack() as esa:
                    def asb(name, shape, dt):
                        return esa.enter_context(nc.sbuf_tensor(f"{name}{l}", list(shape), dt))
                    w3 = [asb(f"w3_{i}_", [128, 8 * 384], BF16) for i in range(2)]
                    qT = asb("qT", [128, S], BF16)
                    kT = asb("kT", [128, S], BF16)
                    NVT = 32
                    Vp = asb("Vp", [128, NVT * 256], BF16)
                    acc = asb("acc", [128, 2 * S], F32)
                    rtmp = asb("rtmp", [128, 1024], F32)
                    NPT = 6
                    PW = 640 if kind == "B" else 384
                    PTb = [asb(f"PT{i}_", [128, PW], BF16) for i in range(NPT)]
                    OTp = asb("OTp", [128, S], BF16)
                    if kind == "A":
                        EBW = 256
                    elif kind == "C":
                        EBW = 384
                    if kind in ("A", "C"):
                        EB = [asb(f"EB{i}_", [128, 2 * EBW], F32) for i in range(2)]
                        Bhi = [asb(f"Bhi{i}_", [128, 2 * EBW], BF16) for i in range(2)]
                        Blo = [asb(f"Blo{i}_", [128, 2 * EBW], BF16) for i in range(2)]
                    else:
                        Tall = asb("Tall", [128, 2 * 896], F32)
                        BhiB = asb("BhiB", [128, 2 * B_EBW], BF16)
                        BloB = asb("BloB", [128, 2 * B_EBW], BF16)
                        btmp = asb("btmp", [128, 640], F32)
                        validB = asb("validB", [128, B_EBW], BF16)
                        simple_load("pool", validB[:], validB_d, "validB")
                    Vp4 = Vp[:].rearrange("p (t h c) -> p t h c", h=2, c=128)
                    P.op("pool", lambda e: e.memset(Vp[:], 1.0), writes=["Vp"])
                    es_t = es_sink if kind == "C" else es_zero
                    if kind == "A":
                        w_in_d = a_w_in[mj]
                        groups = A_GROUPS
                    elif kind == "B":
                        w_in_d = b_w_in[0]
                        groups = [(1, S)]
                    else:
                        w_in_d = c_w_in[0]
                        groups = [(1, S)]
                    witer = [0]
                    ebiter = [0]
                    ecnt = [0]
                    ptcnt = [0]
                    stcnt = [0]
                    otcnt = [0]
                    pjcnt = [0]

                    def wsrc(col0, n):
                        return w_in_d[:, col0:col0 + n].rearrange("(kc p) n -> p kc n", p=128)

                    for hp in range(8):
                        for gi, (dil, L) in enumerate(groups):
                            wslot = witer[0] % 2
                            witer[0] += 1
                            wb = w3[wslot]
                            wv3 = wb[:].rearrange("p (kc n) -> p kc n", kc=8)
                            if kind == "A":
                                specs = [(0, 128, gi * 3072 + hp * 128), (128, 128, gi * 3072 + 1024 + hp * 128),
                                         (256, 128, gi * 3072 + 2048 + hp * 128)]
                            elif kind == "B":
                                specs = [(0, 128, hp * 128), (128, 128, 1024 + hp * 128), (256, 128, 2048 + hp * 128)]
                            else:
                                kv = hp // 2
                                specs = [(0, 128, hp * 128), (128, 64, 1024 + kv * 64), (192, 64, 1024 + kv * 64),
                                         (256, 64, 1280 + kv * 64), (320, 64, 1280 + kv * 64)]

                            def wload(e, s, specs=specs, wv3=wv3):
                                for (o, n, c0) in specs:
                                    e.dma_start(out=wv3[:, :, o:o + n], in_=wsrc(c0, n)).then_inc(s, 16)
                            P.dma("pool", wload, writes=[("w3", wslot)], semkey=("w3", wslot), n=len(specs))

                            if kind in ("A", "C"):
                                ebslot = ebiter[0] % 2
                                ebiter[0] += 1
                                ebt = EB[ebslot]

                                def ebload(e, s, ebt=ebt, gi=gi, hp=hp):
                                    for h2 in range(2):
                                        h = hp * 2 + h2
                                        if kind == "A":
                                            src = bass.AP(repA.tensor, ((gi * 16 + h) * 128) * 384 + 127, [[383, 128], [1, 256]])
                                        else:
                                            src = bass.AP(repC.tensor, (h * 128) * 512 + 127, [[511, 128], [1, 384]])
                                        e.dma_start(out=ebt[:, h2 * EBW:(h2 + 1) * EBW], in_=src).then_inc(s, 16)
                                P.dma("sp", ebload, reads=[("rep", gi if kind == "A" else 3)], writes=[("EB", ebslot)],
                                      semkey=("EB", ebslot), n=2)
                                bhi = Bhi[ebslot]
                                blo = Blo[ebslot]
                                P.op("pool", (lambda e, ebt=ebt, bhi=bhi: e.tensor_copy(out=bhi[:], in_=ebt[:])),
                                     reads=[("EB", ebslot)], writes=[("Bhi", ebslot)])
                                P.op("pool", (lambda e, ebt=ebt, bhi=bhi, blo=blo: e.tensor_tensor(
                                    out=blo[:], in0=ebt[:], in1=bhi[:], op=ALU.subtract)),
                                    reads=[("EB", ebslot), ("Bhi", ebslot)], writes=[("Blo", ebslot)])
                            else:
                                def tload(e, s, hp=hp):
                                    for h2 in range(2):
                                        h = hp * 2 + h2
                                        src = bass.AP(repB.tensor, (h * 128) * 1024 + 127, [[1023, 128], [1, 896]])
                                        e.dma_start(out=Tall[:, h2 * 896:(h2 + 1) * 896], in_=src).then_inc(s, 16)
                                P.dma("sp", tload, reads=["repB"], writes=["Tall"], semkey="Tall", n=2)
                                for h2 in range(2):
                                    for name, _j, U, x0 in B_CLASSES:
                                        off, nU, _ = B_OFF[name]
                                        P.op("pool", (lambda e, h2=h2, off=off, nU=nU, x0=x0: e.tensor_tensor(
                                            out=btmp[:, 0:128 * nU],
                                            in0=Tall[:, h2 * 896 + x0: h2 * 896 + x0 + 128 * nU],
                                            in1=validB[:, off: off + 128 * nU], op=ALU.add)),
                                            reads=["Tall", "validB"], writes=["btmp"])
                                        P.op("pool", (lambda e, h2=h2, off=off, nU=nU: e.tensor_copy(
                                            out=BhiB[:, h2 * B_EBW + off: h2 * B_EBW + off + 128 * nU], in_=btmp[:, 0:128 * nU])),
                                            reads=["btmp"], writes=[("BhiB", h2)])
                                        P.op("pool", (lambda e, h2=h2, off=off, nU=nU: e.tensor_tensor(
                                            out=BloB[:, h2 * B_EBW + off: h2 * B_EBW + off + 128 * nU], in0=btmp[:, 0:128 * nU],
                                            in1=BhiB[:, h2 * B_EBW + off: h2 * B_EBW + off + 128 * nU], op=ALU.subtract)),
                                            reads=["btmp", ("BhiB", h2)], writes=[("BloB", h2)])

                            for which, dstT in ((0, qT), (1, kT)):
                                for tt in range(8):
                                    bank = pjcnt[0] % 2
                                    pjcnt[0] += 1

                                    def mm_p(e, which=which, tt=tt, bank=bank, wb=wb):
                                        ins = None
                                        for kc in range(8):
                                            ins = e.matmul(psb[bank][:, 0:512],
                                                           wb[:, kc * 384 + which * 128: kc * 384 + (which + 1) * 128],
                                                           hT[:, kc * S + tt * 512: kc * S + (tt + 1) * 512],
                                                           start=(kc == 0), stop=(kc == 7))
                                        return ins
                                    P.op("pe", mm_p, reads=[("w3", wslot)] + [("hT", tt)], writes=[PSR(bank)])
                                    o_qk = dstT[:, tt * 512:(tt + 1) * 512]
                                    i_qk = psb[bank][:, 0:512]
                                    if which == 1:
                                        P.op("act", (lambda e, o_qk=o_qk, i_qk=i_qk: e.activation(out=o_qk, in_=i_qk, func=AF.Copy)),
                                             reads=[PSR(bank)], writes=[("qk", which)])
                                    else:
                                        P.op("dve", (lambda e, o_qk=o_qk, i_qk=i_qk: e.tensor_copy(out=o_qk, in_=i_qk)),
                                             reads=[PSR(bank)], writes=[("qk", which)])
                            nT = L // 128
                            vt = 0
                            for r in range(dil):
                                for u0 in range(0, nT, 4):
                                    nb = min(4, nT - u0)
                                    bank = pjcnt[0] % 2
                                    pjcnt[0] += 1
                                    t_idx = r * nT + u0

                                    def mm_v(e, r=r, u0=u0, nb=nb, bank=bank, wb=wb, dil=dil):
                                        ins = None
                                        for i in range(nb):
                                            u = u0 + i
                                            for kc in range(8):
                                                ins = e.matmul(psb[bank][:, i * 128:(i + 1) * 128],
                                                               hT[:, sl(kc * S + r + dil * 128 * u, 128, dil)],
                                                               wb[:, kc * 384 + 256: kc * 384 + 384],
                                                               start=(kc == 0), stop=(kc == 7))
                                        return ins
                                    P.op("pe", mm_v, reads=[("w3", wslot)] + [("hT", t) for t in range(8)], writes=[PSR(bank)])
                                    vbase = Vp[:]
                                    o_ap = bass.AP(vbase.tensor, vbase.offset + t_idx * 256, [[vbase.ap[0][0], 128], [256, nb], [192, 2], [1, 64]])
                                    pb = psb[bank][:]
                                    i_ap = bass.AP(pb.tensor, pb.offset, [[pb.ap[0][0], 128], [128, nb], [64, 2], [1, 64]])
                                    P.op("dve", (lambda e, o_ap=o_ap, i_ap=i_ap: e.tensor_copy(out=o_ap, in_=i_ap)),
                                         reads=[PSR(bank)], writes=["Vp"])

                            first_group = (gi == 0)
                            for h2 in range(2):
                                prow = 64 * h2
                                if kind in ("A", "C"):
                                    Rr = 64 if kind == "A" else 128
                                    if kind == "A":
                                        blocks = []
                                        for j in range(nT + 1):
                                            lo, hi = max(0, 128 * j - 64), min(L, 128 * j + 64)
                                            tiles = [u for u in (j - 1, j) if 0 <= u < nT]
                                            blocks.append((lo, hi, tiles, min(j, nT - 1)))
                                    else:
                                        blocks = []
                                        for j in range(nT):
                                            tiles = [u for u in (j - 1, j, j + 1) if 0 <= u < nT]
                                            blocks.append((128 * j, 128 * j + 128, tiles, min(j + 1, nT - 1)))
                                    ogroups = []
                                    cur = []
                                    curw = 0
                                    for b in blocks:
                                        w = b[1] - b[0]
                                        if curw + w > 512:
                                            ogroups.append(cur)
                                            cur, curw = [], 0
                                        cur.append(b)
                                        curw += w
                                    if cur:
                                        ogroups.append(cur)
                                    blk_group = {}
                                    for gidx, gl in enumerate(ogroups):
                                        for b in gl:
                                            blk_group[b[0]] = gidx
                                    LAG = 2
                                    for r in range(dil):
                                        ptslot_of = {}
                                        qlo_of = {}
                                        grp_bank = {}
                                        done_in_grp = {}
                                        for step in range(nT + LAG):
                                            u = step
                                            if u < nT:
                                                qlo = max(0, 128 * u - Rr)
                                                qhi = min(L, 128 * u + 128 + Rr)
                                                W = qhi - qlo
                                                ebc = qlo - (128 * u - Rr)
                                                sbank = 2 + stcnt[0] % 4
                                                stcnt[0] += 1
                                                pslot = ptcnt[0] % NPT
                                                ptcnt[0] += 1
                                                ptslot_of[u] = pslot
                                                qlo_of[u] = qlo

                                                def mm_st(e, u=u, qlo=qlo, W=W, sbank=sbank, r=r, prow=prow, dil=dil, ebc=ebc, h2=h2, bhi=bhi, blo=blo):
                                                    e.matmul(psb[sbank][:, 0:W],
                                                             kT[prow:prow + 64, sl(r + dil * 128 * u, 128, dil)],
                                                             qT[prow:prow + 64, sl(r + dil * qlo, W, dil)], start=True, stop=False)
                                                    e.matmul(psb[sbank][:, 0:W], ident[:],
                                                             bhi[:, h2 * EBW + ebc: h2 * EBW + ebc + W], start=False, stop=False)
                                                    return e.matmul(psb[sbank][:, 0:W], ident[:],
                                                                    blo[:, h2 * EBW + ebc: h2 * EBW + ebc + W], start=False, stop=True)
                                                P.op("pe", mm_st, reads=[("qk", 0), ("qk", 1), ("Bhi", ebslot), ("Blo", ebslot), "ident"],
                                                     writes=[PSR(sbank)])
                                                P.op("act", (lambda e, W=W, sbank=sbank, pslot=pslot: e.activation(
                                                    out=PTb[pslot][:, 0:W], in_=psb[sbank][:, 0:W], func=AF.Exp, scale=0.125)),
                                                    reads=[PSR(sbank)], writes=[("PT", pslot)])
                                            v = step - LAG
                                            if v < 0:
                                                continue
                                            for b in blocks:
                                                lo, hi, tiles, ready = b
                                                if ready != v:
                                                    continue
                                                gidx = blk_group[lo]
                                                gl = ogroups[gidx]
                                                if gidx not in grp_bank:
                                                    grp_bank[gidx] = 6 + otcnt[0] % 2
                                                    otcnt[0] += 1
                                                    done_in_grp[gidx] = 0
                                                obank = grp_bank[gidx]
                                                base = gl[0][0]
                                                c0 = lo - base

                                                def mm_pv(e, lo=lo, hi=hi, tiles=tiles, obank=obank, c0=c0, r=r, h2=h2,
                                                          pts=dict(ptslot_of), qls=dict(qlo_of), nT=nT):
                                                    ins = None
                                                    for i, uu in enumerate(tiles):
                                                        pc = lo - qls[uu]
                                                        ins = e.matmul(psb[obank][:, c0:c0 + (hi - lo)],
                                                                       Vp4[:, r * nT + uu, h2, :],
                                                                       PTb[pts[uu]][:, pc:pc + (hi - lo)],
                                                                       start=(i == 0), stop=(i == len(tiles) - 1))
                                                    return ins
                                                P.op("pe", mm_pv, reads=[("PT", ptslot_of[uu]) for uu in tiles] + ["Vp"],
                                                     writes=[PSR(obank)])
                                                done_in_grp[gidx] += 1
                                                if done_in_grp[gidx] == len(gl):
                                                    wtot = gl[-1][1] - base
                                                    a_ap = acc[:, sl(h2 * S + r + dil * base, wtot, dil)]
                                                    if first_group:
                                                        P.op("dve", (lambda e, a_ap=a_ap, obank=obank, wtot=wtot: e.tensor_copy(
                                                            out=a_ap, in_=psb[obank][:, 0:wtot])),
                                                            reads=[PSR(obank), ("accbar", h2)], writes=[])
                                                    else:
                                                        P.op("dve", (lambda e, a_ap=a_ap, obank=obank, wtot=wtot: e.tensor_tensor(
                                                            out=a_ap, in0=psb[obank][:, 0:wtot], in1=a_ap, op=ALU.add)),
                                                            reads=[PSR(obank), ("accbar", h2)], writes=[])
                                else:
                                    LAG = 1
                                    pend = []
                                    for step in range(32 + LAG):
                                        j = step
                                        if j < 32:
                                            U = _b_tiles(j)
                                            n = len(U)
                                            off, _nU, _x0 = B_OFF[_b_class(j)]
                                            sb0 = 2 + 2 * (stcnt[0] % 2)
                                            stcnt[0] += 1
                                            pslot = ptcnt[0] % NPT
                                            ptcnt[0] += 1

                                            def mm_sb(e, j=j, U=U, n=n, sb0=sb0, prow=prow, off=off, h2=h2):
                                                ins = None
                                                for i, uu in enumerate(U):
                                                    bk = sb0 + (i // 4)
                                                    ins = e.matmul(psb[bk][:, (i % 4) * 128:(i % 4 + 1) * 128],
                                                                   kT[prow:prow + 64, uu * 128:(uu + 1) * 128],
                                                                   qT[prow:prow + 64, j * 128:(j + 1) * 128],
                                                                   start=(i % 4 == 0), stop=False, skip_group_check=True)
                                                n0 = min(n, 4) * 128
                                                b0 = h2 * B_EBW + off
                                                e.matmul(psb[sb0][:, 0:n0], ident[:], BhiB[:, b0:b0 + n0], start=False, stop=False, skip_group_check=True)
                                                ins = e.matmul(psb[sb0][:, 0:n0], ident[:], BloB[:, b0:b0 + n0], start=False, stop=True, skip_group_check=True)
                                                if n > 4:
                                                    e.matmul(psb[sb0 + 1][:, 0:128], ident[:], BhiB[:, b0 + 512:b0 + 640], start=False, stop=False, skip_group_check=True)
                                                    ins = e.matmul(psb[sb0 + 1][:, 0:128], ident[:], BloB[:, b0 + 512:b0 + 640], start=False, stop=True, skip_group_check=True)
                                                return ins
                                            P.op("pe", mm_sb, reads=[("qk", 0), ("qk", 1), ("BhiB", h2), ("BloB", h2), "ident"],
                                                 writes=[PSR(sb0), PSR(sb0 + 1)])

                                            def ex_b(e, n=n, sb0=sb0, pslot=pslot):
                                                ins = e.activation(out=PTb[pslot][:, 0:min(n, 4) * 128], in_=psb[sb0][:, 0:min(n, 4) * 128],
                                                                   func=AF.Exp, scale=0.125)
                                                if n > 4:
                                                    ins = e.activation(out=PTb[pslot][:, 512:640], in_=psb[sb0 + 1][:, 0:128],
                                                                       func=AF.Exp, scale=0.125)
                                                return ins
                                            P.op("act", ex_b, reads=[PSR(sb0), PSR(sb0 + 1)], writes=[("PT", pslot)])
                                            pend.append((j, U, pslot))
                                        v = step - LAG
                                        if v < 0:
                                            continue
                                        jv, Uv, psl = pend[v]
                                        if jv % 4 == 0:
                                            obank_cur = 6 + otcnt[0] % 2
                                            otcnt[0] += 1
                                        obank = obank_cur
                                        c0 = (jv % 4) * 128

                                        def mm_pvb(e, Uv=Uv, psl=psl, obank=obank, c0=c0, h2=h2):
                                            ins = None
                                            for i, uu in enumerate(Uv):
                                                ins = e.matmul(psb[obank][:, c0:c0 + 128], Vp4[:, uu, h2, :],
                                                               PTb[psl][:, i * 128:(i + 1) * 128],
                                                               start=(i == 0), stop=(i == len(Uv) - 1))
                                            return ins
                                        P.op("pe", mm_pvb, reads=[("PT", psl), "Vp"], writes=[PSR(obank)])
                                        if jv % 4 == 3:
                                            a_ap = acc[:, h2 * S + (jv - 3) * 128: h2 * S + (jv + 1) * 128]
                                            P.op("dve", (lambda e, a_ap=a_ap, obank=obank: e.tensor_copy(
                                                out=a_ap, in_=psb[obank][:, 0:512])),
                                                reads=[PSR(obank), ("accbar", h2)], writes=[])
                            for h2 in range(2):
                                P.op("dve", (lambda e: e.memset(dummy[:, 0:1], 0.0)), reads=[], writes=[("accbar", h2)])

                        for ck in range(4):
                            t0 = ck * 1024
                            h0 = hp * 2
                            P.op("dve", (lambda e, t0=t0, h0=h0: e.tensor_scalar(
                                out=rtmp[0:64, :], in0=acc[64:128, t0:t0 + 1024], scalar1=es_t[64:128, h0:h0 + 1], scalar2=None,
                                op0=ALU.add)),
                                reads=[("accbar", 0), "es_sink", "es_zero"], writes=["rtmp"])
                            P.op("dve", (lambda e: e.reciprocal(out=rtmp[0:64, :], in_=rtmp[0:64, :])),
                                reads=["rtmp"], writes=["rtmp"])
                            P.op("dve", (lambda e, t0=t0, h0=h0: e.tensor_scalar(
                                out=rtmp[64:128, :], in0=acc[0:64, S + t0:S + t0 + 1024], scalar1=es_t[0:64, h0 + 1:h0 + 2], scalar2=None,
                                op0=ALU.add)),
                                reads=[("accbar", 1), "es_sink", "es_zero"], writes=["rtmp2"])
                            P.op("dve", (lambda e: e.reciprocal(out=rtmp[64:128, :], in_=rtmp[64:128, :])),
                                reads=["rtmp2"], writes=["rtmp2"])
                            P.op("pool", (lambda e, t0=t0: e.tensor_tensor(
                                out=OTp[0:64, t0:t0 + 1024], in0=acc[0:64, t0:t0 + 1024], in1=rtmp[0:64, :], op=ALU.mult)),
                                reads=["rtmp", ("accbar", 0)], writes=["OTp"])
                            P.op("pool", (lambda e, t0=t0: e.tensor_tensor(
                                out=OTp[64:128, t0:t0 + 1024], in0=acc[64:128, S + t0:S + t0 + 1024], in1=rtmp[64:128, :], op=ALU.mult)),
                                reads=["rtmp2", ("accbar", 1)], writes=["OTp"])
                        P.dma("sp", (lambda e, s, hp=hp: e.dma_start(out=OTd[hp * 128:(hp + 1) * 128, :], in_=OTp[:]).then_inc(s, 16)),
                              reads=["OTp"], writes=["OTd"], semkey="OTp")
                        for h2 in range(2):
                            P.op("dve", (lambda e: e.memset(dummy[:, 0:1], 0.0)), reads=[], writes=[("accbar", h2)])
                    P.barrier()
            P.barrier()

            with contextlib.ExitStack() as esf:
                def fsb(name, shape, dt):
                    return esf.enter_context(nc.sbuf_tensor(f"{name}{l}", list(shape), dt))
                TW = 256
                wo = fsb("wo", [128, 8 * 1024], BF16)
                win = fsb("win", [128, 8 * 2 * DFF], BF16)
                wout = fsb("wout", [128, NFC * 1024], BF16)
                xts = [fsb(f"fx{i}_", [128, 8 * TW], F32) for i in range(2)]
                ots = [fsb(f"fo{i}_", [128, 8 * TW], BF16) for i in range(2)]
                sq = fsb("fsq", [128, 8 * TW], BF16)
                rs = fsb("frs", [128, TW], F32)
                tmp = fsb("ftmp", [128, 8 * TW], F32)
                h2T = fsb("fh2", [128, 8 * TW], BF16)
                aT = fsb("faT", [128, NFC * TW], BF16)
                sg = [fsb(f"fsg{i}_", [128, TW], BF16) for i in range(2)]
                wo_d = {"A": a_w_out, "B": b_w_out, "C": c_w_out}[kind][mj]
                wo3 = wo[:].rearrange("p (kc n) -> p kc n", kc=8)
                for pc in range(2):
                    P.dma("pool", (lambda e, s, pc=pc: e.dma_start(
                        out=wo3[:, :, pc * 512:(pc + 1) * 512],
                        in_=wo_d[:, pc * 512:(pc + 1) * 512].rearrange("(kc p) n -> p kc n", p=128)).then_inc(s, 16)),
                        writes=[("wo", pc)], semkey=("wo", pc))
                win3 = win[:].rearrange("p (kc n) -> p kc n", kc=8)
                for pc in range(11):
                    P.dma("pool", (lambda e, s, pc=pc: e.dma_start(
                        out=win3[:, :, pc * 512:(pc + 1) * 512],
                        in_=ffn_w_in[l, :, pc * 512:(pc + 1) * 512].rearrange("(kc p) n -> p kc n", p=128)).then_inc(s, 16)),
                        writes=[("win", pc)], semkey=("win", pc))
                wout3 = wout[:].rearrange("p (fc n) -> p fc n", fc=NFC)
                for pc in range(11):
                    P.dma("pool", (lambda e, s, pc=pc: e.dma_start(
                        out=wout3[:, 2 * pc:2 * pc + 2, :],
                        in_=ffn_w_out[l, pc * 256:(pc + 1) * 256, :].rearrange("(fc p) n -> p fc n", p=128)).then_inc(s, 16)),
                        writes=[("wout", pc)], semkey=("wout", pc))
                mcol = l * 48
                NTT = S // TW
                bankc = [0]

                def nb():
                    b = bankc[0] % 8
                    bankc[0] += 1
                    return b
                for tt in range(NTT):
                    slot = tt % 2
                    xt = xts[slot]
                    ot = ots[slot]
                    t0 = tt * TW
                    P.dma("sp", (lambda e, s, xt=xt, t0=t0: e.dma_start(
                        out=xt[:].rearrange("p (c t) -> p c t", c=8), in_=xview(x_src, t0, TW)).then_inc(s, 16)),
                        reads=["xdram"], writes=[("xt", slot)], semkey=("fx", slot))
                    P.dma("sp", (lambda e, s, ot=ot, t0=t0: e.dma_start(
                        out=ot[:].rearrange("p (c t) -> p c t", c=8), in_=xview(OTd, t0, TW)).then_inc(s, 16)),
                        reads=["OTd"], writes=[("ot", slot)], semkey=("fo", slot))
                    for m in range(8):
                        bank = nb()

                        def mm_o(e, m=m, bank=bank, ot=ot):
                            ins = None
                            for kc in range(8):
                                ins = e.matmul(psb[bank][:, 0:TW], wo[:, kc * 1024 + m * 128: kc * 1024 + (m + 1) * 128],
                                               ot[:, kc * TW:(kc + 1) * TW], start=(kc == 0), stop=(kc == 7))
                            return ins
                        P.op("pe", mm_o, reads=[("wo", m // 4), ("ot", slot)], writes=[PSR(bank)])
                        P.op("dve", (lambda e, m=m, bank=bank, xt=xt: e.scalar_tensor_tensor(
                            out=xt[:, m * TW:(m + 1) * TW], in0=psb[bank][:, 0:TW], scalar=modsb[:, mcol + 16 + m: mcol + 17 + m],
                            in1=xt[:, m * TW:(m + 1) * TW], op0=ALU.mult, op1=ALU.add)),
                            reads=[PSR(bank), ("xt", slot), "modsb"], writes=[("xt", slot)])
                    def out_fn2(c):
                        return h2T[:, c * TW:(c + 1) * TW], "h2T"
                    norm_tile(xt, TW, sq, nb(), rs, tmp, a2, (modsb, mcol + 24), l * 8, out_fn2, slot)
                    for f in range(NFC):
                        bg = nb()
                        bu = nb()
                        if bu == bg:
                            bu = nb()

                        def mm_gu(e, f=f, bg=bg, bu=bu):
                            ins = None
                            for kc in range(8):
                                ins = e.matmul(psb[bg][:, 0:TW], win[:, kc * 2 * DFF + f * 128: kc * 2 * DFF + (f + 1) * 128],
                                               h2T[:, kc * TW:(kc + 1) * TW], start=(kc == 0), stop=(kc == 7))
                            for kc in range(8):
                                ins = e.matmul(psb[bu][:, 0:TW], win[:, kc * 2 * DFF + DFF + f * 128: kc * 2 * DFF + DFF + (f + 1) * 128],
                                               h2T[:, kc * TW:(kc + 1) * TW], start=(kc == 0), stop=(kc == 7))
                            return ins
                        P.op("pe", mm_gu, reads=[("win", (f * 128) // 512), ("win", (DFF + f * 128) // 512), "h2T"],
                             writes=[PSR(bg), PSR(bu)])
                        sslot = f % 2
                        P.op("act", (lambda e, bg=bg, sslot=sslot: e.activation(out=sg[sslot][:], in_=psb[bg][:, 0:TW], func=AF.Silu)),
                             reads=[PSR(bg)], writes=[("sg", sslot)])
                        P.op("dve", (lambda e, f=f, bu=bu, sslot=sslot: e.tensor_tensor(
                            out=aT[:, f * TW:(f + 1) * TW], in0=psb[bu][:, 0:TW], in1=sg[sslot][:], op=ALU.mult)),
                            reads=[PSR(bu), ("sg", sslot)], writes=[("aT", f)])
                    for m in range(8):
                        bank = nb()

                        def mm_f(e, m=m, bank=bank):
                            ins = None
                            for f in range(NFC):
                                ins = e.matmul(psb[bank][:, 0:TW], wout[:, f * 1024 + m * 128: f * 1024 + (m + 1) * 128],
                                               aT[:, f * TW:(f + 1) * TW], start=(f == 0), stop=(f == NFC - 1))
                            return ins
                        P.op("pe", mm_f, reads=[("wout", pc) for pc in range(11)] + [("aT", f) for f in range(NFC)],
                             writes=[PSR(bank)])
                        P.op("dve", (lambda e, m=m, bank=bank, xt=xt: e.scalar_tensor_tensor(
                            out=xt[:, m * TW:(m + 1) * TW], in0=psb[bank][:, 0:TW], scalar=modsb[:, mcol + 40 + m: mcol + 41 + m],
                            in1=xt[:, m * TW:(m + 1) * TW], op0=ALU.mult, op1=ALU.add)),
                            reads=[PSR(bank), ("xt", slot), "modsb"], writes=[("xt", slot)])
                    if last_layer:
                        def out_fn3(c):
                            return None, None
                        norm_tile(xt, TW, sq, nb(), rs, tmp, gfin_sb, None, 0, out_fn3, slot)
                        o = P.dma("sp", (lambda e, s, t0=t0: e.dma_start(
                            out=xview(x_dst, t0, TW), in_=tmp[:].rearrange("p (c t) -> p c t", c=8)).then_inc(s, 16)),
                            reads=[("ntmp", c) for c in range(8)], writes=["ydram"], semkey="yo")
                        out_dma_ops.append(o)
                    else:
                        P.dma("sp", (lambda e, s, xt=xt, t0=t0: e.dma_start(
                            out=xview(x_dst, t0, TW), in_=xt[:].rearrange("p (c t) -> p c t", c=8)).then_inc(s, 16)),
                            reads=[("xt", slot)], writes=["xdram_w"], semkey=("fxs", slot))
                P.barrier()
        for l in range(nlayers):
            layer(l)
        P.emit(final_waits=out_dma_ops[-1:])
    return nc


_PROGRAM_CACHE = {}


def _layout_inputs(inputs):
    f = np.float32
    x = np.asarray(inputs["x"], f)
    c = np.asarray(inputs["c"], f)
    ohA, ohC, validB = _static_tables()

    def l128(v):
        return np.ascontiguousarray(np.asarray(v, f).reshape(-1, 128).T)

    shared = {
        "rel_bias": np.ascontiguousarray(np.asarray(inputs["rel_bias"], f)),
        "ada_w": np.ascontiguousarray(np.asarray(inputs["ada_w"], f)),
        "ada_bl": np.ascontiguousarray(np.asarray(inputs["ada_b"], f).reshape(DEPTH, 48, 128).transpose(0, 2, 1)),
        "gmix": l128(np.asarray(inputs["norm_mix"], f)),
        "gffn": l128(np.asarray(inputs["norm_ffn"], f)),
        "gfin": l128(np.asarray(inputs["norm_final"], f)),
        "a_w_in": np.ascontiguousarray(np.asarray(inputs["a_w_in"], f)),
        "a_w_out": np.ascontiguousarray(np.asarray(inputs["a_w_out"], f)),
        "b_w_in": np.ascontiguousarray(np.asarray(inputs["b_w_in"], f)),
        "b_w_out": np.ascontiguousarray(np.asarray(inputs["b_w_out"], f)),
        "rpbff": np.ascontiguousarray(np.asarray(inputs["b_rpb"], f)[0][:, ::-1, ::-1].reshape(16, 465)),
        "c_w_in": np.ascontiguousarray(np.asarray(inputs["c_w_in"], f)),
        "c_w_out": np.ascontiguousarray(np.asarray(inputs["c_w_out"], f)),
        "c_sink": np.ascontiguousarray(np.asarray(inputs["c_sink"], f)),
        "ffn_w_in": np.ascontiguousarray(np.asarray(inputs["ffn_w_in"], f)),
        "ffn_w_out": np.ascontiguousarray(np.asarray(inputs["ffn_w_out"], f)),
        "ohA": ohA, "ohC": ohC, "validB": np.ascontiguousarray((validB - 1.0) * 262144.0).astype(np.float32),
        "ident": np.eye(128, dtype=np.float32),
    }
    in_maps = []
    for b in range(8):
        m = dict(shared)
        m["xT"] = np.ascontiguousarray(x[b].T)
        m["cb"] = l128(c[b])
        in_maps.append(m)
    return in_maps


def kernel(**inputs):
    in_maps = _layout_inputs(inputs)
    nc = build_program(DEPTH)
    res = run_bass_kernel_spmd(nc, in_maps, core_ids=list(range(8)))
    out = np.stack([np.ascontiguousarray(np.asarray(r["yT"]).T) for r in res.results], axis=0)
    return out.astype(np.float32)
```

```python
import math
import contextlib
import numpy as np
import concourse.bass as bass
import concourse.mybir as mybir
from concourse.bass_utils import run_bass_kernel_spmd

F32 = mybir.dt.float32
BF16 = mybir.dt.bfloat16
ALU = mybir.AluOpType
AF = mybir.ActivationFunctionType

D = 1024
S = 4096
DEPTH = 4
DFF = 2816
NFC = DFF // 128
EPS = 1e-6
NEG = -30000.0
ENGS = ("pe", "act", "dve", "pool", "sp")


class _Op:
    __slots__ = ("eng", "fn", "waits", "sig_key", "sig_val", "signaled", "idx", "epoch", "is_dma")


class Prog:
    def __init__(self, nc):
        self.nc = nc
        self.ops = {e: [] for e in ENGS}
        self.last_w = {}
        self.readers = {}
        self.epoch = 0
        self.dma_cnt = {}
        self.dma_last = {}
        self.waited = {}

    def new_epoch(self):
        self.epoch += 1

    def _stream(self, op):
        return ("dma", op.sig_key) if op.is_dma else ("eng", op.eng)

    def _pos(self, d):
        return d.sig_val if d.is_dma else (d.epoch, d.idx)

    def _finish(self, eng, op, deps):
        waits = {}
        for d, raw in deps:
            if (not d.is_dma) and d.eng == eng:
                if eng in ("pe", "sp") or not raw:
                    continue
            stt = self._stream(d)
            pos = self._pos(d)
            prev = waits.get(stt)
            if prev is None or pos > prev[0]:
                waits[stt] = (pos, d)
        final = []
        for stt, (pos, d) in waits.items():
            k = (eng, stt)
            have = self.waited.get(k)
            if have is not None and have >= pos:
                continue
            self.waited[k] = pos
            d.signaled = True
            final.append(d)
        op.waits = final
        self.ops[eng].append(op)

    def _add(self, eng, fn, reads, writes, is_dma=False, semkey=None, n_dma=1):
        op = _Op()
        op.eng = eng
        op.fn = fn
        op.is_dma = is_dma
        op.signaled = is_dma
        op.epoch = self.epoch
        op.idx = len(self.ops[eng])
        op.sig_key = None
        op.sig_val = None
        if is_dma:
            c = self.dma_cnt.get(semkey, 0) + 16 * n_dma
            self.dma_cnt[semkey] = c
            op.sig_key = semkey
            op.sig_val = c
            self.dma_last[semkey] = op
        deps = []
        for r in reads:
            w = self.last_w.get(r)
            if w is not None:
                deps.append((w, True))
        for w_ in writes:
            w = self.last_w.get(w_)
            if w is not None:
                deps.append((w, False))
            for rd in self.readers.get(w_, {}).values():
                deps.append((rd, False))
        self._finish(eng, op, deps)
        for r in reads:
            self.readers.setdefault(r, {})[self._stream(op)] = op
        for w_ in writes:
            self.last_w[w_] = op
            self.readers[w_] = {}
        return op

    def op(self, eng, fn, reads=(), writes=()):
        return self._add(eng, fn, tuple(reads), tuple(writes))

    def dma(self, q, fn, reads=(), writes=(), semkey=None, n=1):
        return self._add(q, fn, tuple(reads), tuple(writes), is_dma=True, semkey=semkey, n_dma=n)

    def barrier(self):
        lasts = [self.ops[e][-1] for e in ENGS if self.ops[e] and not self.ops[e][-1].is_dma]
        lasts = []
        for e in ENGS:
            for o in reversed(self.ops[e]):
                if not o.is_dma and o.fn is not None:
                    lasts.append(o)
                    break
        lasts += list(self.dma_last.values())
        for e in ENGS:
            op = _Op()
            op.eng = e
            op.fn = None
            op.is_dma = False
            op.signaled = False
            op.epoch = self.epoch
            op.idx = len(self.ops[e])
            op.sig_key = None
            op.sig_val = None
            self._finish(e, op, [(d, True) for d in lasts if d.is_dma or d.eng != e])

    def emit(self, final_waits=()):
        nc = self.nc
        with contextlib.ExitStack() as st:
            esem = {}
            for e in ENGS:
                used = sorted({o.epoch for o in self.ops[e] if o.signaled and not o.is_dma})
                for ep in used:
                    esem[(e, ep)] = st.enter_context(nc.semaphore(f"s_{e}_{ep}"))
            dsem = {}
            for k in self.dma_cnt:
                dsem[k] = st.enter_context(nc.semaphore(f"d_{len(dsem)}"))
            for e in ENGS:
                cnt = {}
                for o in self.ops[e]:
                    if o.is_dma or not o.signaled:
                        continue
                    c = cnt.get(o.epoch, 0) + 1
                    cnt[o.epoch] = c
                    o.sig_key = (e, o.epoch)
                    o.sig_val = c
            block = st.enter_context(nc.Block())

            def run(e, engine):
                for o in self.ops[e]:
                    for d in o.waits:
                        if d.is_dma:
                            engine.wait_ge(dsem[d.sig_key], d.sig_val)
                        else:
                            engine.wait_ge(esem[d.sig_key], d.sig_val)
                    if o.fn is None:
                        continue
                    if o.is_dma:
                        o.fn(engine, dsem[o.sig_key])
                    else:
                        ins = o.fn(engine)
                        if o.signaled:
                            ins.then_inc(esem[o.sig_key], 1)
                if e == "sp":
                    for d in final_waits:
                        engine.wait_ge(dsem[d.sig_key], d.sig_val)

            @block.tensor
            def _(eng):
                run("pe", eng)

            @block.scalar
            def _(eng):
                run("act", eng)

            @block.vector
            def _(eng):
                run("dve", eng)

            @block.gpsimd
            def _(eng):
                run("pool", eng)

            @block.sync
            def _(eng):
                run("sp", eng)


def sl(start, count, step=1):
    return slice(start, start + step * (count - 1) + 1, step)


def _t5_bucket(rel):
    half, max_exact = 16, 8
    ret = np.where(rel > 0, half, 0)
    n = np.abs(rel)
    nf = np.maximum(n, 1).astype(np.float32)
    large = max_exact + (np.log(nf / np.float32(max_exact)) / np.float32(math.log(1024 / max_exact))
                         * np.float32(half - max_exact)).astype(np.int32)
    large = np.minimum(large, half - 1)
    return ret + np.where(n < max_exact, n, large)


def _onehot_band(length, pad, center, radius, dil):
    oh = np.zeros((33, pad), np.float32)
    i = np.arange(pad)
    rel = center - i
    valid = (np.abs(rel) <= radius) & (i < length)
    b = _t5_bucket(rel * dil)
    for k in range(pad):
        if valid[k]:
            oh[b[k], k] = 1.0
        else:
            oh[32, k] = 1.0
    return oh


B_CLASSES = [
    ("int", 2, [4, 3, 2, 1, 0], 128),
    ("e0", 0, [3, 2, 1, 0], 0),
    ("e1", 1, [3, 2, 1, 0], 128),
    ("e30", 30, [31, 30, 29, 28], 256),
    ("e31", 31, [31, 30, 29, 28], 384),
]
B_OFF = {}
_o = 0
for _n, _j, _U, _x in B_CLASSES:
    B_OFF[_n] = (_o, len(_U), _x)
    _o += 128 * len(_U)
B_EBW = _o


def _b_class(j):
    return {0: "e0", 1: "e1", 30: "e30", 31: "e31"}.get(j, "int")


def _b_tiles(j):
    if j <= 1:
        return [3, 2, 1, 0]
    if j >= 30:
        return [31, 30, 29, 28]
    return [j + 2, j + 1, j, j - 1, j - 2]


def _valid_b():
    out = np.zeros((128, B_EBW), np.float32)
    kk = np.arange(128)[:, None]
    qq = np.arange(128)[None, :]
    for name, j, U, _x in B_CLASSES:
        off = B_OFF[name][0]
        for i, u in enumerate(U):
            kt = 128 * u + kk
            qt = 128 * j + qq
            kr, kc = kt // 64, kt % 64
            r, c = qt // 64, qt % 64
            rs = np.clip(r - 4, 0, 56)
            cs = np.clip(c - 8, 0, 48)
            v = (kr >= rs) & (kr < rs + 8) & (kc >= cs) & (kc < cs + 16)
            out[:, off + 128 * i: off + 128 * (i + 1)] = v.astype(np.float32)
    return out


def _static_tables():
    ohA = np.stack([_onehot_band(383, 384, 191, 64, d) for d in (1, 4, 16)])
    ohC = _onehot_band(511, 512, 255, 128, 1)
    return ohA, ohC, _valid_b()


MIX = ["A", "B", "C", "A"]
MIXJ = [0, 0, 0, 1]
A_GROUPS = [(1, 4096), (4, 1024), (16, 256)]


def build_program(nlayers=DEPTH):
    nc = bass.Bass("TRN2", target_bir_lowering=False)

    def din(name, shape, dt=F32):
        return nc.dram_tensor(name, list(shape), dt, kind="ExternalInput").ap()

    xT_in = din("xT", [D, S])
    cb_in = din("cb", [128, 8])
    relb = din("rel_bias", [32, 16])
    ada_w = din("ada_w", [DEPTH, D, 6 * D])
    ada_b = din("ada_bl", [DEPTH, 128, 48])
    gmix = din("gmix", [128, DEPTH * 8])
    gffn = din("gffn", [128, DEPTH * 8])
    gfin = din("gfin", [128, 8])
    a_w_in = din("a_w_in", [2, D, 9216])
    a_w_out = din("a_w_out", [2, D, D])
    b_w_in = din("b_w_in", [1, D, 3072])
    b_w_out = din("b_w_out", [1, D, D])
    rpbff = din("rpbff", [16, 15 * 31])
    c_w_in = din("c_w_in", [1, D, 1536])
    c_w_out = din("c_w_out", [1, D, D])
    c_sink = din("c_sink", [1, 16])
    ffn_w_in = din("ffn_w_in", [DEPTH, D, 2 * DFF])
    ffn_w_out = din("ffn_w_out", [DEPTH, DFF, D])
    ohA_d = din("ohA", [3, 33, 384])
    ohC_d = din("ohC", [33, 512])
    validB_d = din("validB", [128, B_EBW])
    yT = nc.dram_tensor("yT", [D, S], F32, kind="ExternalOutput").ap()

    xs = nc.dram_tensor("xs", [D, S], F32).ap()
    OTd = nc.dram_tensor("OTd", [D, S], BF16).ap()
    vecA = nc.dram_tensor("vecA", [3, 16, 384], F32).ap()
    vecC = nc.dram_tensor("vecC", [16, 512], F32).ap()
    vecB = nc.dram_tensor("vecB", [16, 1024], F32).ap()
    repA = nc.dram_tensor("repA", [3, 16, 128, 384], F32).ap()
    repC = nc.dram_tensor("repC", [16, 128, 512], F32).ap()
    repB = nc.dram_tensor("repB", [16, 128, 1024], F32).ap()

    P = Prog(nc)
    ES = contextlib.ExitStack()
    with ES:
        def sb(name, shape, dt):
            return ES.enter_context(nc.sbuf_tensor("t_" + name, list(shape), dt))

        psb = [ES.enter_context(nc.psum_tensor(f"psb{i}", [128, 512], F32)) for i in range(8)]

        def PSR(i):
            return ("ps", i)

        ones_bf = sb("ones_bf", [128, 128], BF16)
        cb = sb("cb", [128, 8], F32)
        condb = sb("condb", [128, 8], BF16)
        modsb = sb("modsb", [128, DEPTH * 48], F32)
        adab = sb("adab", [128, DEPTH * 48], F32)
        gmix_sb = sb("gmix_sb", [128, DEPTH * 8], F32)
        gffn_sb = sb("gffn_sb", [128, DEPTH * 8], F32)
        gfin_sb = sb("gfin_sb", [128, 8], F32)
        a1 = sb("a1", [128, DEPTH * 8], F32)
        a2 = sb("a2", [128, DEPTH * 8], F32)
        es_sink = sb("es_sink", [128, 16], F32)
        es_zero = sb("es_zero", [128, 16], F32)
        dummy = sb("dummy", [128, 8], F32)

        P.op("dve", lambda e: e.memset(ones_bf[:], 1.0), writes=["ones"])
        P.op("dve", lambda e: e.memset(es_zero[:], 0.0), writes=["es_zero"])

        def simple_load(q, dst, src, res):
            P.dma(q, lambda e, s: e.dma_start(out=dst, in_=src).then_inc(s, 16), writes=[res], semkey=res)

        simple_load("sp", cb[:], cb_in, "cb")
        simple_load("sp", adab[:].rearrange("p (l j) -> p l j", l=DEPTH), ada_b.rearrange("l p j -> p l j"), "adab")
        simple_load("sp", gmix_sb[:], gmix, "gmix")
        simple_load("sp", gffn_sb[:], gffn, "gffn")
        simple_load("sp", gfin_sb[:], gfin, "gfin")
        P.dma("sp", lambda e, s: e.dma_start(out=es_sink[:], in_=bass.AP(c_sink.tensor, 0, [[0, 128], [1, 16]])).then_inc(s, 16),
              writes=["es_sink"], semkey="es_sink")
        P.op("act", lambda e: e.activation(out=condb[:], in_=cb[:], func=AF.Silu), reads=["cb"], writes=["condb"])
        P.op("act", lambda e: e.activation(out=es_sink[:], in_=es_sink[:], func=AF.Exp), reads=["es_sink"], writes=["es_sink"])

        with contextlib.ExitStack() as es1:
            adw = [es1.enter_context(nc.sbuf_tensor(f"adw{i}", [128, 8 * 1024], BF16)) for i in range(2)]
            it = 0
            for l in range(nlayers):
                for piece in range(6):
                    slot = it % 2
                    it += 1
                    buf = adw[slot]
                    src = ada_w[l, :, piece * 1024:(piece + 1) * 1024].rearrange("(kc p) n -> p kc n", p=128)
                    dst = buf[:].rearrange("p (kc n) -> p kc n", kc=8)
                    P.dma("pool", (lambda e, s, dst=dst, src=src: e.dma_start(out=dst, in_=src).then_inc(s, 16)),
                          writes=[("adw", slot)], semkey=("adw", slot))
                    bank = (l * 6 + piece) % 2

                    def mm_mod(e, buf=buf, piece=piece, bank=bank):
                        ins = None
                        for fc in range(8):
                            for kc in range(8):
                                ins = e.matmul(psb[bank][:, fc:fc + 1],
                                               buf[:, kc * 1024 + fc * 128: kc * 1024 + (fc + 1) * 128],
                                               condb[:, kc:kc + 1], start=(kc == 0), stop=(kc == 7))
                        return ins
                    P.op("pe", mm_mod, reads=[("adw", slot), "condb"], writes=[PSR(bank)])
                    col = l * 48 + piece * 8
                    P.op("dve", (lambda e, bank=bank, col=col: e.tensor_tensor(
                        out=modsb[:, col:col + 8], in0=psb[bank][:, 0:8], in1=adab[:, col:col + 8], op=ALU.add)),
                        reads=[PSR(bank), "adab"], writes=["modsb"])
            for l in range(nlayers):
                P.op("dve", (lambda e, l=l: e.scalar_tensor_tensor(
                    out=a1[:, l * 8:(l + 1) * 8], in0=modsb[:, l * 48 + 8: l * 48 + 16], scalar=1.0,
                    in1=gmix_sb[:, l * 8:(l + 1) * 8], op0=ALU.add, op1=ALU.mult)),
                    reads=["modsb", "gmix"], writes=["a1"])
                P.op("dve", (lambda e, l=l: e.scalar_tensor_tensor(
                    out=a2[:, l * 8:(l + 1) * 8], in0=modsb[:, l * 48 + 32: l * 48 + 40], scalar=1.0,
                    in1=gffn_sb[:, l * 8:(l + 1) * 8], op0=ALU.add, op1=ALU.mult)),
                    reads=["modsb", "gffn"], writes=["a2"])
            P.barrier()

        kinds_used = set(MIX[:nlayers])
        with contextlib.ExitStack() as es2:
            tab33 = es2.enter_context(nc.sbuf_tensor("tab33", [33, 16], F32))
            oh = es2.enter_context(nc.sbuf_tensor("oh", [33, 4 * 512], F32))
            vsb = es2.enter_context(nc.sbuf_tensor("vsb", [16, 4 * 512], F32))
            rpb_sb = es2.enter_context(nc.sbuf_tensor("rpb_sb", [16, 465], F32))
            zB = es2.enter_context(nc.sbuf_tensor("zB", [16, 1024], F32))
            P.op("dve", lambda e: e.memset(tab33[32:33, :], NEG), writes=["tab33b"])
            simple_load("sp", tab33[0:32, :], relb, "tab33a")
            for g in range(3):
                simple_load("sp", oh[:, g * 512: g * 512 + 384], ohA_d[g], ("oh", g))
            simple_load("sp", oh[:, 3 * 512: 4 * 512], ohC_d, ("oh", 3))
            for g in range(4):
                W = 384 if g < 3 else 512
                bank = g % 2
                P.op("pe", (lambda e, g=g, W=W, bank=bank: e.matmul(
                    psb[bank][0:16, 0:W], tab33[0:33, 0:16], oh[0:33, g * 512: g * 512 + W], start=True, stop=True)),
                    reads=["tab33a", "tab33b", ("oh", g)], writes=[PSR(bank)])
                P.op("act", (lambda e, g=g, W=W, bank=bank: e.activation(
                    out=vsb[:, g * 512: g * 512 + W], in_=psb[bank][0:16, 0:W], func=AF.Exp)),
                    reads=[PSR(bank)], writes=[("vsb", g)])
                dstv = vecA[g] if g < 3 else vecC
                P.dma("sp", (lambda e, s, g=g, W=W, dstv=dstv: e.dma_start(out=dstv, in_=vsb[:, g * 512: g * 512 + W]).then_inc(s, 16)),
                      reads=[("vsb", g)], writes=[("vec", g)], semkey=("vec", g))
                if g < 3:
                    srcb = bass.AP(vecA.tensor, g * 16 * 384, [[384, 16], [0, 128], [1, 384]])
                    dstb = repA[g]
                else:
                    srcb = bass.AP(vecC.tensor, 0, [[512, 16], [0, 128], [1, 512]])
                    dstb = repC
                P.dma("sp", (lambda e, s, srcb=srcb, dstb=dstb: e.dma_start(out=dstb, in_=srcb).then_inc(s, 16)),
                      reads=[("vec", g)], writes=[("rep", g)], semkey=("rep", g))
            simple_load("sp", rpb_sb[:], rpbff, "rpb_sb")
            P.op("dve", lambda e: e.memset(zB[:], 0.0), writes=["zB"])
            zview = bass.AP(zB[:].tensor, zB[:].offset + 48, [[zB[:].ap[0][0], 16], [64, 15], [1, 31]])
            P.op("act", lambda e: e.activation(out=zview, in_=rpb_sb[:].rearrange("p (a j) -> p a j", a=15), func=AF.Exp),
                 reads=["rpb_sb", "zB"], writes=["zB"])
            P.dma("sp", lambda e, s: e.dma_start(out=vecB, in_=zB[:]).then_inc(s, 16), reads=["zB"], writes=["vecB"], semkey="vecB")
            srcb = bass.AP(vecB.tensor, 0, [[1024, 16], [0, 128], [1, 1024]])
            P.dma("sp", lambda e, s: e.dma_start(out=repB, in_=srcb).then_inc(s, 16), reads=["vecB"], writes=["repB"], semkey="repB")
            P.barrier()

        def norm_tile(xt, W, sq, bank, rs, tmp, a_t, b_t, col0, out_fn, tag, nslots=8, inplace=False):
            P.op("act", lambda e: e.activation(out=sq[:, 0:8 * W], in_=xt[:, 0:8 * W], func=AF.Square),
                 reads=[("xt", tag)], writes=["nsq"])

            def mm_ss(e):
                ins = None
                for c in range(8):
                    ins = e.matmul(psb[bank][:, 0:W], ones_bf[:], sq[:, c * W:(c + 1) * W], start=(c == 0), stop=(c == 7))
                return ins
            P.op("pe", mm_ss, reads=["nsq", "ones"], writes=[PSR(bank)])
            P.op("dve", lambda e: e.tensor_scalar(out=rs[:, 0:W], in0=psb[bank][:, 0:W], scalar1=1.0 / D, scalar2=EPS,
                                                  op0=ALU.mult, op1=ALU.add),
                 reads=[PSR(bank)], writes=["nrs"])
            P.op("act", lambda e: e.activation(out=rs[:, 0:W], in_=rs[:, 0:W], func=AF.Sqrt),
                 reads=["nrs"], writes=["nrs"])
            P.op("dve", lambda e: e.reciprocal(out=rs[:, 0:W], in_=rs[:, 0:W]),
                 reads=["nrs"], writes=["nrs"])
            for c in range(8):
                if inplace:
                    P.op("dve", (lambda e, c=c: e.scalar_tensor_tensor(
                        out=xt[:, c * W:(c + 1) * W], in0=xt[:, c * W:(c + 1) * W], scalar=a_t[:, col0 + c: col0 + c + 1],
                        in1=rs[:, 0:W], op0=ALU.mult, op1=ALU.mult)),
                        reads=[("xt", tag), "nrs", "a1", "a2", "gfin"], writes=[("xt", tag)])
                    continue
                ts_ = c % nslots
                P.op("dve", (lambda e, c=c, ts_=ts_: e.scalar_tensor_tensor(
                    out=tmp[:, ts_ * W:(ts_ + 1) * W], in0=xt[:, c * W:(c + 1) * W], scalar=a_t[:, col0 + c: col0 + c + 1],
                    in1=rs[:, 0:W], op0=ALU.mult, op1=ALU.mult)),
                    reads=[("xt", tag), "nrs", "a1", "a2", "gfin"], writes=[("ntmp", ts_)])
                o_ap, o_res = out_fn(c)
                bt, bcol = b_t
                P.op("act", (lambda e, c=c, ts_=ts_, o_ap=o_ap, bt=bt, bcol=bcol: e.activation(
                    out=o_ap, in_=tmp[:, ts_ * W:(ts_ + 1) * W], func=AF.Identity, bias=bt[:, bcol + c: bcol + c + 1])),
                    reads=[("ntmp", ts_), "modsb"], writes=[o_res])

        def xview(ap, t0, W):
            return ap[:, t0:t0 + W].rearrange("(c p) t -> p c t", p=128)

        out_dma_ops = []
        def layer(l):
            kind = MIX[l]
            mj = MIXJ[l]
            x_src = xT_in if l == 0 else xs
            last_layer = (l == nlayers - 1)
            x_dst = yT if last_layer else xs
            P.new_epoch()
            with contextlib.ExitStack() as esL:
                hT = esL.enter_context(nc.sbuf_tensor(f"hT{l}", [128, 8 * S], BF16))

                with contextlib.ExitStack() as esn:
                    xts = [esn.enter_context(nc.sbuf_tensor(f"xt{l}_{i}", [128, 8 * 512], F32)) for i in range(2)]
                    sq = esn.enter_context(nc.sbuf_tensor(f"sq{l}", [128, 8 * 512], BF16))
                    rs = esn.enter_context(nc.sbuf_tensor(f"rs{l}", [128, 512], F32))
                    tmp = esn.enter_context(nc.sbuf_tensor(f"ntmp{l}", [128, 8 * 512], F32))
                    for tt in range(8):
                        slot = tt % 2
                        xt = xts[slot]
                        P.dma("sp", (lambda e, s, xt=xt, tt=tt: e.dma_start(
                            out=xt[:].rearrange("p (c t) -> p c t", c=8), in_=xview(x_src, tt * 512, 512)).then_inc(s, 16)),
                            reads=["xdram"], writes=[("xt", slot)], semkey=("xt", slot))

                        def out_fn(c, tt=tt):
                            return hT[:, c * S + tt * 512: c * S + (tt + 1) * 512], ("hT", tt)
                        norm_tile(xt, 512, sq, tt % 2, rs, tmp, a1, (modsb, l * 48 + 0), l * 8, out_fn, slot)
                    P.barrier()

                with contextlib.ExitStack() as esa:
                    def asb(name, shape, dt):
                        return esa.enter_context(nc.sbuf_tensor(f"{name}{l}", list(shape), dt))
                    w3 = [asb(f"w3_{i}_", [128, 8 * 384], BF16) for i in range(2)]
                    qT = asb("qT", [128, S], BF16)
                    kT = asb("kT", [128, S], BF16)
                    NVT = 32
                    Vp = asb("Vp", [128, NVT * 256], BF16)
                    acc = asb("acc", [128, 2 * S], F32)
                    rtmp = asb("rtmp", [128, 1024], F32)
                    NE, NPT = 3, 6
                    PW = 640 if kind == "B" else 384
                    Eb = [asb(f"E{i}_", [128, PW], BF16) for i in range(NE)]
                    PTb = [asb(f"PT{i}_", [128, PW], BF16) for i in range(NPT)]
                    OTp = asb("OTp", [128, S], BF16)
                    if kind == "A":
                        EBW = 256
                    elif kind == "C":
                        EBW = 384
                    if kind in ("A", "C"):
                        EB = [asb(f"EB{i}_", [128, 2 * EBW], BF16) for i in range(2)]
                    else:
                        Tall = asb("Tall", [128, 2 * 896], F32)
                        EBh = asb("EBh", [128, 2 * B_EBW], BF16)
                        validB = asb("validB", [128, B_EBW], F32)
                        simple_load("sp", validB[:], validB_d, "validB")
                    Vp4 = Vp[:].rearrange("p (t h c) -> p t h c", h=2, c=128)
                    P.op("pool", lambda e: e.memset(Vp[:], 1.0), writes=["Vp"])
                    es_t = es_sink if kind == "C" else es_zero
                    if kind == "A":
                        w_in_d = a_w_in[mj]
                        groups = A_GROUPS
                    elif kind == "B":
                        w_in_d = b_w_in[0]
                        groups = [(1, S)]
                    else:
                        w_in_d = c_w_in[0]
                        groups = [(1, S)]
                    witer = [0]
                    ebiter = [0]
                    ecnt = [0]
                    ptcnt = [0]
                    stcnt = [0]
                    otcnt = [0]
                    pjcnt = [0]

                    def wsrc(col0, n):
                        return w_in_d[:, col0:col0 + n].rearrange("(kc p) n -> p kc n", p=128)

                    for hp in range(8):
                        for gi, (dil, L) in enumerate(groups):
                            n_unit = hp * len(groups) + gi

                            def emit_loads(n):
                                hp_, gi_ = n // len(groups), n % len(groups)
                                wslot_ = n % 2
                                wv3 = w3[wslot_][:].rearrange("p (kc n) -> p kc n", kc=8)
                                if kind == "A":
                                    specs = [(0, 128, gi_ * 3072 + hp_ * 128), (128, 128, gi_ * 3072 + 1024 + hp_ * 128),
                                             (256, 128, gi_ * 3072 + 2048 + hp_ * 128)]
                                elif kind == "B":
                                    specs = [(0, 128, hp_ * 128), (128, 128, 1024 + hp_ * 128), (256, 128, 2048 + hp_ * 128)]
                                else:
                                    kv = hp_ // 2
                                    specs = [(0, 128, hp_ * 128), (128, 64, 1024 + kv * 64), (192, 64, 1024 + kv * 64),
                                             (256, 64, 1280 + kv * 64), (320, 64, 1280 + kv * 64)]

                                def wload(e, s, specs=specs, wv3=wv3):
                                    for (o, n_, c0) in specs:
                                        e.dma_start(out=wv3[:, :, o:o + n_], in_=wsrc(c0, n_)).then_inc(s, 16)
                                P.dma("pool", wload, writes=[("w3", wslot_)], semkey=("w3", wslot_), n=len(specs))
                                if kind in ("A", "C"):
                                    ebt_ = EB[wslot_]

                                    def ebload(e, s, ebt_=ebt_, gi_=gi_, hp_=hp_):
                                        for h2 in range(2):
                                            h = hp_ * 2 + h2
                                            if kind == "A":
                                                src = bass.AP(repA.tensor, ((gi_ * 16 + h) * 128) * 384 + 127, [[383, 128], [1, 256]])
                                            else:
                                                src = bass.AP(repC.tensor, (h * 128) * 512 + 127, [[511, 128], [1, 384]])
                                            e.dma_start(out=ebt_[:, h2 * EBW:(h2 + 1) * EBW], in_=src).then_inc(s, 16)
                                    P.dma("pool", ebload, reads=[("rep", gi_ if kind == "A" else 3)], writes=[("EB", wslot_)],
                                          semkey=("EB", wslot_), n=2)
                            if n_unit == 0:
                                emit_loads(0)
                            if n_unit + 1 < 8 * len(groups):
                                emit_loads(n_unit + 1)
                            wslot = n_unit % 2
                            wb = w3[wslot]
                            if kind in ("A", "C"):
                                ebslot = wslot
                                ebt = EB[ebslot]
                            else:
                                def tload(e, s, hp=hp):
                                    for h2 in range(2):
                                        h = hp * 2 + h2
                                        src = bass.AP(repB.tensor, (h * 128) * 1024 + 127, [[1023, 128], [1, 896]])
                                        e.dma_start(out=Tall[:, h2 * 896:(h2 + 1) * 896], in_=src).then_inc(s, 16)
                                P.dma("sp", tload, reads=["repB"], writes=["Tall"], semkey="Tall", n=2)
                                for h2 in range(2):
                                    for name, _j, U, x0 in B_CLASSES:
                                        off, nU, _ = B_OFF[name]
                                        P.op("pool", (lambda e, h2=h2, off=off, nU=nU, x0=x0: e.tensor_tensor(
                                            out=EBh[:, h2 * B_EBW + off: h2 * B_EBW + off + 128 * nU],
                                            in0=Tall[:, h2 * 896 + x0: h2 * 896 + x0 + 128 * nU],
                                            in1=validB[:, off: off + 128 * nU], op=ALU.mult)),
                                            reads=["Tall", "validB"], writes=[("EBh", h2)])

                            for which, dstT in ((0, qT), (1, kT)):
                                for tt in range(8):
                                    bank = pjcnt[0] % 2
                                    pjcnt[0] += 1

                                    def mm_p(e, which=which, tt=tt, bank=bank, wb=wb):
                                        ins = None
                                        for kc in range(8):
                                            ins = e.matmul(psb[bank][:, 0:512],
                                                           wb[:, kc * 384 + which * 128: kc * 384 + (which + 1) * 128],
                                                           hT[:, kc * S + tt * 512: kc * S + (tt + 1) * 512],
                                                           start=(kc == 0), stop=(kc == 7))
                                        return ins
                                    P.op("pe", mm_p, reads=[("w3", wslot)] + [("hT", tt)], writes=[PSR(bank)])
                                    if which == 1:
                                        P.op("act", (lambda e, dstT=dstT, tt=tt, bank=bank: e.activation(
                                            out=dstT[:, tt * 512:(tt + 1) * 512], in_=psb[bank][:, 0:512], func=AF.Copy)),
                                            reads=[PSR(bank)], writes=[("qk", which)])
                                    else:
                                        P.op("dve", (lambda e, dstT=dstT, tt=tt, bank=bank: e.tensor_copy(
                                            out=dstT[:, tt * 512:(tt + 1) * 512], in_=psb[bank][:, 0:512])),
                                            reads=[PSR(bank)], writes=[("qk", which)])
                            nT = L // 128
                            vt = 0
                            for r in range(dil):
                                for u0 in range(0, nT, 4):
                                    nb = min(4, nT - u0)
                                    bank = pjcnt[0] % 2
                                    pjcnt[0] += 1
                                    t_idx = r * nT + u0

                                    def mm_v(e, r=r, u0=u0, nb=nb, bank=bank, wb=wb, dil=dil):
                                        ins = None
                                        for i in range(nb):
                                            u = u0 + i
                                            for kc in range(8):
                                                ins = e.matmul(psb[bank][:, i * 128:(i + 1) * 128],
                                                               hT[:, sl(kc * S + r + dil * 128 * u, 128, dil)],
                                                               wb[:, kc * 384 + 256: kc * 384 + 384],
                                                               start=(kc == 0), stop=(kc == 7))
                                        return ins
                                    P.op("pe", mm_v, reads=[("w3", wslot)] + [("hT", t) for t in range(8)], writes=[PSR(bank)])
                                    vbase = Vp[:]
                                    o_ap = bass.AP(vbase.tensor, vbase.offset + t_idx * 256, [[vbase.ap[0][0], 128], [256, nb], [192, 2], [1, 64]])
                                    pb = psb[bank][:]
                                    i_ap = bass.AP(pb.tensor, pb.offset, [[pb.ap[0][0], 128], [128, nb], [64, 2], [1, 64]])
                                    P.op("dve", (lambda e, o_ap=o_ap, i_ap=i_ap: e.tensor_copy(out=o_ap, in_=i_ap)),
                                         reads=[PSR(bank)], writes=["Vp"])

                            first_group = (gi == 0)
                            for h2 in range(2):
                                prow = 64 * h2
                                if kind in ("A", "C"):
                                    Rr = 64 if kind == "A" else 128
                                    if kind == "A":
                                        blocks = []
                                        for j in range(nT + 1):
                                            lo, hi = max(0, 128 * j - 64), min(L, 128 * j + 64)
                                            tiles = [u for u in (j - 1, j) if 0 <= u < nT]
                                            blocks.append((lo, hi, tiles, min(j, nT - 1)))
                                    else:
                                        blocks = []
                                        for j in range(nT):
                                            tiles = [u for u in (j - 1, j, j + 1) if 0 <= u < nT]
                                            blocks.append((128 * j, 128 * j + 128, tiles, min(j + 1, nT - 1)))
                                    ogroups = []
                                    cur = []
                                    curw = 0
                                    for b in blocks:
                                        w = b[1] - b[0]
                                        if curw + w > 512:
                                            ogroups.append(cur)
                                            cur, curw = [], 0
                                        cur.append(b)
                                        curw += w
                                    if cur:
                                        ogroups.append(cur)
                                    blk_group = {}
                                    for gidx, gl in enumerate(ogroups):
                                        for b in gl:
                                            blk_group[b[0]] = gidx
                                    LAG = 2
                                    for r in range(dil):
                                        ptslot_of = {}
                                        qlo_of = {}
                                        grp_bank = {}
                                        done_in_grp = {}
                                        for step in range(nT + LAG):
                                            u = step
                                            if u < nT:
                                                qlo = max(0, 128 * u - Rr)
                                                qhi = min(L, 128 * u + 128 + Rr)
                                                W = qhi - qlo
                                                ebc = qlo - (128 * u - Rr)
                                                sbank = 2 + stcnt[0] % 4
                                                stcnt[0] += 1
                                                eslot = ecnt[0] % NE
                                                ecnt[0] += 1
                                                pslot = ptcnt[0] % NPT
                                                ptcnt[0] += 1
                                                ptslot_of[u] = pslot
                                                qlo_of[u] = qlo
                                                P.op("pe", (lambda e, u=u, qlo=qlo, W=W, sbank=sbank, r=r, prow=prow, dil=dil: e.matmul(
                                                    psb[sbank][:, 0:W],
                                                    kT[prow:prow + 64, sl(r + dil * 128 * u, 128, dil)],
                                                    qT[prow:prow + 64, sl(r + dil * qlo, W, dil)], start=True, stop=True)),
                                                    reads=[("qk", 0), ("qk", 1)], writes=[PSR(sbank)])
                                                P.op("act", (lambda e, W=W, sbank=sbank, eslot=eslot: e.activation(
                                                    out=Eb[eslot][:, 0:W], in_=psb[sbank][:, 0:W], func=AF.Exp, scale=0.125)),
                                                    reads=[PSR(sbank)], writes=[("E", eslot)])
                                                P.op("dve", (lambda e, W=W, eslot=eslot, pslot=pslot, ebc=ebc, h2=h2, ebt=ebt: e.tensor_tensor(
                                                    out=PTb[pslot][:, 0:W], in0=Eb[eslot][:, 0:W],
                                                    in1=ebt[:, h2 * EBW + ebc: h2 * EBW + ebc + W], op=ALU.mult)),
                                                    reads=[("E", eslot), ("EB", ebslot)], writes=[("PT", pslot)])
                                            v = step - LAG
                                            if v < 0:
                                                continue
                                            for b in blocks:
                                                lo, hi, tiles, ready = b
                                                if ready != v:
                                                    continue
                                                gidx = blk_group[lo]
                                                gl = ogroups[gidx]
                                                if gidx not in grp_bank:
                                                    grp_bank[gidx] = 6 + otcnt[0] % 2
                                                    otcnt[0] += 1
                                                    done_in_grp[gidx] = 0
                                                obank = grp_bank[gidx]
                                                base = gl[0][0]
                                                c0 = lo - base

                                                def mm_pv(e, lo=lo, hi=hi, tiles=tiles, obank=obank, c0=c0, r=r, h2=h2,
                                                          pts=dict(ptslot_of), qls=dict(qlo_of), nT=nT):
                                                    ins = None
                                                    for i, uu in enumerate(tiles):
                                                        pc = lo - qls[uu]
                                                        ins = e.matmul(psb[obank][:, c0:c0 + (hi - lo)],
                                                                       Vp4[:, r * nT + uu, h2, :],
                                                                       PTb[pts[uu]][:, pc:pc + (hi - lo)],
                                                                       start=(i == 0), stop=(i == len(tiles) - 1))
                                                    return ins
                                                P.op("pe", mm_pv, reads=[("PT", ptslot_of[uu]) for uu in tiles] + ["Vp"],
                                                     writes=[PSR(obank)])
                                                done_in_grp[gidx] += 1
                                                if done_in_grp[gidx] == len(gl):
                                                    wtot = gl[-1][1] - base
                                                    a_ap = acc[:, sl(h2 * S + r + dil * base, wtot, dil)]
                                                    if first_group:
                                                        P.op("dve", (lambda e, a_ap=a_ap, obank=obank, wtot=wtot: e.tensor_copy(
                                                            out=a_ap, in_=psb[obank][:, 0:wtot])),
                                                            reads=[PSR(obank), ("accbar", h2)], writes=[])
                                                    else:
                                                        P.op("dve", (lambda e, a_ap=a_ap, obank=obank, wtot=wtot: e.tensor_tensor(
                                                            out=a_ap, in0=psb[obank][:, 0:wtot], in1=a_ap, op=ALU.add)),
                                                            reads=[PSR(obank), ("accbar", h2)], writes=[])
                                else:
                                    LAG = 1
                                    pend = []
                                    for step in range(32 + LAG):
                                        j = step
                                        if j < 32:
                                            U = _b_tiles(j)
                                            n = len(U)
                                            off, _nU, _x0 = B_OFF[_b_class(j)]
                                            sb0 = 2 + 2 * (stcnt[0] % 2)
                                            stcnt[0] += 1
                                            eslot = ecnt[0] % NE
                                            ecnt[0] += 1
                                            pslot = ptcnt[0] % NPT
                                            ptcnt[0] += 1

                                            def mm_sb(e, j=j, U=U, sb0=sb0, prow=prow):
                                                ins = None
                                                for i, uu in enumerate(U):
                                                    bk = sb0 + (i // 4)
                                                    ins = e.matmul(psb[bk][:, (i % 4) * 128:(i % 4 + 1) * 128],
                                                                   kT[prow:prow + 64, uu * 128:(uu + 1) * 128],
                                                                   qT[prow:prow + 64, j * 128:(j + 1) * 128], start=True, stop=True)
                                                return ins
                                            P.op("pe", mm_sb, reads=[("qk", 0), ("qk", 1)], writes=[PSR(sb0), PSR(sb0 + 1)])

                                            def ex_b(e, n=n, sb0=sb0, eslot=eslot):
                                                ins = e.activation(out=Eb[eslot][:, 0:min(n, 4) * 128], in_=psb[sb0][:, 0:min(n, 4) * 128],
                                                                   func=AF.Exp, scale=0.125)
                                                if n > 4:
                                                    ins = e.activation(out=Eb[eslot][:, 512:640], in_=psb[sb0 + 1][:, 0:128],
                                                                       func=AF.Exp, scale=0.125)
                                                return ins
                                            P.op("act", ex_b, reads=[PSR(sb0), PSR(sb0 + 1)], writes=[("E", eslot)])
                                            P.op("dve", (lambda e, n=n, eslot=eslot, pslot=pslot, off=off, h2=h2: e.tensor_tensor(
                                                out=PTb[pslot][:, 0:n * 128], in0=Eb[eslot][:, 0:n * 128],
                                                in1=EBh[:, h2 * B_EBW + off: h2 * B_EBW + off + n * 128], op=ALU.mult)),
                                                reads=[("E", eslot), ("EBh", h2)], writes=[("PT", pslot)])
                                            pend.append((j, U, pslot))
                                        v = step - LAG
                                        if v < 0:
                                            continue
                                        jv, Uv, psl = pend[v]
                                        if jv % 4 == 0:
                                            obank_cur = 6 + otcnt[0] % 2
                                            otcnt[0] += 1
                                        obank = obank_cur
                                        c0 = (jv % 4) * 128

                                        def mm_pvb(e, Uv=Uv, psl=psl, obank=obank, c0=c0, h2=h2):
                                            ins = None
                                            for i, uu in enumerate(Uv):
                                                ins = e.matmul(psb[obank][:, c0:c0 + 128], Vp4[:, uu, h2, :],
                                                               PTb[psl][:, i * 128:(i + 1) * 128],
                                                               start=(i == 0), stop=(i == len(Uv) - 1))
                                            return ins
                                        P.op("pe", mm_pvb, reads=[("PT", psl), "Vp"], writes=[PSR(obank)])
                                        if jv % 4 == 3:
                                            a_ap = acc[:, h2 * S + (jv - 3) * 128: h2 * S + (jv + 1) * 128]
                                            P.op("dve", (lambda e, a_ap=a_ap, obank=obank: e.tensor_copy(
                                                out=a_ap, in_=psb[obank][:, 0:512])),
                                                reads=[PSR(obank), ("accbar", h2)], writes=[])
                            for h2 in range(2):
                                P.op("dve", (lambda e: e.memset(dummy[:, 0:1], 0.0)), reads=[], writes=[("accbar", h2)])

                        for ck in range(4):
                            t0 = ck * 1024
                            h0 = hp * 2
                            P.op("dve", (lambda e, t0=t0, h0=h0: e.tensor_scalar(
                                out=rtmp[0:64, :], in0=acc[64:128, t0:t0 + 1024], scalar1=es_t[64:128, h0:h0 + 1], scalar2=None,
                                op0=ALU.add)),
                                reads=[("accbar", 0), "es_sink", "es_zero"], writes=["rtmp"])
                            P.op("dve", (lambda e: e.reciprocal(out=rtmp[0:64, :], in_=rtmp[0:64, :])),
                                reads=["rtmp"], writes=["rtmp"])
                            P.op("dve", (lambda e, t0=t0, h0=h0: e.tensor_scalar(
                                out=rtmp[64:128, :], in0=acc[0:64, S + t0:S + t0 + 1024], scalar1=es_t[0:64, h0 + 1:h0 + 2], scalar2=None,
                                op0=ALU.add)),
                                reads=[("accbar", 1), "es_sink", "es_zero"], writes=["rtmp2"])
                            P.op("dve", (lambda e: e.reciprocal(out=rtmp[64:128, :], in_=rtmp[64:128, :])),
                                reads=["rtmp2"], writes=["rtmp2"])
                            P.op("pool", (lambda e, t0=t0: e.tensor_tensor(
                                out=OTp[0:64, t0:t0 + 1024], in0=acc[0:64, t0:t0 + 1024], in1=rtmp[0:64, :], op=ALU.mult)),
                                reads=["rtmp", ("accbar", 0)], writes=["OTp"])
                            P.op("pool", (lambda e, t0=t0: e.tensor_tensor(
                                out=OTp[64:128, t0:t0 + 1024], in0=acc[64:128, S + t0:S + t0 + 1024], in1=rtmp[64:128, :], op=ALU.mult)),
                                reads=["rtmp2", ("accbar", 1)], writes=["OTp"])
                        P.dma("sp", (lambda e, s, hp=hp: e.dma_start(out=OTd[hp * 128:(hp + 1) * 128, :], in_=OTp[:]).then_inc(s, 16)),
                              reads=["OTp"], writes=["OTd"], semkey="OTp")
                        for h2 in range(2):
                            P.op("dve", (lambda e: e.memset(dummy[:, 0:1], 0.0)), reads=[], writes=[("accbar", h2)])
                    P.barrier()
            P.barrier()

            with contextlib.ExitStack() as esf:
                def fsb(name, shape, dt):
                    return esf.enter_context(nc.sbuf_tensor(f"{name}{l}", list(shape), dt))
                TW = 256
                wo = fsb("wo", [128, 8 * 1024], BF16)
                win = fsb("win", [128, 8 * 2 * DFF], BF16)
                wout = fsb("wout", [128, NFC * 1024], BF16)
                xts = [fsb(f"fx{i}_", [128, 8 * TW], F32) for i in range(2)]
                ots = [fsb(f"fo{i}_", [128, 8 * TW], BF16) for i in range(2)]
                sq = fsb("fsq", [128, 8 * TW], BF16)
                rs = fsb("frs", [128, TW], F32)
                tmp = fsb("ftmp", [128, 3 * TW], F32)
                h2Ts = [fsb(f"fh2_{i}_", [128, 8 * TW], BF16) for i in range(2)]
                aT = fsb("faT", [128, NFC * TW], BF16)
                sg = [fsb(f"fsg{i}_", [128, TW], BF16) for i in range(2)]
                wo_d = {"A": a_w_out, "B": b_w_out, "C": c_w_out}[kind][mj]
                wo3 = wo[:].rearrange("p (kc n) -> p kc n", kc=8)
                for pc in range(2):
                    P.dma("pool", (lambda e, s, pc=pc: e.dma_start(
                        out=wo3[:, :, pc * 512:(pc + 1) * 512],
                        in_=wo_d[:, pc * 512:(pc + 1) * 512].rearrange("(kc p) n -> p kc n", p=128)).then_inc(s, 16)),
                        writes=[("wo", pc)], semkey=("wo", pc))
                win3 = win[:].rearrange("p (kc n) -> p kc n", kc=8)
                for pc in range(11):
                    P.dma("pool", (lambda e, s, pc=pc: e.dma_start(
                        out=win3[:, :, pc * 512:(pc + 1) * 512],
                        in_=ffn_w_in[l, :, pc * 512:(pc + 1) * 512].rearrange("(kc p) n -> p kc n", p=128)).then_inc(s, 16)),
                        writes=[("win", pc)], semkey=("win", pc))
                wout3 = wout[:].rearrange("p (fc n) -> p fc n", fc=NFC)
                for pc in range(11):
                    P.dma("pool", (lambda e, s, pc=pc: e.dma_start(
                        out=wout3[:, 2 * pc:2 * pc + 2, :],
                        in_=ffn_w_out[l, pc * 256:(pc + 1) * 256, :].rearrange("(fc p) n -> p fc n", p=128)).then_inc(s, 16)),
                        writes=[("wout", pc)], semkey=("wout", pc))
                mcol = l * 48
                NTT = S // TW
                bankc = [0]

                def nb():
                    b = bankc[0] % 8
                    bankc[0] += 1
                    return b
                def stageA(tt):
                    slot = tt % 2
                    xt = xts[slot]
                    ot = ots[slot]
                    t0 = tt * TW
                    P.dma("sp", (lambda e, s, xt=xt, t0=t0: e.dma_start(
                        out=xt[:].rearrange("p (c t) -> p c t", c=8), in_=xview(x_src, t0, TW)).then_inc(s, 16)),
                        reads=["xdram"], writes=[("xt", slot)], semkey=("fx", slot))
                    P.dma("sp", (lambda e, s, ot=ot, t0=t0: e.dma_start(
                        out=ot[:].rearrange("p (c t) -> p c t", c=8), in_=xview(OTd, t0, TW)).then_inc(s, 16)),
                        reads=["OTd"], writes=[("ot", slot)], semkey=("fo", slot))

                def stageB(tt):
                    slot = tt % 2
                    xt = xts[slot]
                    ot = ots[slot]
                    h2T = h2Ts[slot]
                    for m in range(8):
                        bank = nb()

                        def mm_o(e, m=m, bank=bank, ot=ot):
                            ins = None
                            for kc in range(8):
                                ins = e.matmul(psb[bank][:, 0:TW], wo[:, kc * 1024 + m * 128: kc * 1024 + (m + 1) * 128],
                                               ot[:, kc * TW:(kc + 1) * TW], start=(kc == 0), stop=(kc == 7))
                            return ins
                        P.op("pe", mm_o, reads=[("wo", m // 4), ("ot", slot)], writes=[PSR(bank)])
                        P.op("dve", (lambda e, m=m, bank=bank, xt=xt: e.scalar_tensor_tensor(
                            out=xt[:, m * TW:(m + 1) * TW], in0=psb[bank][:, 0:TW], scalar=modsb[:, mcol + 16 + m: mcol + 17 + m],
                            in1=xt[:, m * TW:(m + 1) * TW], op0=ALU.mult, op1=ALU.add)),
                            reads=[PSR(bank), ("xt", slot), "modsb"], writes=[("xt", slot)])

                    def out_fn2(c, h2T=h2T, slot=slot):
                        return h2T[:, c * TW:(c + 1) * TW], ("h2T", slot)
                    norm_tile(xt, TW, sq, nb(), rs, tmp, a2, (modsb, mcol + 24), l * 8, out_fn2, slot, nslots=3)

                def stageC(tt):
                    slot = tt % 2
                    h2T = h2Ts[slot]
                    for f in range(NFC):
                        bg = nb()
                        bu = nb()

                        def mm_gu(e, f=f, bg=bg, bu=bu, h2T=h2T):
                            ins = None
                            for kc in range(8):
                                ins = e.matmul(psb[bg][:, 0:TW], win[:, kc * 2 * DFF + f * 128: kc * 2 * DFF + (f + 1) * 128],
                                               h2T[:, kc * TW:(kc + 1) * TW], start=(kc == 0), stop=(kc == 7))
                            for kc in range(8):
                                ins = e.matmul(psb[bu][:, 0:TW], win[:, kc * 2 * DFF + DFF + f * 128: kc * 2 * DFF + DFF + (f + 1) * 128],
                                               h2T[:, kc * TW:(kc + 1) * TW], start=(kc == 0), stop=(kc == 7))
                            return ins
                        P.op("pe", mm_gu, reads=[("win", (f * 128) // 512), ("win", (DFF + f * 128) // 512), ("h2T", slot)],
                             writes=[PSR(bg), PSR(bu)])
                        sslot = f % 2
                        P.op("act", (lambda e, bg=bg, sslot=sslot: e.activation(out=sg[sslot][:], in_=psb[bg][:, 0:TW], func=AF.Silu)),
                             reads=[PSR(bg)], writes=[("sg", sslot)])
                        P.op("dve", (lambda e, f=f, bu=bu, sslot=sslot: e.tensor_tensor(
                            out=aT[:, f * TW:(f + 1) * TW], in0=psb[bu][:, 0:TW], in1=sg[sslot][:], op=ALU.mult)),
                            reads=[PSR(bu), ("sg", sslot)], writes=[("aT", f)])

                def stageD(tt):
                    slot = tt % 2
                    xt = xts[slot]
                    t0 = tt * TW
                    for m in range(8):
                        bank = nb()

                        def mm_f(e, m=m, bank=bank):
                            ins = None
                            for f in range(NFC):
                                ins = e.matmul(psb[bank][:, 0:TW], wout[:, f * 1024 + m * 128: f * 1024 + (m + 1) * 128],
                                               aT[:, f * TW:(f + 1) * TW], start=(f == 0), stop=(f == NFC - 1))
                            return ins
                        P.op("pe", mm_f, reads=[("wout", pc) for pc in range(11)] + [("aT", f) for f in range(NFC)],
                             writes=[PSR(bank)])
                        P.op("dve", (lambda e, m=m, bank=bank, xt=xt: e.scalar_tensor_tensor(
                            out=xt[:, m * TW:(m + 1) * TW], in0=psb[bank][:, 0:TW], scalar=modsb[:, mcol + 40 + m: mcol + 41 + m],
                            in1=xt[:, m * TW:(m + 1) * TW], op0=ALU.mult, op1=ALU.add)),
                            reads=[PSR(bank), ("xt", slot), "modsb"], writes=[("xt", slot)])
                    if last_layer:
                        norm_tile(xt, TW, sq, nb(), rs, tmp, gfin_sb, None, 0, None, slot, nslots=3, inplace=True)
                    o = P.dma("sp", (lambda e, s, xt=xt, t0=t0: e.dma_start(
                        out=xview(x_dst, t0, TW), in_=xt[:].rearrange("p (c t) -> p c t", c=8)).then_inc(s, 16)),
                        reads=[("xt", slot)], writes=["xdram_w"], semkey=("fxs", slot))
                    if last_layer:
                        out_dma_ops.append(o)

                stageA(0)
                stageA(1)
                stageB(0)
                for tt in range(NTT):
                    stageC(tt)
                    if tt + 1 < NTT:
                        stageB(tt + 1)
                    stageD(tt)
                    if tt + 2 < NTT:
                        stageA(tt + 2)
                P.barrier()
        for l in range(nlayers):
            layer(l)
        P.emit(final_waits=out_dma_ops[-2:])
    return nc


_PROGRAM_CACHE = {}


def _layout_inputs(inputs):
    f = np.float32
    x = np.asarray(inputs["x"], f)
    c = np.asarray(inputs["c"], f)
    ohA, ohC, validB = _static_tables()

    def l128(v):
        return np.ascontiguousarray(np.asarray(v, f).reshape(-1, 128).T)

    shared = {
        "rel_bias": np.ascontiguousarray(np.asarray(inputs["rel_bias"], f)),
        "ada_w": np.ascontiguousarray(np.asarray(inputs["ada_w"], f)),
        "ada_bl": np.ascontiguousarray(np.asarray(inputs["ada_b"], f).reshape(DEPTH, 48, 128).transpose(0, 2, 1)),
        "gmix": l128(np.asarray(inputs["norm_mix"], f)),
        "gffn": l128(np.asarray(inputs["norm_ffn"], f)),
        "gfin": l128(np.asarray(inputs["norm_final"], f)),
        "a_w_in": np.ascontiguousarray(np.asarray(inputs["a_w_in"], f)),
        "a_w_out": np.ascontiguousarray(np.asarray(inputs["a_w_out"], f)),
        "b_w_in": np.ascontiguousarray(np.asarray(inputs["b_w_in"], f)),
        "b_w_out": np.ascontiguousarray(np.asarray(inputs["b_w_out"], f)),
        "rpbff": np.ascontiguousarray(np.asarray(inputs["b_rpb"], f)[0][:, ::-1, ::-1].reshape(16, 465)),
        "c_w_in": np.ascontiguousarray(np.asarray(inputs["c_w_in"], f)),
        "c_w_out": np.ascontiguousarray(np.asarray(inputs["c_w_out"], f)),
        "c_sink": np.ascontiguousarray(np.asarray(inputs["c_sink"], f)),
        "ffn_w_in": np.ascontiguousarray(np.asarray(inputs["ffn_w_in"], f)),
        "ffn_w_out": np.ascontiguousarray(np.asarray(inputs["ffn_w_out"], f)),
        "ohA": ohA, "ohC": ohC, "validB": validB,
    }
    in_maps = []
    for b in range(8):
        m = dict(shared)
        m["xT"] = np.ascontiguousarray(x[b].T)
        m["cb"] = l128(c[b])
        in_maps.append(m)
    return in_maps


def kernel(**inputs):
    in_maps = _layout_inputs(inputs)
    nc = build_program(DEPTH)
    res = run_bass_kernel_spmd(nc, in_maps, core_ids=list(range(8)))
    out = np.stack([np.ascontiguousarray(np.asarray(r["yT"]).T) for r in res.results], axis=0)
    return out.astype(np.float32)
```

```python
import math
import contextlib
import numpy as np
import concourse.bass as bass
import concourse.mybir as mybir
from concourse.bass_utils import run_bass_kernel_spmd

F32 = mybir.dt.float32
BF16 = mybir.dt.bfloat16
ALU = mybir.AluOpType
AF = mybir.ActivationFunctionType

D = 1024
S = 4096
DEPTH = 4
DFF = 2816
NFC = DFF // 128
EPS = 1e-6
NEG = -30000.0
ENGS = ("pe", "act", "dve", "pool", "sp")


class _Op:
    __slots__ = ("eng", "fn", "waits", "sig_key", "sig_val", "signaled", "idx", "epoch", "is_dma")


class Prog:
    def __init__(self, nc):
        self.nc = nc
        self.ops = {e: [] for e in ENGS}
        self.last_w = {}
        self.readers = {}
        self.epoch = 0
        self.dma_cnt = {}
        self.dma_last = {}
        self.waited = {}

    def new_epoch(self):
        self.epoch += 1

    def _stream(self, op):
        return ("dma", op.sig_key) if op.is_dma else ("eng", op.eng)

    def _pos(self, d):
        return d.sig_val if d.is_dma else (d.epoch, d.idx)

    def _finish(self, eng, op, deps):
        waits = {}
        for d, raw in deps:
            if (not d.is_dma) and d.eng == eng:
                if eng in ("pe", "sp") or not raw:
                    continue
            stt = self._stream(d)
            pos = self._pos(d)
            prev = waits.get(stt)
            if prev is None or pos > prev[0]:
                waits[stt] = (pos, d)
        final = []
        for stt, (pos, d) in waits.items():
            k = (eng, stt)
            have = self.waited.get(k)
            if have is not None and have >= pos:
                continue
            self.waited[k] = pos
            d.signaled = True
            final.append(d)
        op.waits = final
        self.ops[eng].append(op)

    def _add(self, eng, fn, reads, writes, is_dma=False, semkey=None, n_dma=1):
        op = _Op()
        op.eng = eng
        op.fn = fn
        op.is_dma = is_dma
        op.signaled = is_dma
        op.epoch = self.epoch
        op.idx = len(self.ops[eng])
        op.sig_key = None
        op.sig_val = None
        if is_dma:
            c = self.dma_cnt.get(semkey, 0) + 16 * n_dma
            self.dma_cnt[semkey] = c
            op.sig_key = semkey
            op.sig_val = c
            self.dma_last[semkey] = op
        deps = []
        for r in reads:
            w = self.last_w.get(r)
            if w is not None:
                deps.append((w, True))
        for w_ in writes:
            w = self.last_w.get(w_)
            if w is not None:
                deps.append((w, False))
            for rd in self.readers.get(w_, {}).values():
                deps.append((rd, False))
        self._finish(eng, op, deps)
        for r in reads:
            self.readers.setdefault(r, {})[self._stream(op)] = op
        for w_ in writes:
            self.last_w[w_] = op
            self.readers[w_] = {}
        return op

    def op(self, eng, fn, reads=(), writes=()):
        return self._add(eng, fn, tuple(reads), tuple(writes))

    def dma(self, q, fn, reads=(), writes=(), semkey=None, n=1):
        return self._add(q, fn, tuple(reads), tuple(writes), is_dma=True, semkey=semkey, n_dma=n)

    def barrier(self):
        lasts = [self.ops[e][-1] for e in ENGS if self.ops[e] and not self.ops[e][-1].is_dma]
        lasts = []
        for e in ENGS:
            for o in reversed(self.ops[e]):
                if not o.is_dma and o.fn is not None:
                    lasts.append(o)
                    break
        lasts += list(self.dma_last.values())
        for e in ENGS:
            op = _Op()
            op.eng = e
            op.fn = None
            op.is_dma = False
            op.signaled = False
            op.epoch = self.epoch
            op.idx = len(self.ops[e])
            op.sig_key = None
            op.sig_val = None
            self._finish(e, op, [(d, True) for d in lasts if d.is_dma or d.eng != e])

    def emit(self, final_waits=()):
        nc = self.nc
        with contextlib.ExitStack() as st:
            esem = {}
            for e in ENGS:
                used = sorted({o.epoch for o in self.ops[e] if o.signaled and not o.is_dma})
                for ep in used:
                    esem[(e, ep)] = st.enter_context(nc.semaphore(f"s_{e}_{ep}"))
            dsem = {}
            for k in self.dma_cnt:
                dsem[k] = st.enter_context(nc.semaphore(f"d_{len(dsem)}"))
            for e in ENGS:
                cnt = {}
                for o in self.ops[e]:
                    if o.is_dma or not o.signaled:
                        continue
                    c = cnt.get(o.epoch, 0) + 1
                    cnt[o.epoch] = c
                    o.sig_key = (e, o.epoch)
                    o.sig_val = c
            block = st.enter_context(nc.Block())

            def run(e, engine):
                for o in self.ops[e]:
                    for d in o.waits:
                        if d.is_dma:
                            engine.wait_ge(dsem[d.sig_key], d.sig_val)
                        else:
                            engine.wait_ge(esem[d.sig_key], d.sig_val)
                    if o.fn is None:
                        continue
                    if o.is_dma:
                        o.fn(engine, dsem[o.sig_key])
                    else:
                        ins = o.fn(engine)
                        if o.signaled:
                            ins.then_inc(esem[o.sig_key], 1)
                if e == "sp":
                    for d in final_waits:
                        engine.wait_ge(dsem[d.sig_key], d.sig_val)

            @block.tensor
            def _(eng):
                run("pe", eng)

            @block.scalar
            def _(eng):
                run("act", eng)

            @block.vector
            def _(eng):
                run("dve", eng)

            @block.gpsimd
            def _(eng):
                run("pool", eng)

            @block.sync
            def _(eng):
                run("sp", eng)


def sl(start, count, step=1):
    return slice(start, start + step * (count - 1) + 1, step)


def _t5_bucket(rel):
    half, max_exact = 16, 8
    ret = np.where(rel > 0, half, 0)
    n = np.abs(rel)
    nf = np.maximum(n, 1).astype(np.float32)
    large = max_exact + (np.log(nf / np.float32(max_exact)) / np.float32(math.log(1024 / max_exact))
                         * np.float32(half - max_exact)).astype(np.int32)
    large = np.minimum(large, half - 1)
    return ret + np.where(n < max_exact, n, large)


def _onehot_band(length, pad, center, radius, dil):
    oh = np.zeros((33, pad), np.float32)
    i = np.arange(pad)
    rel = center - i
    valid = (np.abs(rel) <= radius) & (i < length)
    b = _t5_bucket(rel * dil)
    for k in range(pad):
        if valid[k]:
            oh[b[k], k] = 1.0
        else:
            oh[32, k] = 1.0
    return oh


B_CLASSES = [
    ("int", 2, [4, 3, 2, 1, 0], 128),
    ("e0", 0, [3, 2, 1, 0], 0),
    ("e1", 1, [3, 2, 1, 0], 128),
    ("e30", 30, [31, 30, 29, 28], 256),
    ("e31", 31, [31, 30, 29, 28], 384),
]
B_OFF = {}
_o = 0
for _n, _j, _U, _x in B_CLASSES:
    B_OFF[_n] = (_o, len(_U), _x)
    _o += 128 * len(_U)
B_EBW = _o


def _b_class(j):
    return {0: "e0", 1: "e1", 30: "e30", 31: "e31"}.get(j, "int")


def _b_tiles(j):
    if j <= 1:
        return [3, 2, 1, 0]
    if j >= 30:
        return [31, 30, 29, 28]
    return [j + 2, j + 1, j, j - 1, j - 2]


def _valid_b():
    out = np.zeros((128, B_EBW), np.float32)
    kk = np.arange(128)[:, None]
    qq = np.arange(128)[None, :]
    for name, j, U, _x in B_CLASSES:
        off = B_OFF[name][0]
        for i, u in enumerate(U):
            kt = 128 * u + kk
            qt = 128 * j + qq
            kr, kc = kt // 64, kt % 64
            r, c = qt // 64, qt % 64
            rs = np.clip(r - 4, 0, 56)
            cs = np.clip(c - 8, 0, 48)
            v = (kr >= rs) & (kr < rs + 8) & (kc >= cs) & (kc < cs + 16)
            out[:, off + 128 * i: off + 128 * (i + 1)] = v.astype(np.float32)
    return out


def _static_tables():
    ohA = np.stack([_onehot_band(383, 384, 191, 64, d) for d in (1, 4, 16)])
    ohC = _onehot_band(511, 512, 255, 128, 1)
    return ohA, ohC, _valid_b()


MIX = ["A", "B", "C", "A"]
MIXJ = [0, 0, 0, 1]
A_GROUPS = [(1, 4096), (4, 1024), (16, 256)]


def build_program(nlayers=DEPTH):
    nc = bass.Bass("TRN2", target_bir_lowering=False)

    def din(name, shape, dt=F32):
        return nc.dram_tensor(name, list(shape), dt, kind="ExternalInput").ap()

    xT_in = din("xT", [D, S])
    cb_in = din("cb", [128, 8])
    relb = din("rel_bias", [32, 16])
    ada_w = din("ada_w", [DEPTH, D, 6 * D])
    ada_b = din("ada_bl", [DEPTH, 128, 48])
    gmix = din("gmix", [128, DEPTH * 8])
    gffn = din("gffn", [128, DEPTH * 8])
    gfin = din("gfin", [128, 8])
    a_w_in = din("a_w_in", [2, D, 9216])
    a_w_out = din("a_w_out", [2, D, D])
    b_w_in = din("b_w_in", [1, D, 3072])
    b_w_out = din("b_w_out", [1, D, D])
    rpbff = din("rpbff", [16, 15 * 31])
    c_w_in = din("c_w_in", [1, D, 1536])
    c_w_out = din("c_w_out", [1, D, D])
    c_sink = din("c_sink", [1, 16])
    ffn_w_in = din("ffn_w_in", [DEPTH, D, 2 * DFF])
    ffn_w_out = din("ffn_w_out", [DEPTH, DFF, D])
    ohA_d = din("ohA", [3, 33, 384])
    ohC_d = din("ohC", [33, 512])
    validB_d = din("validB", [128, B_EBW])
    yT = nc.dram_tensor("yT", [D, S], F32, kind="ExternalOutput").ap()

    xs = nc.dram_tensor("xs", [D, S], F32).ap()
    OTd = nc.dram_tensor("OTd", [D, S], BF16).ap()
    vecA = nc.dram_tensor("vecA", [3, 16, 384], F32).ap()
    vecC = nc.dram_tensor("vecC", [16, 512], F32).ap()
    vecB = nc.dram_tensor("vecB", [16, 1024], F32).ap()
    repA = nc.dram_tensor("repA", [3, 16, 128, 384], F32).ap()
    repC = nc.dram_tensor("repC", [16, 128, 512], F32).ap()
    repB = nc.dram_tensor("repB", [16, 128, 1024], F32).ap()

    P = Prog(nc)
    ES = contextlib.ExitStack()
    with ES:
        def sb(name, shape, dt):
            return ES.enter_context(nc.sbuf_tensor("t_" + name, list(shape), dt))

        psb = [ES.enter_context(nc.psum_tensor(f"psb{i}", [128, 512], F32)) for i in range(8)]

        def PSR(i):
            return ("ps", i)

        ones_bf = sb("ones_bf", [128, 128], BF16)
        cb = sb("cb", [128, 8], F32)
        condb = sb("condb", [128, 8], BF16)
        modsb = sb("modsb", [128, DEPTH * 48], F32)
        adab = sb("adab", [128, DEPTH * 48], F32)
        gmix_sb = sb("gmix_sb", [128, DEPTH * 8], F32)
        gffn_sb = sb("gffn_sb", [128, DEPTH * 8], F32)
        gfin_sb = sb("gfin_sb", [128, 8], F32)
        a1 = sb("a1", [128, DEPTH * 8], F32)
        a2 = sb("a2", [128, DEPTH * 8], F32)
        es_sink = sb("es_sink", [128, 16], F32)
        es_zero = sb("es_zero", [128, 16], F32)
        dummy = sb("dummy", [128, 8], F32)

        P.op("dve", lambda e: e.memset(ones_bf[:], 1.0), writes=["ones"])
        P.op("dve", lambda e: e.memset(es_zero[:], 0.0), writes=["es_zero"])

        def simple_load(q, dst, src, res):
            P.dma(q, lambda e, s: e.dma_start(out=dst, in_=src).then_inc(s, 16), writes=[res], semkey=res)

        simple_load("sp", cb[:], cb_in, "cb")
        simple_load("sp", adab[:].rearrange("p (l j) -> p l j", l=DEPTH), ada_b.rearrange("l p j -> p l j"), "adab")
        simple_load("sp", gmix_sb[:], gmix, "gmix")
        simple_load("sp", gffn_sb[:], gffn, "gffn")
        simple_load("sp", gfin_sb[:], gfin, "gfin")
        P.dma("sp", lambda e, s: e.dma_start(out=es_sink[:], in_=bass.AP(c_sink.tensor, 0, [[0, 128], [1, 16]])).then_inc(s, 16),
              writes=["es_sink"], semkey="es_sink")
        P.op("act", lambda e: e.activation(out=condb[:], in_=cb[:], func=AF.Silu), reads=["cb"], writes=["condb"])
        P.op("act", lambda e: e.activation(out=es_sink[:], in_=es_sink[:], func=AF.Exp), reads=["es_sink"], writes=["es_sink"])

        with contextlib.ExitStack() as es1:
            adw = [es1.enter_context(nc.sbuf_tensor(f"adw{i}", [128, 8 * 1024], BF16)) for i in range(2)]
            it = 0
            for l in range(nlayers):
                for piece in range(6):
                    slot = it % 2
                    it += 1
                    buf = adw[slot]
                    src = ada_w[l, :, piece * 1024:(piece + 1) * 1024].rearrange("(kc p) n -> p kc n", p=128)
                    dst = buf[:].rearrange("p (kc n) -> p kc n", kc=8)
                    P.dma("pool", (lambda e, s, dst=dst, src=src: e.dma_start(out=dst, in_=src).then_inc(s, 16)),
                          writes=[("adw", slot)], semkey=("adw", slot))
                    bank = (l * 6 + piece) % 2

                    def mm_mod(e, buf=buf, piece=piece, bank=bank):
                        ins = None
                        for fc in range(8):
                            for kc in range(8):
                                ins = e.matmul(psb[bank][:, fc:fc + 1],
                                               buf[:, kc * 1024 + fc * 128: kc * 1024 + (fc + 1) * 128],
                                               condb[:, kc:kc + 1], start=(kc == 0), stop=(kc == 7))
                        return ins
                    P.op("pe", mm_mod, reads=[("adw", slot), "condb"], writes=[PSR(bank)])
                    col = l * 48 + piece * 8
                    P.op("dve", (lambda e, bank=bank, col=col: e.tensor_tensor(
                        out=modsb[:, col:col + 8], in0=psb[bank][:, 0:8], in1=adab[:, col:col + 8], op=ALU.add)),
                        reads=[PSR(bank), "adab"], writes=["modsb"])
            for l in range(nlayers):
                P.op("dve", (lambda e, l=l: e.scalar_tensor_tensor(
                    out=a1[:, l * 8:(l + 1) * 8], in0=modsb[:, l * 48 + 8: l * 48 + 16], scalar=1.0,
                    in1=gmix_sb[:, l * 8:(l + 1) * 8], op0=ALU.add, op1=ALU.mult)),
                    reads=["modsb", "gmix"], writes=["a1"])
                P.op("dve", (lambda e, l=l: e.scalar_tensor_tensor(
                    out=a2[:, l * 8:(l + 1) * 8], in0=modsb[:, l * 48 + 32: l * 48 + 40], scalar=1.0,
                    in1=gffn_sb[:, l * 8:(l + 1) * 8], op0=ALU.add, op1=ALU.mult)),
                    reads=["modsb", "gffn"], writes=["a2"])
            P.barrier()

        kinds_used = set(MIX[:nlayers])
        with contextlib.ExitStack() as es2:
            tab33 = es2.enter_context(nc.sbuf_tensor("tab33", [33, 16], F32))
            oh = es2.enter_context(nc.sbuf_tensor("oh", [33, 4 * 512], F32))
            vsb = es2.enter_context(nc.sbuf_tensor("vsb", [16, 4 * 512], F32))
            rpb_sb = es2.enter_context(nc.sbuf_tensor("rpb_sb", [16, 465], F32))
            zB = es2.enter_context(nc.sbuf_tensor("zB", [16, 1024], F32))
            P.op("dve", lambda e: e.memset(tab33[32:33, :], NEG), writes=["tab33b"])
            simple_load("sp", tab33[0:32, :], relb, "tab33a")
            for g in range(3):
                simple_load("sp", oh[:, g * 512: g * 512 + 384], ohA_d[g], ("oh", g))
            simple_load("sp", oh[:, 3 * 512: 4 * 512], ohC_d, ("oh", 3))
            for g in range(4):
                W = 384 if g < 3 else 512
                bank = g % 2
                P.op("pe", (lambda e, g=g, W=W, bank=bank: e.matmul(
                    psb[bank][0:16, 0:W], tab33[0:33, 0:16], oh[0:33, g * 512: g * 512 + W], start=True, stop=True)),
                    reads=["tab33a", "tab33b", ("oh", g)], writes=[PSR(bank)])
                P.op("act", (lambda e, g=g, W=W, bank=bank: e.activation(
                    out=vsb[:, g * 512: g * 512 + W], in_=psb[bank][0:16, 0:W], func=AF.Exp)),
                    reads=[PSR(bank)], writes=[("vsb", g)])
                dstv = vecA[g] if g < 3 else vecC
                P.dma("sp", (lambda e, s, g=g, W=W, dstv=dstv: e.dma_start(out=dstv, in_=vsb[:, g * 512: g * 512 + W]).then_inc(s, 16)),
                      reads=[("vsb", g)], writes=[("vec", g)], semkey=("vec", g))
                if g < 3:
                    srcb = bass.AP(vecA.tensor, g * 16 * 384, [[384, 16], [0, 128], [1, 384]])
                    dstb = repA[g]
                else:
                    srcb = bass.AP(vecC.tensor, 0, [[512, 16], [0, 128], [1, 512]])
                    dstb = repC
                P.dma("sp", (lambda e, s, srcb=srcb, dstb=dstb: e.dma_start(out=dstb, in_=srcb).then_inc(s, 16)),
                      reads=[("vec", g)], writes=[("rep", g)], semkey=("rep", g))
            simple_load("sp", rpb_sb[:], rpbff, "rpb_sb")
            P.op("dve", lambda e: e.memset(zB[:], 0.0), writes=["zB"])
            zview = bass.AP(zB[:].tensor, zB[:].offset + 48, [[zB[:].ap[0][0], 16], [64, 15], [1, 31]])
            P.op("act", lambda e: e.activation(out=zview, in_=rpb_sb[:].rearrange("p (a j) -> p a j", a=15), func=AF.Exp),
                 reads=["rpb_sb", "zB"], writes=["zB"])
            P.dma("sp", lambda e, s: e.dma_start(out=vecB, in_=zB[:]).then_inc(s, 16), reads=["zB"], writes=["vecB"], semkey="vecB")
            srcb = bass.AP(vecB.tensor, 0, [[1024, 16], [0, 128], [1, 1024]])
            P.dma("sp", lambda e, s: e.dma_start(out=repB, in_=srcb).then_inc(s, 16), reads=["vecB"], writes=["repB"], semkey="repB")
            P.barrier()

        def norm_tile(xt, W, sq, bank, rs, tmp, a_t, b_t, col0, out_fn, tag, nslots=8, inplace=False):
            P.op("act", lambda e: e.activation(out=sq[:, 0:8 * W], in_=xt[:, 0:8 * W], func=AF.Square),
                 reads=[("xt", tag)], writes=["nsq"])

            def mm_ss(e):
                ins = None
                for c in range(8):
                    ins = e.matmul(psb[bank][:, 0:W], ones_bf[:], sq[:, c * W:(c + 1) * W], start=(c == 0), stop=(c == 7))
                return ins
            P.op("pe", mm_ss, reads=["nsq", "ones"], writes=[PSR(bank)])
            P.op("dve", lambda e: e.tensor_scalar(out=rs[:, 0:W], in0=psb[bank][:, 0:W], scalar1=1.0 / D, scalar2=EPS,
                                                  op0=ALU.mult, op1=ALU.add),
                 reads=[PSR(bank)], writes=["nrs"])
            P.op("act", lambda e: e.activation(out=rs[:, 0:W], in_=rs[:, 0:W], func=AF.Sqrt),
                 reads=["nrs"], writes=["nrs"])
            P.op("dve", lambda e: e.reciprocal(out=rs[:, 0:W], in_=rs[:, 0:W]),
                 reads=["nrs"], writes=["nrs"])
            for c in range(8):
                if inplace:
                    P.op("dve", (lambda e, c=c: e.scalar_tensor_tensor(
                        out=xt[:, c * W:(c + 1) * W], in0=xt[:, c * W:(c + 1) * W], scalar=a_t[:, col0 + c: col0 + c + 1],
                        in1=rs[:, 0:W], op0=ALU.mult, op1=ALU.mult)),
                        reads=[("xt", tag), "nrs", "a1", "a2", "gfin"], writes=[("xt", tag)])
                    continue
                ts_ = c % nslots
                P.op("dve", (lambda e, c=c, ts_=ts_: e.scalar_tensor_tensor(
                    out=tmp[:, ts_ * W:(ts_ + 1) * W], in0=xt[:, c * W:(c + 1) * W], scalar=a_t[:, col0 + c: col0 + c + 1],
                    in1=rs[:, 0:W], op0=ALU.mult, op1=ALU.mult)),
                    reads=[("xt", tag), "nrs", "a1", "a2", "gfin"], writes=[("ntmp", ts_)])
                o_ap, o_res = out_fn(c)
                bt, bcol = b_t
                P.op("act", (lambda e, c=c, ts_=ts_, o_ap=o_ap, bt=bt, bcol=bcol: e.activation(
                    out=o_ap, in_=tmp[:, ts_ * W:(ts_ + 1) * W], func=AF.Identity, bias=bt[:, bcol + c: bcol + c + 1])),
                    reads=[("ntmp", ts_), "modsb"], writes=[o_res])

        def xview(ap, t0, W):
            return ap[:, t0:t0 + W].rearrange("(c p) t -> p c t", p=128)

        out_dma_ops = []
        def layer(l):
            kind = MIX[l]
            mj = MIXJ[l]
            x_src = xT_in if l == 0 else xs
            last_layer = (l == nlayers - 1)
            x_dst = yT if last_layer else xs
            P.new_epoch()
            with contextlib.ExitStack() as esL:
                hT = esL.enter_context(nc.sbuf_tensor(f"hT{l}", [128, 8 * S], BF16))

                with contextlib.ExitStack() as esn:
                    xts = [esn.enter_context(nc.sbuf_tensor(f"xt{l}_{i}", [128, 8 * 512], F32)) for i in range(2)]
                    sq = esn.enter_context(nc.sbuf_tensor(f"sq{l}", [128, 8 * 512], BF16))
                    rs = esn.enter_context(nc.sbuf_tensor(f"rs{l}", [128, 512], F32))
                    tmp = esn.enter_context(nc.sbuf_tensor(f"ntmp{l}", [128, 8 * 512], F32))
                    for tt in range(8):
                        slot = tt % 2
                        xt = xts[slot]
                        P.dma("sp", (lambda e, s, xt=xt, tt=tt: e.dma_start(
                            out=xt[:].rearrange("p (c t) -> p c t", c=8), in_=xview(x_src, tt * 512, 512)).then_inc(s, 16)),
                            reads=["xdram"], writes=[("xt", slot)], semkey=("xt", slot))

                        def out_fn(c, tt=tt):
                            return hT[:, c * S + tt * 512: c * S + (tt + 1) * 512], ("hT", tt)
                        norm_tile(xt, 512, sq, tt % 2, rs, tmp, a1, (modsb, l * 48 + 0), l * 8, out_fn, slot)
                    P.barrier()

                with contextlib.ExitStack() as esa:
                    def asb(name, shape, dt):
                        return esa.enter_context(nc.sbuf_tensor(f"{name}{l}", list(shape), dt))
                    dbl_v = (kind != "B")
                    w3 = [asb(f"w3_{i}_", [128, 8 * 384], BF16) for i in range(2)]
                    qTs = [asb(f"qT{i}_", [128, S], BF16) for i in range(2)]
                    kTs = [asb(f"kT{i}_", [128, S], BF16) for i in range(2)]
                    NVT = 32
                    Vps = [asb(f"Vp{i}_", [128, NVT * 256], BF16) for i in range(2 if dbl_v else 1)]
                    acc = asb("acc", [128, 2 * S], F32)
                    rtmps = [asb(f"rtmp{i}_", [128, 1024], F32) for i in range(2)]
                    NE, NPT = (3, 6) if kind != "B" else (2, 4)
                    PW = 640 if kind == "B" else 384
                    Eb = [asb(f"E{i}_", [128, PW], BF16) for i in range(NE)]
                    PTb = [asb(f"PT{i}_", [128, PW], BF16) for i in range(NPT)]
                    OTp = asb("OTp", [128, S], BF16)
                    if kind == "A":
                        EBW = 256
                    elif kind == "C":
                        EBW = 384
                    if kind in ("A", "C"):
                        EB = [asb(f"EB{i}_", [128, 2 * EBW], BF16) for i in range(3)]
                    else:
                        Tall = asb("Tall", [128, 2 * 896], F32)
                        EBh = asb("EBh", [128, 2 * B_EBW], BF16)
                        validB = asb("validB", [128, B_EBW], BF16)
                        simple_load("pool", validB[:], validB_d, "validB")
                    Vp4s = [v[:].rearrange("p (t h c) -> p t h c", h=2, c=128) for v in Vps]
                    for vi, v in enumerate(Vps):
                        P.op("pool", (lambda e, v=v: e.memset(v[:], 1.0)), writes=[("Vp", vi)])
                    es_t = es_sink if kind == "C" else es_zero
                    if kind == "A":
                        w_in_d = a_w_in[mj]
                        groups = A_GROUPS
                    elif kind == "B":
                        w_in_d = b_w_in[0]
                        groups = [(1, S)]
                    else:
                        w_in_d = c_w_in[0]
                        groups = [(1, S)]
                    NG = len(groups)
                    NU = 8 * NG
                    ecnt = [0]
                    ptcnt = [0]
                    stcnt = [0]
                    otcnt = [0]
                    pjcnt = [0]

                    def wsrc(col0, n):
                        return w_in_d[:, col0:col0 + n].rearrange("(kc p) n -> p kc n", p=128)

                    def emit_loads(n):
                        hp_, gi_ = n // NG, n % NG
                        wslot_ = n % 2
                        wv3 = w3[wslot_][:].rearrange("p (kc n) -> p kc n", kc=8)
                        if kind == "A":
                            specs = [(0, 128, gi_ * 3072 + hp_ * 128), (128, 128, gi_ * 3072 + 1024 + hp_ * 128),
                                     (256, 128, gi_ * 3072 + 2048 + hp_ * 128)]
                        elif kind == "B":
                            specs = [(0, 128, hp_ * 128), (128, 128, 1024 + hp_ * 128), (256, 128, 2048 + hp_ * 128)]
                        else:
                            kv = hp_ // 2
                            specs = [(0, 128, hp_ * 128), (128, 64, 1024 + kv * 64), (192, 64, 1024 + kv * 64),
                                     (256, 64, 1280 + kv * 64), (320, 64, 1280 + kv * 64)]

                        def wload(e, s, specs=specs, wv3=wv3):
                            for (o, n_, c0) in specs:
                                e.dma_start(out=wv3[:, :, o:o + n_], in_=wsrc(c0, n_)).then_inc(s, 16)
                        P.dma("pool", wload, writes=[("w3", wslot_)], semkey=("w3", wslot_), n=len(specs))
                        if kind in ("A", "C"):
                            es3 = n % 3
                            ebt_ = EB[es3]

                            def ebload(e, s, ebt_=ebt_, gi_=gi_, hp_=hp_):
                                for h2 in range(2):
                                    h = hp_ * 2 + h2
                                    if kind == "A":
                                        src = bass.AP(repA.tensor, ((gi_ * 16 + h) * 128) * 384 + 127, [[383, 128], [1, 256]])
                                    else:
                                        src = bass.AP(repC.tensor, (h * 128) * 512 + 127, [[511, 128], [1, 384]])
                                    e.dma_start(out=ebt_[:, h2 * EBW:(h2 + 1) * EBW], in_=src).then_inc(s, 16)
                            P.dma("pool", ebload, reads=[("rep", gi_ if kind == "A" else 3)], writes=[("EB", es3)],
                                  semkey=("EB", es3), n=2)

                    def proj_gen(n, part):
                        hp_, gi_ = n // NG, n % NG
                        dil, L = groups[gi_]
                        slot = n % 2
                        vslot = slot if dbl_v else 0
                        wb = w3[slot]
                        if part == "qk":
                            for which, dstT in ((0, qTs[slot]), (1, kTs[slot])):
                                for tt in range(8):
                                    bank = pjcnt[0] % 2
                                    pjcnt[0] += 1

                                    def mm_p(e, which=which, tt=tt, bank=bank, wb=wb):
                                        ins = None
                                        for kc in range(8):
                                            ins = e.matmul(psb[bank][:, 0:512],
                                                           wb[:, kc * 384 + which * 128: kc * 384 + (which + 1) * 128],
                                                           hT[:, kc * S + tt * 512: kc * S + (tt + 1) * 512],
                                                           start=(kc == 0), stop=(kc == 7))
                                        return ins
                                    P.op("pe", mm_p, reads=[("w3", slot), ("hT", tt)], writes=[PSR(bank)])
                                    if which == 1:
                                        P.op("act", (lambda e, dstT=dstT, tt=tt, bank=bank: e.activation(
                                            out=dstT[:, tt * 512:(tt + 1) * 512], in_=psb[bank][:, 0:512], func=AF.Copy)),
                                            reads=[PSR(bank)], writes=[("qk", which, slot)])
                                    else:
                                        P.op("dve", (lambda e, dstT=dstT, tt=tt, bank=bank: e.tensor_copy(
                                            out=dstT[:, tt * 512:(tt + 1) * 512], in_=psb[bank][:, 0:512])),
                                            reads=[PSR(bank)], writes=[("qk", which, slot)])
                                    yield
                        else:
                            nT = L // 128
                            for r in range(dil):
                                for u0 in range(0, nT, 4):
                                    nbk = min(4, nT - u0)
                                    bank = pjcnt[0] % 2
                                    pjcnt[0] += 1
                                    t_idx = r * nT + u0

                                    def mm_v(e, r=r, u0=u0, nbk=nbk, bank=bank, wb=wb, dil=dil):
                                        ins = None
                                        for i in range(nbk):
                                            u = u0 + i
                                            for kc in range(8):
                                                ins = e.matmul(psb[bank][:, i * 128:(i + 1) * 128],
                                                               hT[:, sl(kc * S + r + dil * 128 * u, 128, dil)],
                                                               wb[:, kc * 384 + 256: kc * 384 + 384],
                                                               start=(kc == 0), stop=(kc == 7))
                                        return ins
                                    P.op("pe", mm_v, reads=[("w3", slot)] + [("hT", t) for t in range(8)], writes=[PSR(bank)])
                                    vbase = Vps[vslot][:]
                                    o_ap = bass.AP(vbase.tensor, vbase.offset + t_idx * 256,
                                                   [[vbase.ap[0][0], 128], [256, nbk], [192, 2], [1, 64]])
                                    pb = psb[bank][:]
                                    i_ap = bass.AP(pb.tensor, pb.offset, [[pb.ap[0][0], 128], [128, nbk], [64, 2], [1, 64]])
                                    P.op("dve", (lambda e, o_ap=o_ap, i_ap=i_ap: e.tensor_copy(out=o_ap, in_=i_ap)),
                                         reads=[PSR(bank)], writes=[("Vp", vslot)])
                                    yield

                    def run_all(g):
                        for _ in g:
                            pass

                    class Ticker:
                        def __init__(self, gens, total_steps, n_items):
                            self.gens = list(gens)
                            self.rate = n_items / max(1, total_steps)
                            self.acc = 0.0

                        def tick(self):
                            self.acc += self.rate
                            while self.acc >= 1.0 and self.gens:
                                self.acc -= 1.0
                                try:
                                    next(self.gens[0])
                                except StopIteration:
                                    self.gens.pop(0)

                        def flush(self):
                            for g in self.gens:
                                run_all(g)
                            self.gens = []

                    def attention(n, tk):
                        hp, gi = n // NG, n % NG
                        dil, L = groups[gi]
                        slot = n % 2
                        vslot = slot if dbl_v else 0
                        qT, kT, Vp4 = qTs[slot], kTs[slot], Vp4s[vslot]
                        nT = L // 128
                        first_group = (gi == 0)
                        qkres = [("qk", 0, slot), ("qk", 1, slot)]
                        if kind in ("A", "C"):
                            ebslot = n % 3
                            ebt = EB[ebslot]
                        for h2 in range(2):
                            prow = 64 * h2
                            if kind in ("A", "C"):
                                Rr = 64 if kind == "A" else 128
                                blocks = []
                                if kind == "A":
                                    for j in range(nT + 1):
                                        lo, hi = max(0, 128 * j - 64), min(L, 128 * j + 64)
                                        tiles = [u for u in (j - 1, j) if 0 <= u < nT]
                                        blocks.append((lo, hi, tiles, min(j, nT - 1)))
                                else:
                                    for j in range(nT):
                                        tiles = [u for u in (j - 1, j, j + 1) if 0 <= u < nT]
                                        blocks.append((128 * j, 128 * j + 128, tiles, min(j + 1, nT - 1)))
                                ogroups = []
                                cur = []
                                curw = 0
                                for b in blocks:
                                    w = b[1] - b[0]
                                    if curw + w > 512:
                                        ogroups.append(cur)
                                        cur, curw = [], 0
                                    cur.append(b)
                                    curw += w
                                if cur:
                                    ogroups.append(cur)
                                blk_group = {}
                                for gidx, gl in enumerate(ogroups):
                                    for b in gl:
                                        blk_group[b[0]] = gidx
                                LAG = 2
                                for r in range(dil):
                                    ptslot_of = {}
                                    qlo_of = {}
                                    grp_bank = {}
                                    done_in_grp = {}
                                    for step in range(nT + LAG):
                                        u = step
                                        if u < nT:
                                            qlo = max(0, 128 * u - Rr)
                                            qhi = min(L, 128 * u + 128 + Rr)
                                            W = qhi - qlo
                                            ebc = qlo - (128 * u - Rr)
                                            sbank = 2 + stcnt[0] % 4
                                            stcnt[0] += 1
                                            eslot = ecnt[0] % NE
                                            ecnt[0] += 1
                                            pslot = ptcnt[0] % NPT
                                            ptcnt[0] += 1
                                            ptslot_of[u] = pslot
                                            qlo_of[u] = qlo
                                            P.op("pe", (lambda e, u=u, qlo=qlo, W=W, sbank=sbank, r=r, prow=prow, dil=dil, kT=kT, qT=qT: e.matmul(
                                                psb[sbank][:, 0:W],
                                                kT[prow:prow + 64, sl(r + dil * 128 * u, 128, dil)],
                                                qT[prow:prow + 64, sl(r + dil * qlo, W, dil)], start=True, stop=True)),
                                                reads=qkres, writes=[PSR(sbank)])
                                            P.op("act", (lambda e, W=W, sbank=sbank, eslot=eslot: e.activation(
                                                out=Eb[eslot][:, 0:W], in_=psb[sbank][:, 0:W], func=AF.Exp, scale=0.125)),
                                                reads=[PSR(sbank)], writes=[("E", eslot)])
                                            P.op("dve", (lambda e, W=W, eslot=eslot, pslot=pslot, ebc=ebc, h2=h2, ebt=ebt: e.tensor_tensor(
                                                out=PTb[pslot][:, 0:W], in0=Eb[eslot][:, 0:W],
                                                in1=ebt[:, h2 * EBW + ebc: h2 * EBW + ebc + W], op=ALU.mult)),
                                                reads=[("E", eslot), ("EB", ebslot)], writes=[("PT", pslot)])
                                        v = step - LAG
                                        if v >= 0:
                                            for b in blocks:
                                                lo, hi, tiles, ready = b
                                                if ready != v:
                                                    continue
                                                gidx = blk_group[lo]
                                                gl = ogroups[gidx]
                                                if gidx not in grp_bank:
                                                    grp_bank[gidx] = 6 + otcnt[0] % 2
                                                    otcnt[0] += 1
                                                    done_in_grp[gidx] = 0
                                                obank = grp_bank[gidx]
                                                base = gl[0][0]
                                                c0 = lo - base

                                                def mm_pv(e, lo=lo, hi=hi, tiles=tiles, obank=obank, c0=c0, r=r, h2=h2,
                                                          pts=dict(ptslot_of), qls=dict(qlo_of), nT=nT, Vp4=Vp4):
                                                    ins = None
                                                    for i, uu in enumerate(tiles):
                                                        pc = lo - qls[uu]
                                                        ins = e.matmul(psb[obank][:, c0:c0 + (hi - lo)],
                                                                       Vp4[:, r * nT + uu, h2, :],
                                                                       PTb[pts[uu]][:, pc:pc + (hi - lo)],
                                                                       start=(i == 0), stop=(i == len(tiles) - 1))
                                                    return ins
                                                P.op("pe", mm_pv, reads=[("PT", ptslot_of[uu]) for uu in tiles] + [("Vp", vslot)],
                                                     writes=[PSR(obank)])
                                                done_in_grp[gidx] += 1
                                                if done_in_grp[gidx] == len(gl):
                                                    wtot = gl[-1][1] - base
                                                    a_ap = acc[:, sl(h2 * S + r + dil * base, wtot, dil)]
                                                    if first_group:
                                                        P.op("dve", (lambda e, a_ap=a_ap, obank=obank, wtot=wtot: e.tensor_copy(
                                                            out=a_ap, in_=psb[obank][:, 0:wtot])),
                                                            reads=[PSR(obank), ("accbar", h2)], writes=[])
                                                    else:
                                                        P.op("dve", (lambda e, a_ap=a_ap, obank=obank, wtot=wtot: e.tensor_tensor(
                                                            out=a_ap, in0=psb[obank][:, 0:wtot], in1=a_ap, op=ALU.add)),
                                                            reads=[PSR(obank), ("accbar", h2)], writes=[])
                                        tk.tick()
                            else:
                                LAG = 1
                                pend = []
                                obank_cur = None
                                for step in range(32 + LAG):
                                    j = step
                                    if j < 32:
                                        U = _b_tiles(j)
                                        nU_ = len(U)
                                        off, _nU, _x0 = B_OFF[_b_class(j)]
                                        sb0 = 2 + 2 * (stcnt[0] % 2)
                                        stcnt[0] += 1
                                        eslot = ecnt[0] % NE
                                        ecnt[0] += 1
                                        pslot = ptcnt[0] % NPT
                                        ptcnt[0] += 1

                                        def mm_sb(e, j=j, U=U, sb0=sb0, prow=prow, kT=kT, qT=qT):
                                            ins = None
                                            for i, uu in enumerate(U):
                                                bk = sb0 + (i // 4)
                                                ins = e.matmul(psb[bk][:, (i % 4) * 128:(i % 4 + 1) * 128],
                                                               kT[prow:prow + 64, uu * 128:(uu + 1) * 128],
                                                               qT[prow:prow + 64, j * 128:(j + 1) * 128], start=True, stop=True)
                                            return ins
                                        P.op("pe", mm_sb, reads=qkres, writes=[PSR(sb0), PSR(sb0 + 1)])

                                        def ex_b(e, nU_=nU_, sb0=sb0, eslot=eslot):
                                            ins = e.activation(out=Eb[eslot][:, 0:min(nU_, 4) * 128], in_=psb[sb0][:, 0:min(nU_, 4) * 128],
                                                               func=AF.Exp, scale=0.125)
                                            if nU_ > 4:
                                                ins = e.activation(out=Eb[eslot][:, 512:640], in_=psb[sb0 + 1][:, 0:128],
                                                                   func=AF.Exp, scale=0.125)
                                            return ins
                                        P.op("act", ex_b, reads=[PSR(sb0), PSR(sb0 + 1)], writes=[("E", eslot)])
                                        P.op("dve", (lambda e, nU_=nU_, eslot=eslot, pslot=pslot, off=off, h2=h2: e.tensor_tensor(
                                            out=PTb[pslot][:, 0:nU_ * 128], in0=Eb[eslot][:, 0:nU_ * 128],
                                            in1=EBh[:, h2 * B_EBW + off: h2 * B_EBW + off + nU_ * 128], op=ALU.mult)),
                                            reads=[("E", eslot), ("EBh", h2)], writes=[("PT", pslot)])
                                        pend.append((j, U, pslot))
                                    v = step - LAG
                                    if v >= 0:
                                        jv, Uv, psl = pend[v]
                                        if jv % 4 == 0:
                                            obank_cur = 6 + otcnt[0] % 2
                                            otcnt[0] += 1
                                        obank = obank_cur
                                        c0 = (jv % 4) * 128

                                        def mm_pvb(e, Uv=Uv, psl=psl, obank=obank, c0=c0, h2=h2, Vp4=Vp4):
                                            ins = None
                                            for i, uu in enumerate(Uv):
                                                ins = e.matmul(psb[obank][:, c0:c0 + 128], Vp4[:, uu, h2, :],
                                                               PTb[psl][:, i * 128:(i + 1) * 128],
                                                               start=(i == 0), stop=(i == len(Uv) - 1))
                                            return ins
                                        P.op("pe", mm_pvb, reads=[("PT", psl), ("Vp", vslot)], writes=[PSR(obank)])
                                        if jv % 4 == 3:
                                            a_ap = acc[:, h2 * S + (jv - 3) * 128: h2 * S + (jv + 1) * 128]
                                            P.op("dve", (lambda e, a_ap=a_ap, obank=obank: e.tensor_copy(
                                                out=a_ap, in_=psb[obank][:, 0:512])),
                                                reads=[PSR(obank), ("accbar", h2)], writes=[])
                                    tk.tick()
                        for h2 in range(2):
                            P.op("dve", (lambda e: e.memset(dummy[:, 0:1], 0.0)), reads=[], writes=[("accbar", h2)])

                    def normalize(hp):
                        for ck in range(4):
                            t0 = ck * 1024
                            h0 = hp * 2
                            rt = rtmps[ck % 2]
                            rr = ck % 2
                            P.op("dve", (lambda e, t0=t0, h0=h0, rt=rt: e.tensor_scalar(
                                out=rt[0:64, :], in0=acc[64:128, t0:t0 + 1024], scalar1=es_t[64:128, h0:h0 + 1], scalar2=None,
                                op0=ALU.add)),
                                reads=[("accbar", 0), "es_sink", "es_zero"], writes=[("rtmp", rr)])
                            P.op("dve", (lambda e, rt=rt: e.reciprocal(out=rt[0:64, :], in_=rt[0:64, :])),
                                 reads=[("rtmp", rr)], writes=[("rtmp", rr)])
                            P.op("dve", (lambda e, t0=t0, h0=h0, rt=rt: e.tensor_scalar(
                                out=rt[64:128, :], in0=acc[0:64, S + t0:S + t0 + 1024], scalar1=es_t[0:64, h0 + 1:h0 + 2], scalar2=None,
                                op0=ALU.add)),
                                reads=[("accbar", 1), "es_sink", "es_zero"], writes=[("rtmp2", rr)])
                            P.op("dve", (lambda e, rt=rt: e.reciprocal(out=rt[64:128, :], in_=rt[64:128, :])),
                                 reads=[("rtmp2", rr)], writes=[("rtmp2", rr)])
                            P.op("pool", (lambda e, t0=t0, rt=rt: e.tensor_tensor(
                                out=OTp[0:64, t0:t0 + 1024], in0=acc[0:64, t0:t0 + 1024], in1=rt[0:64, :], op=ALU.mult)),
                                reads=[("rtmp", rr), ("accbar", 0)], writes=["OTp"])
                            P.op("pool", (lambda e, t0=t0, rt=rt: e.tensor_tensor(
                                out=OTp[64:128, t0:t0 + 1024], in0=acc[64:128, S + t0:S + t0 + 1024], in1=rt[64:128, :], op=ALU.mult)),
                                reads=[("rtmp2", rr), ("accbar", 1)], writes=["OTp"])
                        P.dma("sp", (lambda e, s, hp=hp: e.dma_start(out=OTd[hp * 128:(hp + 1) * 128, :], in_=OTp[:]).then_inc(s, 16)),
                              reads=["OTp"], writes=["OTd"], semkey="OTp")
                        for h2 in range(2):
                            P.op("dve", (lambda e: e.memset(dummy[:, 0:1], 0.0)), reads=[], writes=[("accbar", h2)])

                    def b_prep(n):
                        hp = n // NG

                        def tload(e, s, hp=hp):
                            for h2 in range(2):
                                h = hp * 2 + h2
                                src = bass.AP(repB.tensor, (h * 128) * 1024 + 127, [[1023, 128], [1, 896]])
                                e.dma_start(out=Tall[:, h2 * 896:(h2 + 1) * 896], in_=src).then_inc(s, 16)
                        P.dma("sp", tload, reads=["repB"], writes=["Tall"], semkey="Tall", n=2)
                        for h2 in range(2):
                            for name, _j, U, x0 in B_CLASSES:
                                off, nU, _ = B_OFF[name]
                                P.op("pool", (lambda e, h2=h2, off=off, nU=nU, x0=x0: e.tensor_tensor(
                                    out=EBh[:, h2 * B_EBW + off: h2 * B_EBW + off + 128 * nU],
                                    in0=Tall[:, h2 * 896 + x0: h2 * 896 + x0 + 128 * nU],
                                    in1=validB[:, off: off + 128 * nU], op=ALU.mult)),
                                    reads=["Tall", "validB"], writes=[("EBh", h2)])

                    emit_loads(0)
                    if NU > 1:
                        emit_loads(1)
                    run_all(proj_gen(0, "qk"))
                    run_all(proj_gen(0, "v"))
                    for n in range(NU):
                        gi = n % NG
                        dil, L = groups[gi]
                        if n + 2 < NU:
                            emit_loads(n + 2)
                        if kind == "B":
                            b_prep(n)
                        gens = []
                        n_items = 0
                        if n + 1 < NU:
                            gens.append(proj_gen(n + 1, "qk"))
                            n_items += 16
                            if dbl_v:
                                gens.append(proj_gen(n + 1, "v"))
                                n_items += 8
                        if kind == "B":
                            total_steps = 2 * 33
                        else:
                            total_steps = 2 * dil * (L // 128 + 2)
                        tk = Ticker(gens, int(total_steps * 0.85), n_items)
                        attention(n, tk)
                        tk.flush()
                        if n + 1 < NU and not dbl_v:
                            run_all(proj_gen(n + 1, "v"))
                        if gi == NG - 1:
                            normalize(n // NG)
                    P.barrier()
            P.barrier()

            with contextlib.ExitStack() as esf:
                def fsb(name, shape, dt):
                    return esf.enter_context(nc.sbuf_tensor(f"{name}{l}", list(shape), dt))
                TW = 256
                wo = fsb("wo", [128, 8 * 1024], BF16)
                win = fsb("win", [128, 8 * 2 * DFF], BF16)
                wout = fsb("wout", [128, NFC * 1024], BF16)
                xts = [fsb(f"fx{i}_", [128, 8 * TW], F32) for i in range(2)]
                ots = [fsb(f"fo{i}_", [128, 8 * TW], BF16) for i in range(2)]
                sq = fsb("fsq", [128, 8 * TW], BF16)
                rs = fsb("frs", [128, TW], F32)
                tmp = fsb("ftmp", [128, 3 * TW], F32)
                h2Ts = [fsb(f"fh2_{i}_", [128, 8 * TW], BF16) for i in range(2)]
                aT = fsb("faT", [128, NFC * TW], BF16)
                sg = [fsb(f"fsg{i}_", [128, TW], BF16) for i in range(2)]
                wo_d = {"A": a_w_out, "B": b_w_out, "C": c_w_out}[kind][mj]
                wo3 = wo[:].rearrange("p (kc n) -> p kc n", kc=8)
                for pc in range(2):
                    P.dma("pool", (lambda e, s, pc=pc: e.dma_start(
                        out=wo3[:, :, pc * 512:(pc + 1) * 512],
                        in_=wo_d[:, pc * 512:(pc + 1) * 512].rearrange("(kc p) n -> p kc n", p=128)).then_inc(s, 16)),
                        writes=[("wo", pc)], semkey=("wo", pc))
                win3 = win[:].rearrange("p (kc n) -> p kc n", kc=8)
                for pc in range(11):
                    P.dma("pool", (lambda e, s, pc=pc: e.dma_start(
                        out=win3[:, :, pc * 512:(pc + 1) * 512],
                        in_=ffn_w_in[l, :, pc * 512:(pc + 1) * 512].rearrange("(kc p) n -> p kc n", p=128)).then_inc(s, 16)),
                        writes=[("win", pc)], semkey=("win", pc))
                wout3 = wout[:].rearrange("p (fc n) -> p fc n", fc=NFC)
                for pc in range(11):
                    P.dma("pool", (lambda e, s, pc=pc: e.dma_start(
                        out=wout3[:, 2 * pc:2 * pc + 2, :],
                        in_=ffn_w_out[l, pc * 256:(pc + 1) * 256, :].rearrange("(fc p) n -> p fc n", p=128)).then_inc(s, 16)),
                        writes=[("wout", pc)], semkey=("wout", pc))
                mcol = l * 48
                NTT = S // TW
                bankc = [0]

                def nb():
                    b = bankc[0] % 8
                    bankc[0] += 1
                    return b
                def stageA(tt):
                    slot = tt % 2
                    xt = xts[slot]
                    ot = ots[slot]
                    t0 = tt * TW
                    P.dma("sp", (lambda e, s, xt=xt, t0=t0: e.dma_start(
                        out=xt[:].rearrange("p (c t) -> p c t", c=8), in_=xview(x_src, t0, TW)).then_inc(s, 16)),
                        reads=["xdram"], writes=[("xt", slot)], semkey=("fx", slot))
                    P.dma("sp", (lambda e, s, ot=ot, t0=t0: e.dma_start(
                        out=ot[:].rearrange("p (c t) -> p c t", c=8), in_=xview(OTd, t0, TW)).then_inc(s, 16)),
                        reads=["OTd"], writes=[("ot", slot)], semkey=("fo", slot))

                def stageB(tt):
                    slot = tt % 2
                    xt = xts[slot]
                    ot = ots[slot]
                    h2T = h2Ts[slot]
                    for m in range(8):
                        bank = nb()

                        def mm_o(e, m=m, bank=bank, ot=ot):
                            ins = None
                            for kc in range(8):
                                ins = e.matmul(psb[bank][:, 0:TW], wo[:, kc * 1024 + m * 128: kc * 1024 + (m + 1) * 128],
                                               ot[:, kc * TW:(kc + 1) * TW], start=(kc == 0), stop=(kc == 7))
                            return ins
                        P.op("pe", mm_o, reads=[("wo", m // 4), ("ot", slot)], writes=[PSR(bank)])
                        P.op("dve", (lambda e, m=m, bank=bank, xt=xt: e.scalar_tensor_tensor(
                            out=xt[:, m * TW:(m + 1) * TW], in0=psb[bank][:, 0:TW], scalar=modsb[:, mcol + 16 + m: mcol + 17 + m],
                            in1=xt[:, m * TW:(m + 1) * TW], op0=ALU.mult, op1=ALU.add)),
                            reads=[PSR(bank), ("xt", slot), "modsb"], writes=[("xt", slot)])

                    def out_fn2(c, h2T=h2T, slot=slot):
                        return h2T[:, c * TW:(c + 1) * TW], ("h2T", slot)
                    norm_tile(xt, TW, sq, nb(), rs, tmp, a2, (modsb, mcol + 24), l * 8, out_fn2, slot, nslots=3)

                def stageC(tt):
                    slot = tt % 2
                    h2T = h2Ts[slot]
                    for f in range(NFC):
                        bg = nb()
                        bu = nb()

                        def mm_gu(e, f=f, bg=bg, bu=bu, h2T=h2T):
                            ins = None
                            for kc in range(8):
                                ins = e.matmul(psb[bg][:, 0:TW], win[:, kc * 2 * DFF + f * 128: kc * 2 * DFF + (f + 1) * 128],
                                               h2T[:, kc * TW:(kc + 1) * TW], start=(kc == 0), stop=(kc == 7))
                            for kc in range(8):
                                ins = e.matmul(psb[bu][:, 0:TW], win[:, kc * 2 * DFF + DFF + f * 128: kc * 2 * DFF + DFF + (f + 1) * 128],
                                               h2T[:, kc * TW:(kc + 1) * TW], start=(kc == 0), stop=(kc == 7))
                            return ins
                        P.op("pe", mm_gu, reads=[("win", (f * 128) // 512), ("win", (DFF + f * 128) // 512), ("h2T", slot)],
                             writes=[PSR(bg), PSR(bu)])
                        sslot = f % 2
                        P.op("act", (lambda e, bg=bg, sslot=sslot: e.activation(out=sg[sslot][:], in_=psb[bg][:, 0:TW], func=AF.Silu)),
                             reads=[PSR(bg)], writes=[("sg", sslot)])
                        P.op("dve", (lambda e, f=f, bu=bu, sslot=sslot: e.tensor_tensor(
                            out=aT[:, f * TW:(f + 1) * TW], in0=psb[bu][:, 0:TW], in1=sg[sslot][:], op=ALU.mult)),
                            reads=[PSR(bu), ("sg", sslot)], writes=[("aT", f)])

                def stageD(tt):
                    slot = tt % 2
                    xt = xts[slot]
                    t0 = tt * TW
                    for m in range(8):
                        bank = nb()

                        def mm_f(e, m=m, bank=bank):
                            ins = None
                            for f in range(NFC):
                                ins = e.matmul(psb[bank][:, 0:TW], wout[:, f * 1024 + m * 128: f * 1024 + (m + 1) * 128],
                                               aT[:, f * TW:(f + 1) * TW], start=(f == 0), stop=(f == NFC - 1))
                            return ins
                        P.op("pe", mm_f, reads=[("wout", pc) for pc in range(11)] + [("aT", f) for f in range(NFC)],
                             writes=[PSR(bank)])
                        P.op("dve", (lambda e, m=m, bank=bank, xt=xt: e.scalar_tensor_tensor(
                            out=xt[:, m * TW:(m + 1) * TW], in0=psb[bank][:, 0:TW], scalar=modsb[:, mcol + 40 + m: mcol + 41 + m],
                            in1=xt[:, m * TW:(m + 1) * TW], op0=ALU.mult, op1=ALU.add)),
                            reads=[PSR(bank), ("xt", slot), "modsb"], writes=[("xt", slot)])
                    if last_layer:
                        norm_tile(xt, TW, sq, nb(), rs, tmp, gfin_sb, None, 0, None, slot, nslots=3, inplace=True)
                    o = P.dma("sp", (lambda e, s, xt=xt, t0=t0: e.dma_start(
                        out=xview(x_dst, t0, TW), in_=xt[:].rearrange("p (c t) -> p c t", c=8)).then_inc(s, 16)),
                        reads=[("xt", slot)], writes=["xdram_w"], semkey=("fxs", slot))
                    if last_layer:
                        out_dma_ops.append(o)

                stageA(0)
                stageA(1)
                stageB(0)
                for tt in range(NTT):
                    stageC(tt)
                    if tt + 1 < NTT:
                        stageB(tt + 1)
                    stageD(tt)
                    if tt + 2 < NTT:
                        stageA(tt + 2)
                P.barrier()
        for l in range(nlayers):
            layer(l)
        P.emit(final_waits=out_dma_ops[-2:])
    return nc


_PROGRAM_CACHE = {}


def _layout_inputs(inputs):
    f = np.float32
    x = np.asarray(inputs["x"], f)
    c = np.asarray(inputs["c"], f)
    ohA, ohC, validB = _static_tables()

    def l128(v):
        return np.ascontiguousarray(np.asarray(v, f).reshape(-1, 128).T)

    shared = {
        "rel_bias": np.ascontiguousarray(np.asarray(inputs["rel_bias"], f)),
        "ada_w": np.ascontiguousarray(np.asarray(inputs["ada_w"], f)),
        "ada_bl": np.ascontiguousarray(np.asarray(inputs["ada_b"], f).reshape(DEPTH, 48, 128).transpose(0, 2, 1)),
        "gmix": l128(np.asarray(inputs["norm_mix"], f)),
        "gffn": l128(np.asarray(inputs["norm_ffn"], f)),
        "gfin": l128(np.asarray(inputs["norm_final"], f)),
        "a_w_in": np.ascontiguousarray(np.asarray(inputs["a_w_in"], f)),
        "a_w_out": np.ascontiguousarray(np.asarray(inputs["a_w_out"], f)),
        "b_w_in": np.ascontiguousarray(np.asarray(inputs["b_w_in"], f)),
        "b_w_out": np.ascontiguousarray(np.asarray(inputs["b_w_out"], f)),
        "rpbff": np.ascontiguousarray(np.asarray(inputs["b_rpb"], f)[0][:, ::-1, ::-1].reshape(16, 465)),
        "c_w_in": np.ascontiguousarray(np.asarray(inputs["c_w_in"], f)),
        "c_w_out": np.ascontiguousarray(np.asarray(inputs["c_w_out"], f)),
        "c_sink": np.ascontiguousarray(np.asarray(inputs["c_sink"], f)),
        "ffn_w_in": np.ascontiguousarray(np.asarray(inputs["ffn_w_in"], f)),
        "ffn_w_out": np.ascontiguousarray(np.asarray(inputs["ffn_w_out"], f)),
        "ohA": ohA, "ohC": ohC, "validB": validB,
    }
    in_maps = []
    for b in range(8):
        m = dict(shared)
        m["xT"] = np.ascontiguousarray(x[b].T)
        m["cb"] = l128(c[b])
        in_maps.append(m)
    return in_maps


def kernel(**inputs):
    in_maps = _layout_inputs(inputs)
    nc = build_program(DEPTH)
    res = run_bass_kernel_spmd(nc, in_maps, core_ids=list(range(8)))
    out = np.stack([np.ascontiguousarray(np.asarray(r["yT"]).T) for r in res.results], axis=0)
    return out.astype(np.float32)
```

```python
import math
import contextlib
import numpy as np
import concourse.bass as bass
import concourse.mybir as mybir
from concourse.bass_utils import run_bass_kernel_spmd

F32 = mybir.dt.float32
BF16 = mybir.dt.bfloat16
ALU = mybir.AluOpType
AF = mybir.ActivationFunctionType

D = 1024
S = 4096
DEPTH = 4
DFF = 2816
NFC = DFF // 128
EPS = 1e-6
NEG = -30000.0
ENGS = ("pe", "act", "dve", "pool", "sp")


class _Op:
    __slots__ = ("eng", "fn", "waits", "sig_key", "sig_val", "signaled", "idx", "epoch", "is_dma")


class Prog:
    def __init__(self, nc):
        self.nc = nc
        self.ops = {e: [] for e in ENGS}
        self.last_w = {}
        self.readers = {}
        self.epoch = 0
        self.dma_cnt = {}
        self.dma_last = {}
        self.waited = {}

    def new_epoch(self):
        self.epoch += 1

    def _stream(self, op):
        return ("dma", op.sig_key) if op.is_dma else ("eng", op.eng)

    def _pos(self, d):
        return d.sig_val if d.is_dma else (d.epoch, d.idx)

    def _finish(self, eng, op, deps):
        waits = {}
        for d, raw in deps:
            if (not d.is_dma) and d.eng == eng:
                if eng in ("pe", "sp") or not raw:
                    continue
            stt = self._stream(d)
            pos = self._pos(d)
            prev = waits.get(stt)
            if prev is None or pos > prev[0]:
                waits[stt] = (pos, d)
        final = []
        for stt, (pos, d) in waits.items():
            k = (eng, stt)
            have = self.waited.get(k)
            if have is not None and have >= pos:
                continue
            self.waited[k] = pos
            d.signaled = True
            final.append(d)
        op.waits = final
        self.ops[eng].append(op)

    def _add(self, eng, fn, reads, writes, is_dma=False, semkey=None, n_dma=1):
        op = _Op()
        op.eng = eng
        op.fn = fn
        op.is_dma = is_dma
        op.signaled = is_dma
        op.epoch = self.epoch
        op.idx = len(self.ops[eng])
        op.sig_key = None
        op.sig_val = None
        if is_dma:
            c = self.dma_cnt.get(semkey, 0) + 16 * n_dma
            self.dma_cnt[semkey] = c
            op.sig_key = semkey
            op.sig_val = c
            self.dma_last[semkey] = op
        deps = []
        for r in reads:
            w = self.last_w.get(r)
            if w is not None:
                deps.append((w, True))
        for w_ in writes:
            w = self.last_w.get(w_)
            if w is not None:
                deps.append((w, False))
            for rd in self.readers.get(w_, {}).values():
                deps.append((rd, False))
        self._finish(eng, op, deps)
        for r in reads:
            self.readers.setdefault(r, {})[self._stream(op)] = op
        for w_ in writes:
            self.last_w[w_] = op
            self.readers[w_] = {}
        return op

    def op(self, eng, fn, reads=(), writes=()):
        return self._add(eng, fn, tuple(reads), tuple(writes))

    def dma(self, q, fn, reads=(), writes=(), semkey=None, n=1):
        return self._add(q, fn, tuple(reads), tuple(writes), is_dma=True, semkey=semkey, n_dma=n)

    def barrier(self):
        lasts = [self.ops[e][-1] for e in ENGS if self.ops[e] and not self.ops[e][-1].is_dma]
        lasts = []
        for e in ENGS:
            for o in reversed(self.ops[e]):
                if not o.is_dma and o.fn is not None:
                    lasts.append(o)
                    break
        lasts += list(self.dma_last.values())
        for e in ENGS:
            op = _Op()
            op.eng = e
            op.fn = None
            op.is_dma = False
            op.signaled = False
            op.epoch = self.epoch
            op.idx = len(self.ops[e])
            op.sig_key = None
            op.sig_val = None
            self._finish(e, op, [(d, True) for d in lasts if d.is_dma or d.eng != e])

    def emit(self, final_waits=()):
        nc = self.nc
        with contextlib.ExitStack() as st:
            esem = {}
            for e in ENGS:
                used = sorted({o.epoch for o in self.ops[e] if o.signaled and not o.is_dma})
                for ep in used:
                    esem[(e, ep)] = st.enter_context(nc.semaphore(f"s_{e}_{ep}"))
            dsem = {}
            for k in self.dma_cnt:
                dsem[k] = st.enter_context(nc.semaphore(f"d_{len(dsem)}"))
            for e in ENGS:
                cnt = {}
                for o in self.ops[e]:
                    if o.is_dma or not o.signaled:
                        continue
                    c = cnt.get(o.epoch, 0) + 1
                    cnt[o.epoch] = c
                    o.sig_key = (e, o.epoch)
                    o.sig_val = c
            block = st.enter_context(nc.Block())

            def run(e, engine):
                for o in self.ops[e]:
                    for d in o.waits:
                        if d.is_dma:
                            engine.wait_ge(dsem[d.sig_key], d.sig_val)
                        else:
                            engine.wait_ge(esem[d.sig_key], d.sig_val)
                    if o.fn is None:
                        continue
                    if o.is_dma:
                        o.fn(engine, dsem[o.sig_key])
                    else:
                        ins = o.fn(engine)
                        if o.signaled:
                            ins.then_inc(esem[o.sig_key], 1)
                if e == "sp":
                    for d in final_waits:
                        engine.wait_ge(dsem[d.sig_key], d.sig_val)

            @block.tensor
            def _(eng):
                run("pe", eng)

            @block.scalar
            def _(eng):
                run("act", eng)

            @block.vector
            def _(eng):
                run("dve", eng)

            @block.gpsimd
            def _(eng):
                run("pool", eng)

            @block.sync
            def _(eng):
                run("sp", eng)


def sl(start, count, step=1):
    return slice(start, start + step * (count - 1) + 1, step)


def _t5_bucket(rel):
    half, max_exact = 16, 8
    ret = np.where(rel > 0, half, 0)
    n = np.abs(rel)
    nf = np.maximum(n, 1).astype(np.float32)
    large = max_exact + (np.log(nf / np.float32(max_exact)) / np.float32(math.log(1024 / max_exact))
                         * np.float32(half - max_exact)).astype(np.int32)
    large = np.minimum(large, half - 1)
    return ret + np.where(n < max_exact, n, large)


def _onehot_band(length, pad, center, radius, dil):
    oh = np.zeros((33, pad), np.float32)
    i = np.arange(pad)
    rel = center - i
    valid = (np.abs(rel) <= radius) & (i < length)
    b = _t5_bucket(rel * dil)
    for k in range(pad):
        if valid[k]:
            oh[b[k], k] = 1.0
        else:
            oh[32, k] = 1.0
    return oh


B_CLASSES = [
    ("int", 2, [4, 3, 2, 1, 0], 128),
    ("e0", 0, [3, 2, 1, 0], 0),
    ("e1", 1, [3, 2, 1, 0], 128),
    ("e30", 30, [31, 30, 29, 28], 256),
    ("e31", 31, [31, 30, 29, 28], 384),
]
B_OFF = {}
_o = 0
for _n, _j, _U, _x in B_CLASSES:
    B_OFF[_n] = (_o, len(_U), _x)
    _o += 128 * len(_U)
B_EBW = _o


def _b_class(j):
    return {0: "e0", 1: "e1", 30: "e30", 31: "e31"}.get(j, "int")


def _b_tiles(j):
    if j <= 1:
        return [3, 2, 1, 0]
    if j >= 30:
        return [31, 30, 29, 28]
    return [j + 2, j + 1, j, j - 1, j - 2]


def _valid_b():
    out = np.zeros((128, B_EBW), np.float32)
    kk = np.arange(128)[:, None]
    qq = np.arange(128)[None, :]
    for name, j, U, _x in B_CLASSES:
        off = B_OFF[name][0]
        for i, u in enumerate(U):
            kt = 128 * u + kk
            qt = 128 * j + qq
            kr, kc = kt // 64, kt % 64
            r, c = qt // 64, qt % 64
            rs = np.clip(r - 4, 0, 56)
            cs = np.clip(c - 8, 0, 48)
            v = (kr >= rs) & (kr < rs + 8) & (kc >= cs) & (kc < cs + 16)
            out[:, off + 128 * i: off + 128 * (i + 1)] = v.astype(np.float32)
    return out


def _static_tables():
    ohA = np.stack([_onehot_band(383, 384, 191, 64, d) for d in (1, 4, 16)])
    ohC = _onehot_band(511, 512, 255, 128, 1)
    return ohA, ohC, _valid_b()


MIX = ["A", "B", "C", "A"]
MIXJ = [0, 0, 0, 1]
A_GROUPS = [(1, 4096), (4, 1024), (16, 256)]


def build_program(nlayers=DEPTH):
    nc = bass.Bass("TRN2", target_bir_lowering=False)

    def din(name, shape, dt=F32):
        return nc.dram_tensor(name, list(shape), dt, kind="ExternalInput").ap()

    xT_in = din("xT", [D, S])
    cb_in = din("cb", [128, 8])
    relb = din("rel_bias", [32, 16])
    ada_w = din("ada_w", [DEPTH, D, 6 * D])
    ada_b = din("ada_bl", [DEPTH, 128, 48])
    gmix = din("gmix", [128, DEPTH * 8])
    gffn = din("gffn", [128, DEPTH * 8])
    gfin = din("gfin", [128, 8])
    a_w_in = din("a_w_in", [2, D, 9216])
    a_w_out = din("a_w_out", [2, D, D])
    b_w_in = din("b_w_in", [1, D, 3072])
    b_w_out = din("b_w_out", [1, D, D])
    rpbff = din("rpbff", [16, 15 * 31])
    c_w_in = din("c_w_in", [1, D, 1536])
    c_w_out = din("c_w_out", [1, D, D])
    c_sink = din("c_sink", [1, 16])
    ffn_w_in = din("ffn_w_in", [DEPTH, D, 2 * DFF])
    ffn_w_out = din("ffn_w_out", [DEPTH, DFF, D])
    ohA_d = din("ohA", [3, 33, 384])
    ohC_d = din("ohC", [33, 512])
    validB_d = din("validB", [128, B_EBW])
    yT = nc.dram_tensor("yT", [D, S], F32, kind="ExternalOutput").ap()

    xs = nc.dram_tensor("xs", [D, S], F32).ap()
    OTd = nc.dram_tensor("OTd", [D, S], BF16).ap()
    vecA = nc.dram_tensor("vecA", [3, 16, 384], F32).ap()
    vecC = nc.dram_tensor("vecC", [16, 512], F32).ap()
    vecB = nc.dram_tensor("vecB", [16, 1024], F32).ap()
    repA = nc.dram_tensor("repA", [3, 16, 128, 384], F32).ap()
    repC = nc.dram_tensor("repC", [16, 128, 512], F32).ap()
    repB = nc.dram_tensor("repB", [16, 128, 1024], F32).ap()

    P = Prog(nc)
    ES = contextlib.ExitStack()
    with ES:
        def sb(name, shape, dt):
            return ES.enter_context(nc.sbuf_tensor("t_" + name, list(shape), dt))

        psb = [ES.enter_context(nc.psum_tensor(f"psb{i}", [128, 512], F32)) for i in range(8)]

        def PSR(i):
            return ("ps", i)

        ones_bf = sb("ones_bf", [128, 128], BF16)
        cb = sb("cb", [128, 8], F32)
        condb = sb("condb", [128, 8], BF16)
        modsb = sb("modsb", [128, DEPTH * 48], F32)
        adab = sb("adab", [128, DEPTH * 48], F32)
        gmix_sb = sb("gmix_sb", [128, DEPTH * 8], F32)
        gffn_sb = sb("gffn_sb", [128, DEPTH * 8], F32)
        gfin_sb = sb("gfin_sb", [128, 8], F32)
        a1 = sb("a1", [128, DEPTH * 8], F32)
        a2 = sb("a2", [128, DEPTH * 8], F32)
        es_sink = sb("es_sink", [128, 16], F32)
        es_zero = sb("es_zero", [128, 16], F32)
        dummy = sb("dummy", [128, 8], F32)

        P.op("dve", lambda e: e.memset(ones_bf[:], 1.0), writes=["ones"])
        P.op("dve", lambda e: e.memset(es_zero[:], 0.0), writes=["es_zero"])

        def simple_load(q, dst, src, res):
            P.dma(q, lambda e, s: e.dma_start(out=dst, in_=src).then_inc(s, 16), writes=[res], semkey=res)

        simple_load("sp", cb[:], cb_in, "cb")
        simple_load("sp", adab[:].rearrange("p (l j) -> p l j", l=DEPTH), ada_b.rearrange("l p j -> p l j"), "adab")
        simple_load("sp", gmix_sb[:], gmix, "gmix")
        simple_load("sp", gffn_sb[:], gffn, "gffn")
        simple_load("sp", gfin_sb[:], gfin, "gfin")
        P.dma("sp", lambda e, s: e.dma_start(out=es_sink[:], in_=bass.AP(c_sink.tensor, 0, [[0, 128], [1, 16]])).then_inc(s, 16),
              writes=["es_sink"], semkey="es_sink")
        P.op("act", lambda e: e.activation(out=condb[:], in_=cb[:], func=AF.Silu), reads=["cb"], writes=["condb"])
        P.op("act", lambda e: e.activation(out=es_sink[:], in_=es_sink[:], func=AF.Exp), reads=["es_sink"], writes=["es_sink"])

        with contextlib.ExitStack() as es1:
            adw = [es1.enter_context(nc.sbuf_tensor(f"adw{i}", [128, 8 * 1024], BF16)) for i in range(2)]
            it = 0
            for l in range(nlayers):
                for piece in range(6):
                    slot = it % 2
                    it += 1
                    buf = adw[slot]
                    src = ada_w[l, :, piece * 1024:(piece + 1) * 1024].rearrange("(kc p) n -> p kc n", p=128)
                    dst = buf[:].rearrange("p (kc n) -> p kc n", kc=8)
                    P.dma("pool", (lambda e, s, dst=dst, src=src: e.dma_start(out=dst, in_=src).then_inc(s, 16)),
                          writes=[("adw", slot)], semkey=("adw", slot))
                    bank = (l * 6 + piece) % 2

                    def mm_mod(e, buf=buf, piece=piece, bank=bank):
                        ins = None
                        for fc in range(8):
                            for kc in range(8):
                                ins = e.matmul(psb[bank][:, fc:fc + 1],
                                               buf[:, kc * 1024 + fc * 128: kc * 1024 + (fc + 1) * 128],
                                               condb[:, kc:kc + 1], start=(kc == 0), stop=(kc == 7))
                        return ins
                    P.op("pe", mm_mod, reads=[("adw", slot), "condb"], writes=[PSR(bank)])
                    col = l * 48 + piece * 8
                    P.op("dve", (lambda e, bank=bank, col=col: e.tensor_tensor(
                        out=modsb[:, col:col + 8], in0=psb[bank][:, 0:8], in1=adab[:, col:col + 8], op=ALU.add)),
                        reads=[PSR(bank), "adab"], writes=["modsb"])
            for l in range(nlayers):
                P.op("dve", (lambda e, l=l: e.scalar_tensor_tensor(
                    out=a1[:, l * 8:(l + 1) * 8], in0=modsb[:, l * 48 + 8: l * 48 + 16], scalar=1.0,
                    in1=gmix_sb[:, l * 8:(l + 1) * 8], op0=ALU.add, op1=ALU.mult)),
                    reads=["modsb", "gmix"], writes=["a1"])
                P.op("dve", (lambda e, l=l: e.scalar_tensor_tensor(
                    out=a2[:, l * 8:(l + 1) * 8], in0=modsb[:, l * 48 + 32: l * 48 + 40], scalar=1.0,
                    in1=gffn_sb[:, l * 8:(l + 1) * 8], op0=ALU.add, op1=ALU.mult)),
                    reads=["modsb", "gffn"], writes=["a2"])
            P.barrier()

        kinds_used = set(MIX[:nlayers])
        with contextlib.ExitStack() as es2:
            tab33 = es2.enter_context(nc.sbuf_tensor("tab33", [33, 16], F32))
            oh = es2.enter_context(nc.sbuf_tensor("oh", [33, 4 * 512], F32))
            vsb = es2.enter_context(nc.sbuf_tensor("vsb", [16, 4 * 512], F32))
            rpb_sb = es2.enter_context(nc.sbuf_tensor("rpb_sb", [16, 465], F32))
            zB = es2.enter_context(nc.sbuf_tensor("zB", [16, 1024], F32))
            P.op("dve", lambda e: e.memset(tab33[32:33, :], NEG), writes=["tab33b"])
            simple_load("sp", tab33[0:32, :], relb, "tab33a")
            for g in range(3):
                simple_load("sp", oh[:, g * 512: g * 512 + 384], ohA_d[g], ("oh", g))
            simple_load("sp", oh[:, 3 * 512: 4 * 512], ohC_d, ("oh", 3))
            for g in range(4):
                W = 384 if g < 3 else 512
                bank = g % 2
                P.op("pe", (lambda e, g=g, W=W, bank=bank: e.matmul(
                    psb[bank][0:16, 0:W], tab33[0:33, 0:16], oh[0:33, g * 512: g * 512 + W], start=True, stop=True)),
                    reads=["tab33a", "tab33b", ("oh", g)], writes=[PSR(bank)])
                P.op("act", (lambda e, g=g, W=W, bank=bank: e.activation(
                    out=vsb[:, g * 512: g * 512 + W], in_=psb[bank][0:16, 0:W], func=AF.Exp)),
                    reads=[PSR(bank)], writes=[("vsb", g)])
                dstv = vecA[g] if g < 3 else vecC
                P.dma("sp", (lambda e, s, g=g, W=W, dstv=dstv: e.dma_start(out=dstv, in_=vsb[:, g * 512: g * 512 + W]).then_inc(s, 16)),
                      reads=[("vsb", g)], writes=[("vec", g)], semkey=("vec", g))
                if g < 3:
                    srcb = bass.AP(vecA.tensor, g * 16 * 384, [[384, 16], [0, 128], [1, 384]])
                    dstb = repA[g]
                else:
                    srcb = bass.AP(vecC.tensor, 0, [[512, 16], [0, 128], [1, 512]])
                    dstb = repC
                P.dma("sp", (lambda e, s, srcb=srcb, dstb=dstb: e.dma_start(out=dstb, in_=srcb).then_inc(s, 16)),
                      reads=[("vec", g)], writes=[("rep", g)], semkey=("rep", g))
            simple_load("sp", rpb_sb[:], rpbff, "rpb_sb")
            P.op("dve", lambda e: e.memset(zB[:], 0.0), writes=["zB"])
            zview = bass.AP(zB[:].tensor, zB[:].offset + 48, [[zB[:].ap[0][0], 16], [64, 15], [1, 31]])
            P.op("act", lambda e: e.activation(out=zview, in_=rpb_sb[:].rearrange("p (a j) -> p a j", a=15), func=AF.Exp),
                 reads=["rpb_sb", "zB"], writes=["zB"])
            P.dma("sp", lambda e, s: e.dma_start(out=vecB, in_=zB[:]).then_inc(s, 16), reads=["zB"], writes=["vecB"], semkey="vecB")
            srcb = bass.AP(vecB.tensor, 0, [[1024, 16], [0, 128], [1, 1024]])
            P.dma("sp", lambda e, s: e.dma_start(out=repB, in_=srcb).then_inc(s, 16), reads=["vecB"], writes=["repB"], semkey="repB")
            P.barrier()

        def norm_tile(xt, W, sq, bank, rs, tmp, a_t, b_t, col0, out_fn, tag, nslots=8, inplace=False):
            P.op("act", lambda e: e.activation(out=sq[:, 0:8 * W], in_=xt[:, 0:8 * W], func=AF.Square),
                 reads=[("xt", tag)], writes=["nsq"])

            def mm_ss(e):
                ins = None
                for c in range(8):
                    ins = e.matmul(psb[bank][:, 0:W], ones_bf[:], sq[:, c * W:(c + 1) * W], start=(c == 0), stop=(c == 7))
                return ins
            P.op("pe", mm_ss, reads=["nsq", "ones"], writes=[PSR(bank)])
            P.op("dve", lambda e: e.tensor_scalar(out=rs[:, 0:W], in0=psb[bank][:, 0:W], scalar1=1.0 / D, scalar2=EPS,
                                                  op0=ALU.mult, op1=ALU.add),
                 reads=[PSR(bank)], writes=["nrs"])
            P.op("act", lambda e: e.activation(out=rs[:, 0:W], in_=rs[:, 0:W], func=AF.Sqrt),
                 reads=["nrs"], writes=["nrs"])
            P.op("dve", lambda e: e.reciprocal(out=rs[:, 0:W], in_=rs[:, 0:W]),
                 reads=["nrs"], writes=["nrs"])
            for c in range(8):
                if inplace:
                    P.op("dve", (lambda e, c=c: e.scalar_tensor_tensor(
                        out=xt[:, c * W:(c + 1) * W], in0=xt[:, c * W:(c + 1) * W], scalar=a_t[:, col0 + c: col0 + c + 1],
                        in1=rs[:, 0:W], op0=ALU.mult, op1=ALU.mult)),
                        reads=[("xt", tag), "nrs", "a1", "a2", "gfin"], writes=[("xt", tag)])
                    continue
                ts_ = c % nslots
                P.op("dve", (lambda e, c=c, ts_=ts_: e.scalar_tensor_tensor(
                    out=tmp[:, ts_ * W:(ts_ + 1) * W], in0=xt[:, c * W:(c + 1) * W], scalar=a_t[:, col0 + c: col0 + c + 1],
                    in1=rs[:, 0:W], op0=ALU.mult, op1=ALU.mult)),
                    reads=[("xt", tag), "nrs", "a1", "a2", "gfin"], writes=[("ntmp", ts_)])
                o_ap, o_res = out_fn(c)
                bt, bcol = b_t
                P.op("act", (lambda e, c=c, ts_=ts_, o_ap=o_ap, bt=bt, bcol=bcol: e.activation(
                    out=o_ap, in_=tmp[:, ts_ * W:(ts_ + 1) * W], func=AF.Identity, bias=bt[:, bcol + c: bcol + c + 1])),
                    reads=[("ntmp", ts_), "modsb"], writes=[o_res])

        def xview(ap, t0, W):
            return ap[:, t0:t0 + W].rearrange("(c p) t -> p c t", p=128)

        out_dma_ops = []
        def layer(l):
            kind = MIX[l]
            mj = MIXJ[l]
            x_src = xT_in if l == 0 else xs
            last_layer = (l == nlayers - 1)
            x_dst = yT if last_layer else xs
            P.new_epoch()
            with contextlib.ExitStack() as esL:
                hT = esL.enter_context(nc.sbuf_tensor(f"hT{l}", [128, 8 * S], BF16))

                with contextlib.ExitStack() as esn:
                    xts = [esn.enter_context(nc.sbuf_tensor(f"xt{l}_{i}", [128, 8 * 512], F32)) for i in range(2)]
                    sq = esn.enter_context(nc.sbuf_tensor(f"sq{l}", [128, 8 * 512], BF16))
                    rs = esn.enter_context(nc.sbuf_tensor(f"rs{l}", [128, 512], F32))
                    tmp = esn.enter_context(nc.sbuf_tensor(f"ntmp{l}", [128, 8 * 512], F32))
                    for tt in range(8):
                        slot = tt % 2
                        xt = xts[slot]
                        P.dma("sp", (lambda e, s, xt=xt, tt=tt: e.dma_start(
                            out=xt[:].rearrange("p (c t) -> p c t", c=8), in_=xview(x_src, tt * 512, 512)).then_inc(s, 16)),
                            reads=["xdram"], writes=[("xt", slot)], semkey=("xt", slot))

                        def out_fn(c, tt=tt):
                            return hT[:, c * S + tt * 512: c * S + (tt + 1) * 512], ("hT", tt)
                        norm_tile(xt, 512, sq, tt % 2, rs, tmp, a1, (modsb, l * 48 + 0), l * 8, out_fn, slot)
                    P.barrier()

                with contextlib.ExitStack() as esa:
                    def asb(name, shape, dt):
                        return esa.enter_context(nc.sbuf_tensor(f"{name}{l}", list(shape), dt))
                    dbl_v = (kind != "B")
                    w3 = [asb(f"w3_{i}_", [128, 8 * 384], BF16) for i in range(2)]
                    qTs = [asb(f"qT{i}_", [128, S], BF16) for i in range(2)]
                    kTs = [asb(f"kT{i}_", [128, S], BF16) for i in range(2)]
                    NVT = 32
                    Vps = [asb(f"Vp{i}_", [128, NVT * 256], BF16) for i in range(2 if dbl_v else 1)]
                    acc = asb("acc", [128, 2 * S], F32)
                    rtmps = [asb(f"rtmp{i}_", [128, 1024], F32) for i in range(2)]
                    NE, NPT = (3, 8) if kind != "B" else (2, 4)
                    PW = 640 if kind == "B" else 384
                    Eb = [asb(f"E{i}_", [128, PW], BF16) for i in range(NE)]
                    PTb = [asb(f"PT{i}_", [128, PW], BF16) for i in range(NPT)]
                    OTp = asb("OTp", [128, S], BF16)
                    if kind == "A":
                        EBW = 256
                    elif kind == "C":
                        EBW = 384
                    if kind in ("A", "C"):
                        EB = [asb(f"EB{i}_", [128, 2 * EBW], BF16) for i in range(3)]
                    else:
                        Tall = asb("Tall", [128, 2 * 896], F32)
                        EBh = asb("EBh", [128, 2 * B_EBW], BF16)
                        validB = asb("validB", [128, B_EBW], BF16)
                        simple_load("pool", validB[:], validB_d, "validB")
                    Vp4s = [v[:].rearrange("p (t h c) -> p t h c", h=2, c=128) for v in Vps]
                    for vi, v in enumerate(Vps):
                        P.op("pool", (lambda e, v=v: e.memset(v[:], 1.0)), writes=[("Vp", vi)])
                    es_t = es_sink if kind == "C" else es_zero
                    if kind == "A":
                        w_in_d = a_w_in[mj]
                        groups = A_GROUPS
                    elif kind == "B":
                        w_in_d = b_w_in[0]
                        groups = [(1, S)]
                    else:
                        w_in_d = c_w_in[0]
                        groups = [(1, S)]
                    NG = len(groups)
                    NU = 8 * NG
                    ecnt = [0]
                    ptcnt = [0]
                    stcnt = [0]
                    otcnt = [0]
                    pjcnt = [0]

                    def wsrc(col0, n):
                        return w_in_d[:, col0:col0 + n].rearrange("(kc p) n -> p kc n", p=128)

                    def emit_loads(n):
                        hp_, gi_ = n // NG, n % NG
                        wslot_ = n % 2
                        wv3 = w3[wslot_][:].rearrange("p (kc n) -> p kc n", kc=8)
                        if kind == "A":
                            specs = [(0, 128, gi_ * 3072 + hp_ * 128), (128, 128, gi_ * 3072 + 1024 + hp_ * 128),
                                     (256, 128, gi_ * 3072 + 2048 + hp_ * 128)]
                        elif kind == "B":
                            specs = [(0, 128, hp_ * 128), (128, 128, 1024 + hp_ * 128), (256, 128, 2048 + hp_ * 128)]
                        else:
                            kv = hp_ // 2
                            specs = [(0, 128, hp_ * 128), (128, 64, 1024 + kv * 64), (192, 64, 1024 + kv * 64),
                                     (256, 64, 1280 + kv * 64), (320, 64, 1280 + kv * 64)]

                        def wload(e, s, specs=specs, wv3=wv3):
                            for (o, n_, c0) in specs:
                                e.dma_start(out=wv3[:, :, o:o + n_], in_=wsrc(c0, n_)).then_inc(s, 16)
                        P.dma("pool", wload, writes=[("w3", wslot_)], semkey=("w3", wslot_), n=len(specs))
                        if kind in ("A", "C"):
                            es3 = n % 3
                            ebt_ = EB[es3]

                            def ebload(e, s, ebt_=ebt_, gi_=gi_, hp_=hp_):
                                for h2 in range(2):
                                    h = hp_ * 2 + h2
                                    if kind == "A":
                                        src = bass.AP(repA.tensor, ((gi_ * 16 + h) * 128) * 384 + 127, [[383, 128], [1, 256]])
                                    else:
                                        src = bass.AP(repC.tensor, (h * 128) * 512 + 127, [[511, 128], [1, 384]])
                                    e.dma_start(out=ebt_[:, h2 * EBW:(h2 + 1) * EBW], in_=src).then_inc(s, 16)
                            P.dma("pool", ebload, reads=[("rep", gi_ if kind == "A" else 3)], writes=[("EB", es3)],
                                  semkey=("EB", es3), n=2)

                    def proj_gen(n, part):
                        hp_, gi_ = n // NG, n % NG
                        dil, L = groups[gi_]
                        slot = n % 2
                        vslot = slot if dbl_v else 0
                        wb = w3[slot]
                        if part == "qk":
                            for which, dstT in ((0, qTs[slot]), (1, kTs[slot])):
                                for tt in range(8):
                                    bank = pjcnt[0] % 2
                                    pjcnt[0] += 1

                                    def mm_p(e, which=which, tt=tt, bank=bank, wb=wb):
                                        ins = None
                                        for kc in range(8):
                                            ins = e.matmul(psb[bank][:, 0:512],
                                                           wb[:, kc * 384 + which * 128: kc * 384 + (which + 1) * 128],
                                                           hT[:, kc * S + tt * 512: kc * S + (tt + 1) * 512],
                                                           start=(kc == 0), stop=(kc == 7))
                                        return ins
                                    P.op("pe", mm_p, reads=[("w3", slot), ("hT", tt)], writes=[PSR(bank)])
                                    if which == 1:
                                        P.op("act", (lambda e, dstT=dstT, tt=tt, bank=bank: e.activation(
                                            out=dstT[:, tt * 512:(tt + 1) * 512], in_=psb[bank][:, 0:512], func=AF.Copy)),
                                            reads=[PSR(bank)], writes=[("qk", which, slot)])
                                    else:
                                        P.op("dve", (lambda e, dstT=dstT, tt=tt, bank=bank: e.tensor_copy(
                                            out=dstT[:, tt * 512:(tt + 1) * 512], in_=psb[bank][:, 0:512])),
                                            reads=[PSR(bank)], writes=[("qk", which, slot)])
                                    yield
                        else:
                            nT = L // 128
                            for r in range(dil):
                                for u0 in range(0, nT, 4):
                                    nbk = min(4, nT - u0)
                                    bank = pjcnt[0] % 2
                                    pjcnt[0] += 1
                                    t_idx = r * nT + u0

                                    def mm_v(e, r=r, u0=u0, nbk=nbk, bank=bank, wb=wb, dil=dil):
                                        ins = None
                                        for i in range(nbk):
                                            u = u0 + i
                                            for kc in range(8):
                                                ins = e.matmul(psb[bank][:, i * 128:(i + 1) * 128],
                                                               hT[:, sl(kc * S + r + dil * 128 * u, 128, dil)],
                                                               wb[:, kc * 384 + 256: kc * 384 + 384],
                                                               start=(kc == 0), stop=(kc == 7))
                                        return ins
                                    P.op("pe", mm_v, reads=[("w3", slot)] + [("hT", t) for t in range(8)], writes=[PSR(bank)])
                                    vbase = Vps[vslot][:]
                                    o_ap = bass.AP(vbase.tensor, vbase.offset + t_idx * 256,
                                                   [[vbase.ap[0][0], 128], [256, nbk], [192, 2], [1, 64]])
                                    pb = psb[bank][:]
                                    i_ap = bass.AP(pb.tensor, pb.offset, [[pb.ap[0][0], 128], [128, nbk], [64, 2], [1, 64]])
                                    P.op("dve", (lambda e, o_ap=o_ap, i_ap=i_ap: e.tensor_copy(out=o_ap, in_=i_ap)),
                                         reads=[PSR(bank)], writes=[("Vp", vslot)])
                                    yield

                    def run_all(g):
                        for _ in g:
                            pass

                    class Ticker:
                        def __init__(self, gens, total_steps, n_items):
                            self.gens = list(gens)
                            self.rate = n_items / max(1, total_steps)
                            self.acc = 0.0

                        def tick(self):
                            self.acc += self.rate
                            while self.acc >= 1.0 and self.gens:
                                self.acc -= 1.0
                                try:
                                    next(self.gens[0])
                                except StopIteration:
                                    self.gens.pop(0)

                        def flush(self):
                            for g in self.gens:
                                run_all(g)
                            self.gens = []

                    def attention(n, tk):
                        hp, gi = n // NG, n % NG
                        dil, L = groups[gi]
                        slot = n % 2
                        vslot = slot if dbl_v else 0
                        qT, kT, Vp4 = qTs[slot], kTs[slot], Vp4s[vslot]
                        nT = L // 128
                        first_group = (gi == 0)
                        qkres = [("qk", 0, slot), ("qk", 1, slot)]
                        if kind in ("A", "C"):
                            ebslot = n % 3
                            ebt = EB[ebslot]
                        for h2 in range(2):
                            prow = 64 * h2
                            if kind in ("A", "C"):
                                Rr = 64 if kind == "A" else 128
                                blocks = []
                                if kind == "A":
                                    for j in range(nT + 1):
                                        lo, hi = max(0, 128 * j - 64), min(L, 128 * j + 64)
                                        tiles = [u for u in (j - 1, j) if 0 <= u < nT]
                                        blocks.append((lo, hi, tiles, min(j, nT - 1)))
                                else:
                                    for j in range(nT):
                                        tiles = [u for u in (j - 1, j, j + 1) if 0 <= u < nT]
                                        blocks.append((128 * j, 128 * j + 128, tiles, min(j + 1, nT - 1)))
                                ogroups = []
                                cur = []
                                curw = 0
                                for b in blocks:
                                    w = b[1] - b[0]
                                    if curw + w > 512:
                                        ogroups.append(cur)
                                        cur, curw = [], 0
                                    cur.append(b)
                                    curw += w
                                if cur:
                                    ogroups.append(cur)
                                blk_group = {}
                                for gidx, gl in enumerate(ogroups):
                                    for b in gl:
                                        blk_group[b[0]] = gidx
                                LAG = 3
                                ptslot_of = {}
                                qlo_of = {}
                                pending = []
                                grp_state = {}
                                gcount = [0]

                                def emit_block(rb, b):
                                    lo, hi, tiles, ready = b
                                    gidx = blk_group[lo]
                                    gl = ogroups[gidx]
                                    key = (rb, gidx)
                                    if key not in grp_state:
                                        grp_state[key] = [6 + otcnt[0] % 2, 0]
                                        otcnt[0] += 1
                                    obank = grp_state[key][0]
                                    base = gl[0][0]
                                    c0 = lo - base
                                    pts = {uu: ptslot_of[(rb, uu)] for uu in tiles}
                                    qls = {uu: qlo_of[(rb, uu)] for uu in tiles}

                                    def mm_pv(e, lo=lo, hi=hi, tiles=tiles, obank=obank, c0=c0, rb=rb, h2=h2,
                                              pts=pts, qls=qls, nT=nT, Vp4=Vp4):
                                        ins = None
                                        for i, uu in enumerate(tiles):
                                            pc = lo - qls[uu]
                                            ins = e.matmul(psb[obank][:, c0:c0 + (hi - lo)],
                                                           Vp4[:, rb * nT + uu, h2, :],
                                                           PTb[pts[uu]][:, pc:pc + (hi - lo)],
                                                           start=(i == 0), stop=(i == len(tiles) - 1))
                                        return ins
                                    P.op("pe", mm_pv, reads=[("PT", pts[uu]) for uu in tiles] + [("Vp", vslot)],
                                         writes=[PSR(obank)])
                                    grp_state[key][1] += 1
                                    if grp_state[key][1] == len(gl):
                                        wtot = gl[-1][1] - base
                                        a_ap = acc[:, sl(h2 * S + rb + dil * base, wtot, dil)]
                                        if first_group:
                                            P.op("dve", (lambda e, a_ap=a_ap, obank=obank, wtot=wtot: e.tensor_copy(
                                                out=a_ap, in_=psb[obank][:, 0:wtot])),
                                                reads=[PSR(obank), ("accbar", h2)], writes=[])
                                        else:
                                            P.op("dve", (lambda e, a_ap=a_ap, obank=obank, wtot=wtot: e.tensor_tensor(
                                                out=a_ap, in0=psb[obank][:, 0:wtot], in1=a_ap, op=ALU.add)),
                                                reads=[PSR(obank), ("accbar", h2)], writes=[])

                                for r in range(dil):
                                    for u in range(nT):
                                        g = gcount[0]
                                        gcount[0] += 1
                                        qlo = max(0, 128 * u - Rr)
                                        qhi = min(L, 128 * u + 128 + Rr)
                                        W = qhi - qlo
                                        ebc = qlo - (128 * u - Rr)
                                        sbank = 2 + stcnt[0] % 4
                                        stcnt[0] += 1
                                        eslot = ecnt[0] % NE
                                        ecnt[0] += 1
                                        pslot = ptcnt[0] % NPT
                                        ptcnt[0] += 1
                                        ptslot_of[(r, u)] = pslot
                                        qlo_of[(r, u)] = qlo
                                        P.op("pe", (lambda e, u=u, qlo=qlo, W=W, sbank=sbank, r=r, prow=prow, dil=dil, kT=kT, qT=qT: e.matmul(
                                            psb[sbank][:, 0:W],
                                            kT[prow:prow + 64, sl(r + dil * 128 * u, 128, dil)],
                                            qT[prow:prow + 64, sl(r + dil * qlo, W, dil)], start=True, stop=True)),
                                            reads=qkres, writes=[PSR(sbank)])
                                        P.op("act", (lambda e, W=W, sbank=sbank, eslot=eslot: e.activation(
                                            out=Eb[eslot][:, 0:W], in_=psb[sbank][:, 0:W], func=AF.Exp, scale=0.125)),
                                            reads=[PSR(sbank)], writes=[("E", eslot)])
                                        P.op("dve", (lambda e, W=W, eslot=eslot, pslot=pslot, ebc=ebc, h2=h2, ebt=ebt: e.tensor_tensor(
                                            out=PTb[pslot][:, 0:W], in0=Eb[eslot][:, 0:W],
                                            in1=ebt[:, h2 * EBW + ebc: h2 * EBW + ebc + W], op=ALU.mult)),
                                            reads=[("E", eslot), ("EB", ebslot)], writes=[("PT", pslot)])
                                        for b in blocks:
                                            if b[3] == u:
                                                pending.append((g, r, b))
                                        while pending and pending[0][0] <= g - LAG:
                                            _g, rb, b = pending.pop(0)
                                            emit_block(rb, b)
                                        tk.tick()
                                while pending:
                                    _g, rb, b = pending.pop(0)
                                    emit_block(rb, b)
                            else:
                                LAG = 2
                                pend = []
                                obank_cur = None
                                for step in range(32 + LAG):
                                    j = step
                                    if j < 32:
                                        U = _b_tiles(j)
                                        nU_ = len(U)
                                        off, _nU, _x0 = B_OFF[_b_class(j)]
                                        sb0 = 2 + 2 * (stcnt[0] % 2)
                                        stcnt[0] += 1
                                        eslot = ecnt[0] % NE
                                        ecnt[0] += 1
                                        pslot = ptcnt[0] % NPT
                                        ptcnt[0] += 1

                                        def mm_sb(e, j=j, U=U, sb0=sb0, prow=prow, kT=kT, qT=qT):
                                            ins = None
                                            for i, uu in enumerate(U):
                                                bk = sb0 + (i // 4)
                                                ins = e.matmul(psb[bk][:, (i % 4) * 128:(i % 4 + 1) * 128],
                                                               kT[prow:prow + 64, uu * 128:(uu + 1) * 128],
                                                               qT[prow:prow + 64, j * 128:(j + 1) * 128], start=True, stop=True)
                                            return ins
                                        P.op("pe", mm_sb, reads=qkres, writes=[PSR(sb0), PSR(sb0 + 1)])

                                        def ex_b(e, nU_=nU_, sb0=sb0, eslot=eslot):
                                            ins = e.activation(out=Eb[eslot][:, 0:min(nU_, 4) * 128], in_=psb[sb0][:, 0:min(nU_, 4) * 128],
                                                               func=AF.Exp, scale=0.125)
                                            if nU_ > 4:
                                                ins = e.activation(out=Eb[eslot][:, 512:640], in_=psb[sb0 + 1][:, 0:128],
                                                                   func=AF.Exp, scale=0.125)
                                            return ins
                                        P.op("act", ex_b, reads=[PSR(sb0), PSR(sb0 + 1)], writes=[("E", eslot)])
                                        P.op("dve", (lambda e, nU_=nU_, eslot=eslot, pslot=pslot, off=off, h2=h2: e.tensor_tensor(
                                            out=PTb[pslot][:, 0:nU_ * 128], in0=Eb[eslot][:, 0:nU_ * 128],
                                            in1=EBh[:, h2 * B_EBW + off: h2 * B_EBW + off + nU_ * 128], op=ALU.mult)),
                                            reads=[("E", eslot), ("EBh", h2)], writes=[("PT", pslot)])
                                        pend.append((j, U, pslot))
                                    v = step - LAG
                                    if v >= 0:
                                        jv, Uv, psl = pend[v]
                                        if jv % 4 == 0:
                                            obank_cur = 6 + otcnt[0] % 2
                                            otcnt[0] += 1
                                        obank = obank_cur
                                        c0 = (jv % 4) * 128

                                        def mm_pvb(e, Uv=Uv, psl=psl, obank=obank, c0=c0, h2=h2, Vp4=Vp4):
                                            ins = None
                                            for i, uu in enumerate(Uv):
                                                ins = e.matmul(psb[obank][:, c0:c0 + 128], Vp4[:, uu, h2, :],
                                                               PTb[psl][:, i * 128:(i + 1) * 128],
                                                               start=(i == 0), stop=(i == len(Uv) - 1))
                                            return ins
                                        P.op("pe", mm_pvb, reads=[("PT", psl), ("Vp", vslot)], writes=[PSR(obank)])
                                        if jv % 4 == 3:
                                            a_ap = acc[:, h2 * S + (jv - 3) * 128: h2 * S + (jv + 1) * 128]
                                            P.op("dve", (lambda e, a_ap=a_ap, obank=obank: e.tensor_copy(
                                                out=a_ap, in_=psb[obank][:, 0:512])),
                                                reads=[PSR(obank), ("accbar", h2)], writes=[])
                                    tk.tick()
                        for h2 in range(2):
                            P.op("dve", (lambda e: e.memset(dummy[:, 0:1], 0.0)), reads=[], writes=[("accbar", h2)])

                    def normalize(hp):
                        for ck in range(4):
                            t0 = ck * 1024
                            h0 = hp * 2
                            rt = rtmps[ck % 2]
                            rr = ck % 2
                            P.op("dve", (lambda e, t0=t0, h0=h0, rt=rt: e.tensor_scalar(
                                out=rt[0:64, :], in0=acc[64:128, t0:t0 + 1024], scalar1=es_t[64:128, h0:h0 + 1], scalar2=None,
                                op0=ALU.add)),
                                reads=[("accbar", 0), "es_sink", "es_zero"], writes=[("rtmp", rr)])
                            P.op("dve", (lambda e, rt=rt: e.reciprocal(out=rt[0:64, :], in_=rt[0:64, :])),
                                 reads=[("rtmp", rr)], writes=[("rtmp", rr)])
                            P.op("dve", (lambda e, t0=t0, h0=h0, rt=rt: e.tensor_scalar(
                                out=rt[64:128, :], in0=acc[0:64, S + t0:S + t0 + 1024], scalar1=es_t[0:64, h0 + 1:h0 + 2], scalar2=None,
                                op0=ALU.add)),
                                reads=[("accbar", 1), "es_sink", "es_zero"], writes=[("rtmp2", rr)])
                            P.op("dve", (lambda e, rt=rt: e.reciprocal(out=rt[64:128, :], in_=rt[64:128, :])),
                                 reads=[("rtmp2", rr)], writes=[("rtmp2", rr)])
                            P.op("pool", (lambda e, t0=t0, rt=rt: e.tensor_tensor(
                                out=OTp[0:64, t0:t0 + 1024], in0=acc[0:64, t0:t0 + 1024], in1=rt[0:64, :], op=ALU.mult)),
                                reads=[("rtmp", rr), ("accbar", 0)], writes=["OTp"])
                            P.op("pool", (lambda e, t0=t0, rt=rt: e.tensor_tensor(
                                out=OTp[64:128, t0:t0 + 1024], in0=acc[64:128, S + t0:S + t0 + 1024], in1=rt[64:128, :], op=ALU.mult)),
                                reads=[("rtmp2", rr), ("accbar", 1)], writes=["OTp"])
                        P.dma("sp", (lambda e, s, hp=hp: e.dma_start(out=OTd[hp * 128:(hp + 1) * 128, :], in_=OTp[:]).then_inc(s, 16)),
                              reads=["OTp"], writes=["OTd"], semkey="OTp")
                        for h2 in range(2):
                            P.op("dve", (lambda e: e.memset(dummy[:, 0:1], 0.0)), reads=[], writes=[("accbar", h2)])

                    def b_prep(n):
                        hp = n // NG

                        def tload(e, s, hp=hp):
                            for h2 in range(2):
                                h = hp * 2 + h2
                                src = bass.AP(repB.tensor, (h * 128) * 1024 + 127, [[1023, 128], [1, 896]])
                                e.dma_start(out=Tall[:, h2 * 896:(h2 + 1) * 896], in_=src).then_inc(s, 16)
                        P.dma("sp", tload, reads=["repB"], writes=["Tall"], semkey="Tall", n=2)
                        for h2 in range(2):
                            for name, _j, U, x0 in B_CLASSES:
                                off, nU, _ = B_OFF[name]
                                P.op("pool", (lambda e, h2=h2, off=off, nU=nU, x0=x0: e.tensor_tensor(
                                    out=EBh[:, h2 * B_EBW + off: h2 * B_EBW + off + 128 * nU],
                                    in0=Tall[:, h2 * 896 + x0: h2 * 896 + x0 + 128 * nU],
                                    in1=validB[:, off: off + 128 * nU], op=ALU.mult)),
                                    reads=["Tall", "validB"], writes=[("EBh", h2)])

                    emit_loads(0)
                    if NU > 1:
                        emit_loads(1)
                    run_all(proj_gen(0, "qk"))
                    run_all(proj_gen(0, "v"))
                    for n in range(NU):
                        gi = n % NG
                        dil, L = groups[gi]
                        if n + 2 < NU:
                            emit_loads(n + 2)
                        if kind == "B":
                            b_prep(n)
                        gens = []
                        n_items = 0
                        if n + 1 < NU:
                            gens.append(proj_gen(n + 1, "qk"))
                            n_items += 16
                            if dbl_v:
                                gens.append(proj_gen(n + 1, "v"))
                                n_items += 8
                        if kind == "B":
                            total_steps = 2 * 33
                        else:
                            total_steps = 2 * dil * (L // 128)
                        tk = Ticker(gens, int(total_steps * 0.85), n_items)
                        attention(n, tk)
                        tk.flush()
                        if n + 1 < NU and not dbl_v:
                            run_all(proj_gen(n + 1, "v"))
                        if gi == NG - 1:
                            normalize(n // NG)
                    P.barrier()
            P.barrier()

            with contextlib.ExitStack() as esf:
                def fsb(name, shape, dt):
                    return esf.enter_context(nc.sbuf_tensor(f"{name}{l}", list(shape), dt))
                TW = 256
                wo = fsb("wo", [128, 8 * 1024], BF16)
                win = fsb("win", [128, 8 * 2 * DFF], BF16)
                wout = fsb("wout", [128, NFC * 1024], BF16)
                xts = [fsb(f"fx{i}_", [128, 8 * TW], F32) for i in range(2)]
                ots = [fsb(f"fo{i}_", [128, 8 * TW], BF16) for i in range(2)]
                sq = fsb("fsq", [128, 8 * TW], BF16)
                rs = fsb("frs", [128, TW], F32)
                tmp = fsb("ftmp", [128, 3 * TW], F32)
                h2Ts = [fsb(f"fh2_{i}_", [128, 8 * TW], BF16) for i in range(2)]
                aT = fsb("faT", [128, NFC * TW], BF16)
                sg = [fsb(f"fsg{i}_", [128, TW], BF16) for i in range(2)]
                wo_d = {"A": a_w_out, "B": b_w_out, "C": c_w_out}[kind][mj]
                wo3 = wo[:].rearrange("p (kc n) -> p kc n", kc=8)
                for pc in range(2):
                    P.dma("pool", (lambda e, s, pc=pc: e.dma_start(
                        out=wo3[:, :, pc * 512:(pc + 1) * 512],
                        in_=wo_d[:, pc * 512:(pc + 1) * 512].rearrange("(kc p) n -> p kc n", p=128)).then_inc(s, 16)),
                        writes=[("wo", pc)], semkey=("wo", pc))
                win3 = win[:].rearrange("p (kc n) -> p kc n", kc=8)
                for pc in range(11):
                    P.dma("pool", (lambda e, s, pc=pc: e.dma_start(
                        out=win3[:, :, pc * 512:(pc + 1) * 512],
                        in_=ffn_w_in[l, :, pc * 512:(pc + 1) * 512].rearrange("(kc p) n -> p kc n", p=128)).then_inc(s, 16)),
                        writes=[("win", pc)], semkey=("win", pc))
                wout3 = wout[:].rearrange("p (fc n) -> p fc n", fc=NFC)
                for pc in range(11):
                    P.dma("pool", (lambda e, s, pc=pc: e.dma_start(
                        out=wout3[:, 2 * pc:2 * pc + 2, :],
                        in_=ffn_w_out[l, pc * 256:(pc + 1) * 256, :].rearrange("(fc p) n -> p fc n", p=128)).then_inc(s, 16)),
                        writes=[("wout", pc)], semkey=("wout", pc))
                mcol = l * 48
                NTT = S // TW
                bankc = [0]

                def nb():
                    b = bankc[0] % 8
                    bankc[0] += 1
                    return b
                def stageA(tt):
                    slot = tt % 2
                    xt = xts[slot]
                    ot = ots[slot]
                    t0 = tt * TW
                    P.dma("sp", (lambda e, s, xt=xt, t0=t0: e.dma_start(
                        out=xt[:].rearrange("p (c t) -> p c t", c=8), in_=xview(x_src, t0, TW)).then_inc(s, 16)),
                        reads=["xdram"], writes=[("xt", slot)], semkey=("fx", slot))
                    P.dma("sp", (lambda e, s, ot=ot, t0=t0: e.dma_start(
                        out=ot[:].rearrange("p (c t) -> p c t", c=8), in_=xview(OTd, t0, TW)).then_inc(s, 16)),
                        reads=["OTd"], writes=[("ot", slot)], semkey=("fo", slot))

                def stageB(tt):
                    slot = tt % 2
                    xt = xts[slot]
                    ot = ots[slot]
                    h2T = h2Ts[slot]
                    for m in range(8):
                        bank = nb()

                        def mm_o(e, m=m, bank=bank, ot=ot):
                            ins = None
                            for kc in range(8):
                                ins = e.matmul(psb[bank][:, 0:TW], wo[:, kc * 1024 + m * 128: kc * 1024 + (m + 1) * 128],
                                               ot[:, kc * TW:(kc + 1) * TW], start=(kc == 0), stop=(kc == 7))
                            return ins
                        P.op("pe", mm_o, reads=[("wo", m // 4), ("ot", slot)], writes=[PSR(bank)])
                        P.op("dve", (lambda e, m=m, bank=bank, xt=xt: e.scalar_tensor_tensor(
                            out=xt[:, m * TW:(m + 1) * TW], in0=psb[bank][:, 0:TW], scalar=modsb[:, mcol + 16 + m: mcol + 17 + m],
                            in1=xt[:, m * TW:(m + 1) * TW], op0=ALU.mult, op1=ALU.add)),
                            reads=[PSR(bank), ("xt", slot), "modsb"], writes=[("xt", slot)])

                    def out_fn2(c, h2T=h2T, slot=slot):
                        return h2T[:, c * TW:(c + 1) * TW], ("h2T", slot)
                    norm_tile(xt, TW, sq, nb(), rs, tmp, a2, (modsb, mcol + 24), l * 8, out_fn2, slot, nslots=3)

                def stageC(tt):
                    slot = tt % 2
                    h2T = h2Ts[slot]
                    for f in range(NFC):
                        bg = nb()
                        bu = nb()

                        def mm_gu(e, f=f, bg=bg, bu=bu, h2T=h2T):
                            ins = None
                            for kc in range(8):
                                ins = e.matmul(psb[bg][:, 0:TW], win[:, kc * 2 * DFF + f * 128: kc * 2 * DFF + (f + 1) * 128],
                                               h2T[:, kc * TW:(kc + 1) * TW], start=(kc == 0), stop=(kc == 7))
                            for kc in range(8):
                                ins = e.matmul(psb[bu][:, 0:TW], win[:, kc * 2 * DFF + DFF + f * 128: kc * 2 * DFF + DFF + (f + 1) * 128],
                                               h2T[:, kc * TW:(kc + 1) * TW], start=(kc == 0), stop=(kc == 7))
                            return ins
                        P.op("pe", mm_gu, reads=[("win", (f * 128) // 512), ("win", (DFF + f * 128) // 512), ("h2T", slot)],
                             writes=[PSR(bg), PSR(bu)])
                        sslot = f % 2
                        P.op("act", (lambda e, bg=bg, sslot=sslot: e.activation(out=sg[sslot][:], in_=psb[bg][:, 0:TW], func=AF.Silu)),
                             reads=[PSR(bg)], writes=[("sg", sslot)])
                        P.op("dve", (lambda e, f=f, bu=bu, sslot=sslot: e.tensor_tensor(
                            out=aT[:, f * TW:(f + 1) * TW], in0=psb[bu][:, 0:TW], in1=sg[sslot][:], op=ALU.mult)),
                            reads=[PSR(bu), ("sg", sslot)], writes=[("aT", f)])

                def stageD(tt):
                    slot = tt % 2
                    xt = xts[slot]
                    t0 = tt * TW
                    for m in range(8):
                        bank = nb()

                        def mm_f(e, m=m, bank=bank):
                            ins = None
                            for f in range(NFC):
                                ins = e.matmul(psb[bank][:, 0:TW], wout[:, f * 1024 + m * 128: f * 1024 + (m + 1) * 128],
                                               aT[:, f * TW:(f + 1) * TW], start=(f == 0), stop=(f == NFC - 1))
                            return ins
                        P.op("pe", mm_f, reads=[("wout", pc) for pc in range(11)] + [("aT", f) for f in range(NFC)],
                             writes=[PSR(bank)])
                        P.op("dve", (lambda e, m=m, bank=bank, xt=xt: e.scalar_tensor_tensor(
                            out=xt[:, m * TW:(m + 1) * TW], in0=psb[bank][:, 0:TW], scalar=modsb[:, mcol + 40 + m: mcol + 41 + m],
                            in1=xt[:, m * TW:(m + 1) * TW], op0=ALU.mult, op1=ALU.add)),
                            reads=[PSR(bank), ("xt", slot), "modsb"], writes=[("xt", slot)])
                    if last_layer:
                        norm_tile(xt, TW, sq, nb(), rs, tmp, gfin_sb, None, 0, None, slot, nslots=3, inplace=True)
                    o = P.dma("sp", (lambda e, s, xt=xt, t0=t0: e.dma_start(
                        out=xview(x_dst, t0, TW), in_=xt[:].rearrange("p (c t) -> p c t", c=8)).then_inc(s, 16)),
                        reads=[("xt", slot)], writes=["xdram_w"], semkey=("fxs", slot))
                    if last_layer:
                        out_dma_ops.append(o)

                stageA(0)
                stageA(1)
                stageB(0)
                for tt in range(NTT):
                    stageC(tt)
                    if tt + 1 < NTT:
                        stageB(tt + 1)
                    stageD(tt)
                    if tt + 2 < NTT:
                        stageA(tt + 2)
                P.barrier()
        for l in range(nlayers):
            layer(l)
        P.emit(final_waits=out_dma_ops[-2:])
    return nc


_PROGRAM_CACHE = {}


def _layout_inputs(inputs):
    f = np.float32
    x = np.asarray(inputs["x"], f)
    c = np.asarray(inputs["c"], f)
    ohA, ohC, validB = _static_tables()

    def l128(v):
        return np.ascontiguousarray(np.asarray(v, f).reshape(-1, 128).T)

    shared = {
        "rel_bias": np.ascontiguousarray(np.asarray(inputs["rel_bias"], f)),
        "ada_w": np.ascontiguousarray(np.asarray(inputs["ada_w"], f)),
        "ada_bl": np.ascontiguousarray(np.asarray(inputs["ada_b"], f).reshape(DEPTH, 48, 128).transpose(0, 2, 1)),
        "gmix": l128(np.asarray(inputs["norm_mix"], f)),
        "gffn": l128(np.asarray(inputs["norm_ffn"], f)),
        "gfin": l128(np.asarray(inputs["norm_final"], f)),
        "a_w_in": np.ascontiguousarray(np.asarray(inputs["a_w_in"], f)),
        "a_w_out": np.ascontiguousarray(np.asarray(inputs["a_w_out"], f)),
        "b_w_in": np.ascontiguousarray(np.asarray(inputs["b_w_in"], f)),
        "b_w_out": np.ascontiguousarray(np.asarray(inputs["b_w_out"], f)),
        "rpbff": np.ascontiguousarray(np.asarray(inputs["b_rpb"], f)[0][:, ::-1, ::-1].reshape(16, 465)),
        "c_w_in": np.ascontiguousarray(np.asarray(inputs["c_w_in"], f)),
        "c_w_out": np.ascontiguousarray(np.asarray(inputs["c_w_out"], f)),
        "c_sink": np.ascontiguousarray(np.asarray(inputs["c_sink"], f)),
        "ffn_w_in": np.ascontiguousarray(np.asarray(inputs["ffn_w_in"], f)),
        "ffn_w_out": np.ascontiguousarray(np.asarray(inputs["ffn_w_out"], f)),
        "ohA": ohA, "ohC": ohC, "validB": validB,
    }
    in_maps = []
    for b in range(8):
        m = dict(shared)
        m["xT"] = np.ascontiguousarray(x[b].T)
        m["cb"] = l128(c[b])
        in_maps.append(m)
    return in_maps


def kernel(**inputs):
    in_maps = _layout_inputs(inputs)
    nc = build_program(DEPTH)
    res = run_bass_kernel_spmd(nc, in_maps, core_ids=list(range(8)))
    out = np.stack([np.ascontiguousarray(np.asarray(r["yT"]).T) for r in res.results], axis=0)
    return out.astype(np.float32)
```
